# Optimizing a Trainium2 kernel written in Bass

```python
import jax, jax.numpy as jnp
from jax import lax
import numpy as np

D_MODEL = 1024
BATCH = 8
SEQ = 2048
DEPTH = 2

N_GROUPS = 4
GROUP_HEADS = 4
GROUP_WIDTH = D_MODEL // N_GROUPS
HEAD_DIM = GROUP_WIDTH // GROUP_HEADS
MIX_WIDTH = N_GROUPS * GROUP_WIDTH
CONV_WIDTH = 3
RWKV_DECAY_RANK = 64
RWKV_ICLR_RANK = 64
RWKV_GATE_RANK = 128
RWKV_GN_EPS = 64e-5
RET_CHUNK = 128
MLA_Q_RANK = 384
MLA_KV_RANK = 128
MLA_NOPE = 64
MLA_ROPE = 32
MLA_V = 64
ATTN_BLOCK = 128
D_FF = 2816
ROPE_BASE = 10000.0
NORM_EPS = 1e-6
MAX_POS_OFFSET = 4096
IN_SPLITS = (
    GROUP_WIDTH, GROUP_WIDTH, GROUP_WIDTH,
    GROUP_WIDTH, GROUP_WIDTH, GROUP_WIDTH,
    RWKV_DECAY_RANK, RWKV_DECAY_RANK,
    RWKV_ICLR_RANK, RWKV_ICLR_RANK,
    RWKV_GATE_RANK,
    GROUP_WIDTH, GROUP_WIDTH, GROUP_WIDTH, GROUP_WIDTH,
    MLA_Q_RANK, MLA_KV_RANK + MLA_ROPE,
)
IN_WIDTH = sum(IN_SPLITS)

kernel_name = 'hybrid_parallel_mixer_encoder'


def rmsnorm(x, g):
    xf = x.astype(jnp.float32)
    y = xf * lax.rsqrt(jnp.mean(xf * xf, axis=-1, keepdims=True) + NORM_EPS)
    return (y * g.astype(jnp.float32)).astype(x.dtype)


def swiglu(x, w_gate, w_up, w_down):
    return (jax.nn.silu(x @ w_gate) * (x @ w_up)) @ w_down


def rope(x, pos):
    half = x.shape[-1] // 2
    inv = ROPE_BASE ** (-jnp.arange(half, dtype=jnp.float32) / half)
    ang = pos.astype(jnp.float32)[:, :, None, None] * inv
    cos, sin = jnp.cos(ang), jnp.sin(ang)
    xf = x.astype(jnp.float32)
    x1, x2 = xf[..., :half], xf[..., half:]
    return jnp.concatenate([x1 * cos - x2 * sin, x1 * sin + x2 * cos], axis=-1).astype(x.dtype)


def heads(t):
    b, s, _ = t.shape
    return t.reshape(b, s, GROUP_HEADS, HEAD_DIM).astype(jnp.float32)


def short_conv_mixer(x_in, b_gate, c_gate, conv_w):
    u = c_gate * x_in
    y = lax.conv_general_dilated(
        u, conv_w[:, None, :].astype(u.dtype), window_strides=(1,), padding=[(1, 1)],
        dimension_numbers=('NWC', 'WIO', 'NWC'), feature_group_count=GROUP_WIDTH)
    return b_gate * y


def rwkv7_scan(r, k, v, kk, a, decay, reverse):
    xs = tuple(jnp.moveaxis(t, 1, 0) for t in (r, k, v, kk, a, decay))
    b, _, h, n = r.shape

    def step(S, inp):
        r_t, k_t, v_t, kk_t, a_t, w_t = inp
        sa = jnp.einsum('bhvk,bhk->bhv', S, -kk_t)
        S = (S * w_t[:, :, None, :] + sa[..., None] * (kk_t * a_t)[:, :, None, :]
             + v_t[..., None] * k_t[:, :, None, :])
        return S, jnp.einsum('bhvk,bhk->bhv', S, r_t)

    S0 = jnp.zeros((b, h, n, n), jnp.float32)
    _, y = lax.scan(step, S0, xs, reverse=reverse)
    return jnp.moveaxis(y, 0, 1)


def rwkv7_mixer(r, k, v, wd_f, wd_b, ad_f, ad_b, gd, w0_f, w0_b, w2_f, w2_b,
                a0_f, a0_b, a2_f, a2_b, g2, k_k, k_a, r_k, lnx_g, lnx_b):
    b, s, _ = r.shape
    g = jax.nn.sigmoid(gd) @ g2
    kk = heads(k * k_k)
    kk = kk / jnp.maximum(jnp.sqrt(jnp.sum(kk * kk, axis=-1, keepdims=True)), 1e-12)
    rh, vh = heads(r), heads(v)
    y = jnp.zeros_like(rh)
    k_sum = jnp.zeros_like(rh)
    for wd, ad, w0, w2, a0, a2, rev in ((wd_f, ad_f, w0_f, w2_f, a0_f, a2_f, False),
                                        (wd_b, ad_b, w0_b, w2_b, a0_b, a2_b, True)):
        w_log = -jax.nn.softplus(-(w0 + jnp.tanh(wd) @ w2)) - 0.5
        decay = jnp.exp(-jnp.exp(w_log.astype(jnp.float32)))
        a = jax.nn.sigmoid(a0 + ad @ a2)
        k_dir = heads(k * (1.0 + (a - 1.0) * k_a))
        y = y + rwkv7_scan(rh, k_dir, vh, kk, heads(a), heads(decay), rev)
        k_sum = k_sum + k_dir
    mu = jnp.mean(y, axis=-1, keepdims=True)
    var = jnp.mean(jnp.square(y - mu), axis=-1, keepdims=True)
    yn = ((y - mu) * lax.rsqrt(var + RWKV_GN_EPS)).reshape(b, s, GROUP_WIDTH)
    yn = yn * lnx_g.astype(jnp.float32) + lnx_b.astype(jnp.float32)
    bonus = (jnp.sum(rh * k_sum * r_k.astype(jnp.float32), axis=-1, keepdims=True) * vh).reshape(b, s, GROUP_WIDTH)
    return (yn + bonus).astype(r.dtype) * g


def retention_mixer(q, k, v, g, pos, gn_g):
    b, s, _ = q.shape
    nc, C = s // RET_CHUNK, RET_CHUNK
    qh = rope(q.reshape(b, s, GROUP_HEADS, HEAD_DIM), pos)
    kh = rope(k.reshape(b, s, GROUP_HEADS, HEAD_DIM), pos) * (HEAD_DIM ** -0.5)
    vh = v.reshape(b, s, GROUP_HEADS, HEAD_DIM)
    chunk = lambda t: t.reshape(b, nc, C, GROUP_HEADS, HEAD_DIM).transpose(0, 3, 1, 2, 4).astype(jnp.float32)
    qc, kc, vc = chunk(qh), chunk(kh), chunk(vh)
    log_gamma = jnp.log(1.0 - 2.0 ** (-5.0 - jnp.arange(GROUP_HEADS, dtype=jnp.float32)))
    idx = jnp.arange(C, dtype=jnp.float32)
    intra_decay = jnp.exp(log_gamma[:, None, None] * jnp.abs(idx[:, None] - idx[None, :]))
    dec_start = jnp.exp(log_gamma[:, None] * idx)[None, :, None, :, None]
    dec_end = jnp.exp(log_gamma[:, None] * (C - idx))[None, :, None, :, None]
    chunk_decay = jnp.exp(log_gamma * C)[None, :, None, None]
    scores = jnp.einsum('bhnid,bhnjd->bhnij', qc, kc) * intra_decay[None, :, None]
    o = jnp.einsum('bhnij,bhnjd->bhnid', scores, vc)
    kv_f = jnp.einsum('bhnjd,bhnje->nbhde', kc * dec_end, vc)
    kv_b = jnp.einsum('bhnjd,bhnje->nbhde', kc * dec_start, vc)

    def step(R, summ):
        return R * chunk_decay + summ, R

    R0 = jnp.zeros((b, GROUP_HEADS, HEAD_DIM, HEAD_DIM), jnp.float32)
    _, R_f = lax.scan(step, R0, kv_f)
    _, R_b = lax.scan(step, R0, kv_b, reverse=True)
    o = (o + dec_start * jnp.einsum('bhnid,nbhde->bhnie', qc, R_f)
         + dec_end * jnp.einsum('bhnid,nbhde->bhnie', qc, R_b))
    o = o.transpose(0, 2, 3, 1, 4).reshape(b, s, GROUP_HEADS, HEAD_DIM)
    o = o * lax.rsqrt(jnp.mean(o * o, axis=-1, keepdims=True) + NORM_EPS)
    o = o.reshape(b, s, GROUP_WIDTH) * gn_g.astype(jnp.float32)
    return jax.nn.silu(g) * o.astype(q.dtype)


def mla_mixer(q_a, kv_a, pos, q_a_norm, q_b, kv_a_norm, kv_b):
    b, s, _ = q_a.shape
    q = (rmsnorm(q_a, q_a_norm) @ q_b).reshape(b, s, GROUP_HEADS, MLA_NOPE + MLA_ROPE)
    q_nope, q_rope = q[..., :MLA_NOPE], rope(q[..., MLA_NOPE:], pos)
    c_kv, k_rope = kv_a[..., :MLA_KV_RANK], kv_a[..., MLA_KV_RANK:]
    kv = (rmsnorm(c_kv, kv_a_norm) @ kv_b).reshape(b, s, GROUP_HEADS, MLA_NOPE + MLA_V)
    k_nope, v = kv[..., :MLA_NOPE], kv[..., MLA_NOPE:]
    k_rope = rope(k_rope[:, :, None, :], pos)[:, :, 0]
    scale = (MLA_NOPE + MLA_ROPE) ** -0.5
    nb = s // ATTN_BLOCK
    blocks = lambda t: t.reshape(b, nb, ATTN_BLOCK, GROUP_HEADS, t.shape[-1]).transpose(1, 0, 2, 3, 4)

    def attend(blk):
        qn, qr = blk
        sc = jnp.einsum('bqhd,bkhd->bhqk', qn, k_nope) + jnp.einsum('bqhr,bkr->bhqk', qr, k_rope)
        p = jax.nn.softmax(sc.astype(jnp.float32) * scale, axis=-1)
        return jnp.einsum('bhqk,bkhd->bqhd', p.astype(v.dtype), v)

    o = lax.map(attend, (blocks(q_nope), blocks(q_rope)))
    return o.transpose(1, 0, 2, 3, 4).reshape(b, s, GROUP_WIDTH)


def setup_inputs(seed: int = 0) -> dict:
    key = jax.random.key(seed)
    ks = iter(jax.random.split(key, 48))
    L, W, H, N = DEPTH, GROUP_WIDTH, GROUP_HEADS, HEAD_DIM
    f32 = jnp.float32

    def nrm(shape):
        return jax.random.normal(next(ks), shape, f32)

    def dense(shape, fan_in, scale=1.0):
        return nrm(shape) * (scale * fan_in ** -0.5)

    def gain(shape):
        return 1.0 + 0.02 * nrm(shape)

    decay_base = jnp.tile(jnp.linspace(-6.0, -1.0, N, dtype=f32), H)
    x = nrm((BATCH, SEQ, D_MODEL))
    offsets = jax.random.randint(next(ks), (BATCH, 1), 0, MAX_POS_OFFSET, dtype=jnp.int32)
    positions = offsets + jnp.arange(SEQ, dtype=jnp.int32)[None, :]
    return {
        'x': x,
        'positions': positions,
        'ffn1_norm': gain((L, D_MODEL)),
        'ffn1_w_gate': dense((L, D_MODEL, D_FF), D_MODEL),
        'ffn1_w_up': dense((L, D_MODEL, D_FF), D_MODEL),
        'ffn1_w_down': dense((L, D_FF, D_MODEL), D_FF),
        'mix_norm': gain((L, D_MODEL)),
        'w_in': dense((L, D_MODEL, IN_WIDTH), D_MODEL),
        'w_out': dense((L, MIX_WIDTH, D_MODEL), MIX_WIDTH),
        'conv_w': dense((L, CONV_WIDTH, W), CONV_WIDTH),
        'rwkv_w0_f': decay_base + 0.1 * nrm((L, W)),
        'rwkv_w0_b': decay_base + 0.1 * nrm((L, W)),
        'rwkv_w2_f': dense((L, RWKV_DECAY_RANK, W), RWKV_DECAY_RANK, 0.1),
        'rwkv_w2_b': dense((L, RWKV_DECAY_RANK, W), RWKV_DECAY_RANK, 0.1),
        'rwkv_a0_f': 0.1 * nrm((L, W)),
        'rwkv_a0_b': 0.1 * nrm((L, W)),
        'rwkv_a2_f': dense((L, RWKV_ICLR_RANK, W), RWKV_ICLR_RANK, 0.1),
        'rwkv_a2_b': dense((L, RWKV_ICLR_RANK, W), RWKV_ICLR_RANK, 0.1),
        'rwkv_g2': dense((L, RWKV_GATE_RANK, W), RWKV_GATE_RANK),
        'rwkv_k_k': 0.85 + 0.02 * nrm((L, W)),
        'rwkv_k_a': 1.0 + 0.02 * nrm((L, W)),
        'rwkv_r_k': 0.1 * nrm((L, H, N)),
        'rwkv_lnx_g': gain((L, W)),
        'rwkv_lnx_b': 0.01 * nrm((L, W)),
        'ret_gn_g': gain((L, W)),
        'mla_q_a_norm': gain((L, MLA_Q_RANK)),
        'mla_q_b': dense((L, MLA_Q_RANK, H * (MLA_NOPE + MLA_ROPE)), MLA_Q_RANK),
        'mla_kv_a_norm': gain((L, MLA_KV_RANK)),
        'mla_kv_b': dense((L, MLA_KV_RANK, H * (MLA_NOPE + MLA_V)), MLA_KV_RANK),
        'ffn2_norm': gain((L, D_MODEL)),
        'ffn2_w_gate': dense((L, D_MODEL, D_FF), D_MODEL),
        'ffn2_w_up': dense((L, D_MODEL, D_FF), D_MODEL),
        'ffn2_w_down': dense((L, D_FF, D_MODEL), D_FF),
        'final_norm': gain((D_MODEL,)),
    }


def reference(x, positions, ffn1_norm, ffn1_w_gate, ffn1_w_up, ffn1_w_down, mix_norm, w_in, w_out,
              conv_w, rwkv_w0_f, rwkv_w0_b, rwkv_w2_f, rwkv_w2_b, rwkv_a0_f, rwkv_a0_b, rwkv_a2_f,
              rwkv_a2_b, rwkv_g2, rwkv_k_k, rwkv_k_a, rwkv_r_k, rwkv_lnx_g, rwkv_lnx_b, ret_gn_g,
              mla_q_a_norm, mla_q_b, mla_kv_a_norm, mla_kv_b, ffn2_norm, ffn2_w_gate, ffn2_w_up,
              ffn2_w_down, final_norm):
    split_points = tuple(int(p) for p in np.cumsum(IN_SPLITS)[:-1])
    for l in range(DEPTH):
        x = x + 0.5 * swiglu(rmsnorm(x, ffn1_norm[l]), ffn1_w_gate[l], ffn1_w_up[l], ffn1_w_down[l])
        h = rmsnorm(x, mix_norm[l])
        u = h @ w_in[l]
        (c_x, c_b, c_c, rw_r, rw_k, rw_v, rw_wd_f, rw_wd_b, rw_ad_f, rw_ad_b, rw_gd,
         rt_q, rt_k, rt_v, rt_g, ml_qa, ml_kva) = jnp.split(u, split_points, axis=-1)
        y_conv = short_conv_mixer(c_x, c_b, c_c, conv_w[l])
        y_rwkv = rwkv7_mixer(rw_r, rw_k, rw_v, rw_wd_f, rw_wd_b, rw_ad_f, rw_ad_b, rw_gd,
                             rwkv_w0_f[l], rwkv_w0_b[l], rwkv_w2_f[l], rwkv_w2_b[l],
                             rwkv_a0_f[l], rwkv_a0_b[l], rwkv_a2_f[l], rwkv_a2_b[l], rwkv_g2[l],
                             rwkv_k_k[l], rwkv_k_a[l], rwkv_r_k[l], rwkv_lnx_g[l], rwkv_lnx_b[l])
        y_ret = retention_mixer(rt_q, rt_k, rt_v, rt_g, positions, ret_gn_g[l])
        y_mla = mla_mixer(ml_qa, ml_kva, positions, mla_q_a_norm[l], mla_q_b[l], mla_kv_a_norm[l], mla_kv_b[l])
        x = x + jnp.concatenate([y_conv, y_rwkv, y_ret, y_mla], axis=-1) @ w_out[l]
        x = x + 0.5 * swiglu(rmsnorm(x, ffn2_norm[l]), ffn2_w_gate[l], ffn2_w_up[l], ffn2_w_down[l])
    return rmsnorm(x, final_norm)
```

```python
import contextlib
import numpy as np
import concourse.bass as bass
import concourse.mybir as mybir
from concourse.bass_utils import run_bass_kernel_spmd

F32 = mybir.dt.float32
BF16 = mybir.dt.bfloat16
I32 = mybir.dt.int32
AF = mybir.ActivationFunctionType
ALU = mybir.AluOpType

D = 1024
T = 2048
DFF = 2816
DEPTH = 2
INW = 3488
NCH = 4
CH = 512
EPS = 1e-6

COMPUTE = ("pe", "act", "dve", "pool")
ENGS = ("sp", "pe", "act", "dve", "pool")
NDS = 24
EPOCH = 2000
NEPOCH = {"pe": 16, "act": 6, "dve": 8, "pool": 3}


class Buf:
    __slots__ = ("w", "r")

    def __init__(self):
        self.w = None
        self.r = {}


class TT:
    def __init__(self, h, name):
        self.h = h
        self.name = name
        self.whole = Buf()
        self.parts = {}

    def __getitem__(self, idx):
        return self.h[idx]

    def part(self, k):
        p = self.parts.get(k)
        if p is None:
            p = self.parts[k] = Buf()
        return p


def _split(a):
    if isinstance(a, tuple):
        return a[0], a[1]
    return a, None


class Prog:
    def __init__(self, nc, es):
        self.nc = nc
        self.es = es
        self.streams = {e: [] for e in ENGS}
        self.cnt = {e: 0 for e in COMPUTE}
        self.sem = {}
        for e in COMPUTE:
            for ep in range(NEPOCH[e]):
                self.sem[("c", e, ep)] = es.enter_context(nc.semaphore("s_%s%d" % (e, ep)))
        for i in range(NDS):
            self.sem[("d", i)] = es.enter_context(nc.semaphore("d%d" % i))
        self.dcnt = [0] * NDS
        self.dnext = 0
        self.seen = {e: {} for e in ENGS}
        self.psums = []
        self.psi = 0
        self.uid = 0

    def sb(self, scope, name, shape, dt):
        self.uid += 1
        h = scope.enter_context(self.nc.sbuf_tensor("%s_%d" % (name, self.uid), list(shape), dt))
        return TT(h, name)

    def ps(self):
        t = self.psums[self.psi]
        self.psi = (self.psi + 1) % len(self.psums)
        return t

    def _deps(self, reads, writes):
        deps = {}

        def add(ev):
            if ev is None:
                return
            k, v = ev
            if deps.get(k, 0) < v:
                deps[k] = v

        for a in reads:
            t, key = _split(a)
            add(t.whole.w)
            if key is None:
                for p in t.parts.values():
                    add(p.w)
            else:
                add(t.part(key).w)
        for a in writes:
            t, key = _split(a)
            add(t.whole.w)
            for kv in t.whole.r.items():
                add(kv)
            if key is None:
                for p in t.parts.values():
                    add(p.w)
                    for kv in p.r.items():
                        add(kv)
            else:
                p = t.part(key)
                add(p.w)
                for kv in p.r.items():
                    add(kv)
        return deps

    def _commit(self, reads, writes, ev):
        k, v = ev
        for a in reads:
            t, key = _split(a)
            b = t.whole if key is None else t.part(key)
            if b.r.get(k, 0) < v:
                b.r[k] = v
        for a in writes:
            t, key = _split(a)
            if key is None:
                t.whole.w = ev
                t.whole.r = {}
                t.parts = {}
            else:
                p = t.part(key)
                p.w = ev
                p.r = {}

    def _waits(self, eng, deps):
        waits = []
        for k, v in deps.items():
            if eng == "pe" and k[0] == "c" and k[1] == "pe":
                continue
            if self.seen[eng].get(k, 0) >= v:
                continue
            self.seen[eng][k] = v
            waits.append((k, v))
        return waits

    def op(self, eng, fn, reads=(), writes=()):
        pr_ = [a for a in reads if getattr(_split(a)[0], "psum", False)]
        if pr_:
            reads = [a for a in reads if not getattr(_split(a)[0], "psum", False)]
            writes = list(writes) + pr_
        deps = self._deps(reads, writes)
        waits = self._waits(eng, deps)
        j = self.cnt[eng]
        self.cnt[eng] += 1
        ev = (("c", eng, j // EPOCH), j % EPOCH + 1)
        self.streams[eng].append((waits, fn, ev[0], 1))
        self._commit(reads, writes, ev)

    def dma(self, q, out, in_, reads=(), writes=(), slow=False):
        deps = self._deps(reads, writes)
        i = self.dnext
        self.dnext = (i + 1) % NDS
        if self.dcnt[i] > 0:
            k = ("d", i)
            if deps.get(k, 0) < self.dcnt[i]:
                deps[k] = self.dcnt[i]
        waits = self._waits(q, deps)
        self.dcnt[i] += 16
        ev = (("d", i), self.dcnt[i])
        if slow:
            fn = lambda e, o=out, s=in_: e.dma_start(out=o, in_=s, allow_slow_non_contiguous=True)
        else:
            fn = lambda e, o=out, s=in_: e.dma_start(out=o, in_=s)
        self.streams[q].append((waits, fn, ev[0], 16))
        self._commit(reads, writes, ev)

    def barrier(self):
        tot = {}
        for e in COMPUTE:
            if self.cnt[e] > 0:
                j = self.cnt[e] - 1
                tot[("c", e, j // EPOCH)] = j % EPOCH + 1
        for i in range(NDS):
            if self.dcnt[i] > 0:
                tot[("d", i)] = self.dcnt[i]
        for e in ENGS:
            waits = []
            for k, v in tot.items():
                if k[0] == "c" and k[1] == e and e == "pe":
                    continue
                if self.seen[e].get(k, 0) >= v:
                    continue
                self.seen[e][k] = v
                waits.append((k, v))
            if waits:
                self.streams[e].append((waits, None, None, 0))

    def simulate(self):
        sem = {k: 0 for k in self.sem}
        pc = {e: 0 for e in ENGS}
        progress = True
        while progress:
            progress = False
            for e in ENGS:
                st = self.streams[e]
                while pc[e] < len(st):
                    waits, fn, semk, inc = st[pc[e]]
                    if any(sem[k] < v for k, v in waits):
                        break
                    if fn is not None:
                        sem[semk] += inc
                    pc[e] += 1
                    progress = True
        bad = {e: (pc[e], len(self.streams[e])) for e in ENGS if pc[e] < len(self.streams[e])}
        if bad:
            print("DEADLOCK", bad)
            for e in bad:
                waits = self.streams[e][pc[e]][0]
                print(e, [(k, v, sem[k]) for k, v in waits if sem[k] < v])
        else:
            print("SIM OK")
        return not bad

    def emit(self):
        nc = self.nc
        import os
        if os.environ.get("SIMCHECK"):
            self.simulate()
        print("COUNTS", self.cnt, max(self.dcnt), {e: len(v) for e, v in self.streams.items()})

        def run(stream):
            def f(e):
                for waits, fn, semk, inc in stream:
                    for k, v in waits:
                        e.wait_ge(self.sem[k], v)
                    if fn is not None:
                        fn(e).then_inc(self.sem[semk], inc)
            return f

        with nc.Block() as block:
            block.sync(run(self.streams["sp"]))
            block.tensor(run(self.streams["pe"]))
            block.scalar(run(self.streams["act"]))
            block.vector(run(self.streams["dve"]))
            block.gpsimd(run(self.streams["pool"]))

    def mm(self, ps, out, lhsT, rhs, start, stop, reads, writes=None):
        self.op("pe", lambda e, o=out, l=lhsT, r=rhs, s=start, p=stop: e.matmul(o, l, r, start=s, stop=p),
                reads=reads, writes=[ps] if writes is None else writes)

    def transpose(self, ps, out, in_, ident, reads):
        self.op("pe", lambda e, o=out, i=in_, d=ident: e.transpose(o, i, d), reads=reads, writes=[ps])

    def act(self, out, in_, func, reads, writes, bias=None, scale=None, eng="act"):
        kw = {}
        if bias is not None:
            kw["bias"] = bias
        if scale is not None:
            kw["scale"] = scale
        self.op(eng, lambda e, o=out, i=in_, f=func, kw=kw: e.activation(o, i, f, **kw), reads=reads, writes=writes)

    def tt(self, eng, out, in0, in1, op, reads, writes):
        self.op(eng, lambda e, o=out, a=in0, b=in1, p=op: e.tensor_tensor(o, a, b, p), reads=reads, writes=writes)

    def ts(self, eng, out, in0, s1, s2, op0, op1, reads, writes):
        if op1 is None:
            self.op(eng, lambda e, o=out, a=in0, s=s1, p=op0: e.tensor_scalar(o, a, s, None, p),
                    reads=reads, writes=writes)
        else:
            self.op(eng, lambda e, o=out, a=in0, s=s1, s_2=s2, p=op0, q=op1: e.tensor_scalar(o, a, s, s_2, p, q),
                    reads=reads, writes=writes)

    def stt(self, eng, out, in0, scalar, in1, op0, op1, reads, writes):
        self.op(eng, lambda e, o=out, a=in0, s=scalar, b=in1, p=op0, q=op1: e.scalar_tensor_tensor(o, a, s, b, p, q),
                reads=reads, writes=writes)

    def copy(self, eng, out, in_, reads, writes):
        if eng == "act":
            self.op(eng, lambda e, o=out, i=in_: e.copy(o, i), reads=reads, writes=writes)
        else:
            self.op(eng, lambda e, o=out, i=in_: e.tensor_copy(o, i), reads=reads, writes=writes)

    def memset(self, eng, t, ap, val):
        self.op(eng, lambda e, a=ap, v=val: e.memset(a, v), reads=[], writes=[t])


class Ctx:
    pass


def load_weight(P, C, dram_ap3, nk, ncols, dst, dst_tt, dst_key=None, q="sp", cast_eng="pool"):
    per = max(1, 2048 // ncols)
    k = 0
    while k < nk:
        kk = min(per, nk - k)
        st = C.stage[C.stage_i]
        C.stage_i = (C.stage_i + 1) % len(C.stage)
        sview = st[:, 0:kk * ncols].rearrange("p (k c) -> p k c", c=ncols)
        P.dma(q, sview, dram_ap3[:, k:k + kk, :], reads=[], writes=[st])
        P.copy(cast_eng, dst[:, k:k + kk, :], sview, reads=[st],
               writes=[(dst_tt, dst_key) if dst_key is not None else dst_tt])
        k += kk


def rmsnorm_to_bf16(P, C, xT, gcol, hT, scope):
    sq = [P.sb(scope, "sq", [128, CH], BF16) for _ in range(3)]
    rstd = [P.sb(scope, "rstd", [128, CH], F32) for _ in range(2)]
    xgs = [P.sb(scope, "xg", [128, CH], F32) for _ in range(2)]
    si = 0
    for n in range(NCH):
        cs = slice(n * CH, (n + 1) * CH)
        ps = P.ps()
        for k in range(8):
            s = sq[si % 3]
            si += 1
            P.act(s[:, :], xT[:, k, cs], AF.Square, reads=[(xT, (k, n))], writes=[s])
            P.mm(ps, ps[:, :], C.ones_mean[:, :], s[:, :], k == 0, k == 7, reads=[s, C.ones_mean])
        r = rstd[n % 2]
        P.act(r[:, :], ps[:, :], AF.Sqrt, reads=[ps, C.eps_col], writes=[r], bias=C.eps_col[:, 0:1])
        P.op("dve", lambda e, o=r[:, :]: e.reciprocal(o, o), reads=[r], writes=[r])
        for k in range(8):
            if k % 3 != 2:
                P.stt("dve", hT[:, k, cs], xT[:, k, cs], gcol[:, k:k + 1], r[:, :], ALU.mult, ALU.mult,
                      reads=[(xT, (k, n)), r, gcol], writes=[(hT, (k, n))])
            else:
                xg = xgs[(n * 8 + k) % 2]
                P.act(xg[:, :], xT[:, k, cs], AF.Copy, reads=[(xT, (k, n)), gcol], writes=[xg],
                      scale=gcol[:, k:k + 1])
                P.tt("pool", hT[:, k, cs], xg[:, :], r[:, :], ALU.mult, reads=[xg, r], writes=[(hT, (k, n))])


def phase_ffn(P, C, xT, gcol, wg, wu, wd):
    with contextlib.ExitStack() as scope:
        hT = P.sb(scope, "hT", [128, 8, T], BF16)
        rmsnorm_to_bf16(P, C, xT, gcol, hT, scope)
        actT = P.sb(scope, "actT", [128, 4, T], BF16)
        wgb = [P.sb(scope, "wgb", [128, 8, 512], BF16) for _ in range(2)]
        wub = [P.sb(scope, "wub", [128, 8, 512], BF16) for _ in range(2)]
        wdb = [P.sb(scope, "wdb", [128, 4, D], BF16) for _ in range(2)]
        sg = [P.sb(scope, "sg", [128, CH], F32) for _ in range(3)]
        wg3 = wg.rearrange("(k p) c -> p k c", p=128)
        wu3 = wu.rearrange("(k p) c -> p k c", p=128)
        wd3 = wd.rearrange("(k p) c -> p k c", p=128)
        blocks = []
        f0 = 0
        while f0 < DFF:
            fw = min(512, DFF - f0)
            blocks.append((f0, fw))
            f0 += fw

        def load(i):
            f0, fw = blocks[i]
            par = i % 2
            load_weight(P, C, wg3[:, :, f0:f0 + fw], 8, fw, wgb[par][:, :, 0:fw], wgb[par])
            load_weight(P, C, wu3[:, :, f0:f0 + fw], 8, fw, wub[par][:, :, 0:fw], wub[par])
            nm = fw // 128
            load_weight(P, C, wd3[:, f0 // 128:f0 // 128 + nm, :], nm, D, wdb[par][:, 0:nm, :], wdb[par])

        load(0)
        sgi = 0
        for i, (f0, fw) in enumerate(blocks):
            if i + 1 < len(blocks):
                load(i + 1)
            par = i % 2
            nm = fw // 128
            for n in range(NCH):
                for m in range(nm):
                    cs = slice(n * CH, (n + 1) * CH)
                    psg = P.ps()
                    psu = P.ps()
                    for k in range(8):
                        P.mm(psg, psg[:, :], wgb[par][:, k, m * 128:(m + 1) * 128], hT[:, k, cs], k == 0, k == 7,
                             reads=[wgb[par], (hT, (k, n))])
                    for k in range(8):
                        P.mm(psu, psu[:, :], wub[par][:, k, m * 128:(m + 1) * 128], hT[:, k, cs], k == 0, k == 7,
                             reads=[wub[par], (hT, (k, n))])
                    s = sg[sgi % 3]
                    sgi += 1
                    P.act(s[:, :], psg[:, :], AF.Silu, reads=[psg], writes=[s])
                    P.tt("dve", actT[:, m, cs], s[:, :], psu[:, :], ALU.mult, reads=[s, psu], writes=[(actT, (m, n))])
            for n in range(NCH):
                for j in range(8):
                    cs = slice(n * CH, (n + 1) * CH)
                    pso = P.ps()
                    for kk in range(nm):
                        P.mm(pso, pso[:, :], wdb[par][:, kk, j * 128:(j + 1) * 128], actT[:, kk, cs], kk == 0,
                             kk == nm - 1, reads=[wdb[par], (actT, (kk, n))])
                    P.stt("dve", xT[:, j, cs], pso[:, :], 0.5, xT[:, j, cs], ALU.mult, ALU.add,
                          reads=[pso, (xT, (j, n))], writes=[(xT, (j, n))])
        P.barrier()


def load_x(P, C, xT, x_d):
    with contextlib.ExitStack() as scope:
        xin = [P.sb(scope, "xin", [128, D], F32) for _ in range(4)]
        for b in range(T // 128):
            xi = xin[b % 4]
            P.dma("sp", xi[:, :], x_d[b * 128:(b + 1) * 128, :], reads=[], writes=[xi])
            for g in range(2):
                ps = P.ps()
                for kk in range(4):
                    k = g * 4 + kk
                    P.transpose(ps, ps[:, kk * 128:(kk + 1) * 128], xi[:, k * 128:(k + 1) * 128], C.ident[:, :],
                                reads=[xi, C.ident])
                eng = "dve" if g == 0 else "act"
                P.copy(eng, xT[:, g * 4:(g + 1) * 4, b * 128:(b + 1) * 128],
                       ps[:, :].rearrange("p (k c) -> p k c", c=128), reads=[ps],
                       writes=[(xT, (g * 4 + kk, b // 4)) for kk in range(4)])
        P.barrier()


def store_out(P, C, xT, gcol, out_d, do_norm=True):
    with contextlib.ExitStack() as scope:
        oT = [P.sb(scope, "oT", [128, 8, CH], F32) for _ in range(2)]
        otm = [P.sb(scope, "otm", [128, D], F32) for _ in range(4)]
        sq = [P.sb(scope, "sq", [128, CH], BF16) for _ in range(3)]
        rstd = [P.sb(scope, "rstd", [128, CH], F32) for _ in range(2)]
        si = 0
        for n in range(NCH):
            cs = slice(n * CH, (n + 1) * CH)
            o = oT[n % 2]
            if do_norm:
                ps = P.ps()
                for k in range(8):
                    s = sq[si % 3]
                    si += 1
                    P.act(s[:, :], xT[:, k, cs], AF.Square, reads=[(xT, (k, n))], writes=[s])
                    P.mm(ps, ps[:, :], C.ones_mean[:, :], s[:, :], k == 0, k == 7, reads=[s, C.ones_mean])
                r = rstd[n % 2]
                P.act(r[:, :], ps[:, :], AF.Sqrt, reads=[ps, C.eps_col], writes=[r], bias=C.eps_col[:, 0:1])
                P.op("dve", lambda e, o=r[:, :]: e.reciprocal(o, o), reads=[r], writes=[r])
                for k in range(8):
                    P.stt("dve", o[:, k, :], xT[:, k, cs], gcol[:, k:k + 1], r[:, :], ALU.mult, ALU.mult,
                          reads=[(xT, (k, n)), r, gcol], writes=[(o, k)])
            else:
                for k in range(8):
                    P.copy("dve", o[:, k, :], xT[:, k, cs], reads=[(xT, (k, n))], writes=[(o, k)])
            for bb in range(4):
                b = n * 4 + bb
                ot = otm[b % 4]
                for g in range(2):
                    ps = P.ps()
                    for kk in range(4):
                        k = g * 4 + kk
                        P.transpose(ps, ps[:, kk * 128:(kk + 1) * 128], o[:, k, bb * 128:(bb + 1) * 128],
                                    C.ident[:, :], reads=[(o, k), C.ident])
                    eng = "dve" if g == 0 else "act"
                    P.copy(eng, ot[:, g * 512:(g + 1) * 512], ps[:, :], reads=[ps], writes=[(ot, g)])
                P.dma("sp", out_d[b * 128:(b + 1) * 128, :], ot[:, :], reads=[ot], writes=[])
        P.barrier()


import math
C1_2PI = 6.28125
C2_2PI = 2.0 * math.pi - 6.28125
O_CX, O_CB, O_CC = 0, 256, 512
O_RR, O_RK, O_RV = 768, 1024, 1280
O_WDF, O_WDB, O_ADF, O_ADB, O_GD = 1536, 1600, 1664, 1728, 1792
O_TQ, O_TK, O_TV, O_TG = 1920, 2176, 2432, 2688
O_MQA, O_MKV, O_MKR = 2944, 3328, 3456
O_ROT = INW
NROT = 544
UROWS = INW + NROT


def load_rows(P, dst_tt, dst_ap, U, r0, nrows, q="sp"):
    P.dma(q, dst_ap, U[r0:r0 + nrows, :], reads=[], writes=[dst_tt])


def phase_inproj(P, C, xT, gcol, w_in, w_rot, U, VT):
    with contextlib.ExitStack() as scope:
        hT = P.sb(scope, "hT", [128, 8, T], BF16)
        rmsnorm_to_bf16(P, C, xT, gcol, hT, scope)
        wb = [P.sb(scope, "wb", [128, 8, 512], BF16) for _ in range(2)]
        ev = [P.sb(scope, "ev", [128, CH], F32) for _ in range(4)]
        evi = 0
        jobs = []
        for (w, ncols, row0) in ((w_in, INW, 0), (w_rot, NROT, O_ROT)):
            w3 = w.rearrange("(k p) c -> p k c", p=128)
            c0 = 0
            while c0 < ncols:
                cw = min(512, ncols - c0)
                jobs.append((w3, c0, cw, row0))
                c0 += cw

        def load(i):
            w3, c0, cw, row0 = jobs[i]
            load_weight(P, C, w3[:, :, c0:c0 + cw], 8, cw, wb[i % 2][:, :, 0:cw], wb[i % 2])

        load(0)
        for i, (w3, c0, cw, row0) in enumerate(jobs):
            if i + 1 < len(jobs):
                load(i + 1)
            b = wb[i % 2]
            for n in range(NCH):
                cs = slice(n * CH, (n + 1) * CH)
                m0 = 0
                while m0 < cw:
                    mw = min(128, cw - m0)
                    gcol_ = c0 + m0
                    if row0 == 0 and (O_RV <= gcol_ < O_RV + 256 or O_TV <= gcol_ < O_TV + 256):
                        m0 += mw
                        continue
                    ps = P.ps()
                    for k in range(8):
                        P.mm(ps, ps[0:mw, :], b[:, k, m0:m0 + mw], hT[:, k, cs], k == 0, k == 7,
                             reads=[b, (hT, (k, n))])
                    e = ev[evi % 4]
                    evi += 1
                    P.copy("act" if evi % 2 else "dve", e[0:mw, :], ps[0:mw, :], reads=[ps], writes=[e])
                    P.dma("sp", U[row0 + c0 + m0:row0 + c0 + m0 + mw, cs], e[0:mw, :], reads=[e], writes=[])
                    m0 += mw
        w3 = w_in.rearrange("(k p) c -> p k c", p=128)
        vb = wb[len(jobs) % 2]
        load_weight(P, C, w3[:, :, O_RV:O_RV + 256], 8, 256, vb[:, :, 0:256], vb, dst_key="a")
        load_weight(P, C, w3[:, :, O_TV:O_TV + 256], 8, 256, vb[:, :, 256:512], vb, dst_key="b")
        for b in range(T // 128):
            ps = P.ps()
            for k in range(8):
                P.mm(ps, ps[:, :], hT[:, k, b * 128:(b + 1) * 128], vb[:, k, :], k == 0, k == 7,
                     reads=[vb, (hT, (k, b // 4))])
            e = ev[evi % 4]
            evi += 1
            P.copy("act" if evi % 2 else "dve", e[:, :], ps[:, :], reads=[ps], writes=[e])
            P.dma("sp", VT[b * 128:(b + 1) * 128, :], e[:, :], reads=[e], writes=[])
        P.barrier()


def mixer_conv(P, C, l, dr, U, yT):
    with contextlib.ExitStack() as scope:
        cw = P.sb(scope, "cw", [128, 2, 3], F32)
        for ct in range(2):
            P.dma("sp", cw[:, ct, :], dr["conv_w"][l][:, ct * 128:(ct + 1) * 128].rearrange("j p -> p j"),
                  reads=[], writes=[cw], slow=True)
        sets = [[P.sb(scope, nm, [128, T + (2 if nm == "cu" else 0)], F32) for nm in ("cxs", "cbs", "ccs", "cu", "cacc")]
                for _ in range(2)]
        for ct in range(2):
            xs, bs, cs_, u, acc = sets[ct]
            load_rows(P, xs, xs[:, :], U, O_CX + ct * 128, 128)
            load_rows(P, bs, bs[:, :], U, O_CB + ct * 128, 128)
            load_rows(P, cs_, cs_[:, :], U, O_CC + ct * 128, 128)
        for ct in range(2):
            xs, bs, cs_, u, acc = sets[ct]
            P.memset("pool", u, u[:, :], 0.0)
            P.tt("dve" if ct == 0 else "pool", u[:, 1:T + 1], cs_[:, :], xs[:, :], ALU.mult, reads=[cs_, xs], writes=[u])
            P.ts("dve", acc[:, :], u[:, 0:T], cw[:, ct, 0:1], None, ALU.mult, None, reads=[u, cw], writes=[acc])
            P.stt("dve", acc[:, :], u[:, 1:T + 1], cw[:, ct, 1:2], acc[:, :], ALU.mult, ALU.add,
                  reads=[u, cw, acc], writes=[acc])
            P.stt("dve", acc[:, :], u[:, 2:T + 2], cw[:, ct, 2:3], acc[:, :], ALU.mult, ALU.add,
                  reads=[u, cw, acc], writes=[acc])
            P.tt("dve" if ct == 0 else "pool", yT[:, ct, :], acc[:, :], bs[:, :], ALU.mult, reads=[acc, bs],
                 writes=[(yT, ct)])
        P.barrier()


def rope_tables(P, C, scope, pos_d, inv_ap, sgn_ap):
    cos = P.sb(scope, "cos", [128, T], F32)
    sinS = P.sb(scope, "sinS", [128, T], F32)
    with contextlib.ExitStack() as tmp:
        A = P.sb(tmp, "rpA", [128, T], I32)
        B = P.sb(tmp, "rpB", [128, T], F32)
        Cc = P.sb(tmp, "rpC", [128, T], F32)
        P.dma("sp", A[:, :], pos_d.partition_broadcast(128), reads=[], writes=[A])
        P.copy("dve", B[:, :], A[:, :], reads=[A], writes=[B])
        P.ts("dve", B[:, :], B[:, :], inv_ap, None, ALU.mult, None, reads=[B, C.consts], writes=[B])
        P.ts("dve", Cc[:, :], B[:, :], 1.0 / (2.0 * math.pi), None, ALU.mult, None, reads=[B], writes=[Cc])
        P.copy("dve", A[:, :], Cc[:, :], reads=[Cc], writes=[A])
        P.copy("dve", Cc[:, :], A[:, :], reads=[A], writes=[Cc])
        P.stt("dve", B[:, :], Cc[:, :], -C1_2PI, B[:, :], ALU.mult, ALU.add, reads=[Cc, B], writes=[B])
        P.stt("dve", B[:, :], Cc[:, :], -C2_2PI, B[:, :], ALU.mult, ALU.add, reads=[Cc, B], writes=[B])
        P.ts("dve", Cc[:, :], B[:, :], math.pi, None, ALU.is_gt, None, reads=[B], writes=[Cc])
        P.stt("dve", B[:, :], Cc[:, :], -2.0 * math.pi, B[:, :], ALU.mult, ALU.add, reads=[Cc, B], writes=[B])
        P.act(sinS[:, :], B[:, :], AF.Sin, reads=[B, C.consts], writes=[sinS], scale=sgn_ap)
        P.ts("dve", B[:, :], B[:, :], 0.5 * math.pi, None, ALU.add, None, reads=[B], writes=[B])
        P.ts("dve", Cc[:, :], B[:, :], math.pi, None, ALU.is_gt, None, reads=[B], writes=[Cc])
        P.stt("dve", B[:, :], Cc[:, :], -2.0 * math.pi, B[:, :], ALU.mult, ALU.add, reads=[Cc, B], writes=[B])
        P.act(cos[:, :], B[:, :], AF.Sin, reads=[B], writes=[cos])
        P.barrier()
    return cos, sinS


def rope_load(P, C, scope, R, idx):
    cos = P.sb(scope, "cos", [128, T], F32)
    sinS = P.sb(scope, "sinS", [128, T], F32)
    P.dma("sp", cos[:, :], R[idx], reads=[], writes=[cos])
    P.dma("sp", sinS[:, :], R[idx + 1], reads=[], writes=[sinS])
    return cos, sinS


def rstd_from_sum(P, C, r_ap, ps_ap, r_tt, ps_tt, scale, plo, phi):
    P.act(r_ap, ps_ap, AF.Sqrt, reads=[ps_tt, C.eps_col], writes=[r_tt], bias=C.eps_col[plo:phi, 0:1], scale=scale)
    P.op("dve", lambda e, o=r_ap: e.reciprocal(o, o), reads=[r_tt], writes=[r_tt])


def mixer_ret(P, C, l, dr, U, VT, yT):
    lng = [math.log(1.0 - 2.0 ** (-5.0 - h)) for h in range(4)]
    with contextlib.ExitStack() as scope:
        qT = P.sb(scope, "rqT", [128, 2, T], BF16)
        kT = P.sb(scope, "rkT", [128, 2, T], BF16)
        gng = P.sb(scope, "gng", [128, 2], F32)
        P.dma("sp", gng[:, :], dr["ret_gn_g"][l].rearrange("(t p) -> p t", p=128), reads=[], writes=[gng], slow=True)
        vtm = P.sb(scope, "rvtm", [128, 16, 256], BF16)
        NM = 3968
        with contextlib.ExitStack() as sa:
            vsts = [P.sb(sa, "rvst", [128, 4, 256], F32) for _ in range(1)]
            cos, sinS = rope_load(P, C, sa, C.ROPE, 0)
            abuf = [P.sb(sa, "ra", [128, T], F32) for _ in range(2)]
            bbuf = [P.sb(sa, "rb", [128, T], F32) for _ in range(2)]
            for (dst, o_main, o_rot, scale) in ((qT, O_TQ, O_ROT, 1.0), (kT, O_TK, O_ROT + 256, 0.125)):
                for hp in range(2):
                    a = abuf[hp]
                    b = bbuf[hp]
                    load_rows(P, a, a[:, :], U, o_main + hp * 128, 128)
                    load_rows(P, b, b[:, :], U, o_rot + hp * 128, 128)
                    P.stt("dve", a[:, :], a[:, :], scale, cos[:, :], ALU.mult, ALU.mult, reads=[a, cos], writes=[a])
                    P.stt("dve", b[:, :], b[:, :], scale, sinS[:, :], ALU.mult, ALU.mult, reads=[b, sinS], writes=[b])
                    P.tt("pool", dst[:, hp, :], a[:, :], b[:, :], ALU.add, reads=[a, b], writes=[(dst, hp)])
            for half in range(4):
                vs_ = vsts[0]
                P.dma("sp", vs_[:, :, :],
                      VT[half * 512:(half + 1) * 512, 256:512].rearrange("(b p) c -> p b c", p=128),
                      reads=[], writes=[vs_])
                P.copy("pool", vtm[:, half * 4:(half + 1) * 4, :], vs_[:, :, :], reads=[vs_], writes=[(vtm, half)])
            P.barrier()
        with contextlib.ExitStack() as sb_:
            Dt = P.sb(sb_, "rD", [128, NM], F32)
            P.op("pool", lambda e, Dt=Dt, NM=NM: e.iota(Dt[:, :], [[1, NM]], base=-1920, channel_multiplier=-1,
                                          allow_small_or_imprecise_dtypes=True), reads=[], writes=[Dt])
            P.act(Dt[:, :], Dt[:, :], AF.Abs, reads=[Dt], writes=[Dt])
            mask = P.sb(sb_, "rmask", [128, NM], F32)
            sg = P.sb(sb_, "rsg", [128, T], F32)
            sm = [P.sb(sb_, "rsm", [128, CH], BF16) for _ in range(7)]
            sq = [P.sb(sb_, "rsq", [128, CH], BF16) for _ in range(2)]
            scb = [P.sb(sb_, "rscb", [128, CH], F32) for _ in range(2)]
            rs = [P.sb(sb_, "rrs", [128, CH], F32) for _ in range(2)]
            tmp = [P.sb(sb_, "rtmp", [128, CH], F32) for _ in range(2)]
            smi = 0
            it = 0
            for h in range(4):
                hp, pb = h // 2, (h % 2) * 64
                pr = slice(pb, pb + 64)
                if h % 2 == 0:
                    load_rows(P, sg, sg[:, :], U, O_TG + hp * 128, 128)
                    P.act(sg[:, :], sg[:, :], AF.Silu, reads=[sg], writes=[sg])
                P.act(mask[:, :], Dt[:, :], AF.Exp, reads=[Dt], writes=[mask], scale=lng[h])
                LA = 3
                pend = []

                def stage_c(n, j, m_, it_):
                    cs = slice(n * CH, (n + 1) * CH)
                    num = P.acc[it_ % 2]
                    P.mm(num, num[:, :], vtm[:, j, hp * 128:(hp + 1) * 128], m_[:, :], j == 0, j == 15,
                         reads=[vtm, m_])
                    if j != 15:
                        return
                    q_ = sq[it_ % 2]
                    P.act(q_[pr, :], num[pr, :], AF.Square, reads=[num], writes=[q_])
                    st = P.ps()
                    P.mm(st, st[:, :], C.ones64[pr, :], q_[pr, :], True, True, reads=[C.ones64, q_])
                    r = rs[it_ % 2]
                    rstd_from_sum(P, C, r[pr, :], st[pr, :], r, st, 1.0, pb, pb + 64)
                    t_ = tmp[it_ % 2]
                    P.stt("dve", t_[pr, :], num[pr, :], gng[pr, hp:hp + 1], r[pr, :], ALU.mult, ALU.mult,
                          reads=[num, gng, r], writes=[t_])
                    P.tt("dve", yT[pr, hp, cs], t_[pr, :], sg[pr, cs], ALU.mult, reads=[t_, sg],
                         writes=[(yT, (hp, n, pb))])

                G = 3
                steps = [(n, j) for n in range(NCH) for j in range(16)]
                prev = []
                for g0 in range(0, len(steps), G):
                    grp = steps[g0:g0 + G]
                    sps = []
                    for (n, j) in grp:
                        cs = slice(n * CH, (n + 1) * CH)
                        sp = P.ps()
                        P.mm(sp, sp[:, :], kT[pr, hp, j * 128:(j + 1) * 128], qT[pr, hp, cs], True, True,
                             reads=[kT, qT])
                        sps.append(sp)
                    cur = []
                    for (n, j), sp in zip(grp, sps):
                        off = n * CH - j * 128 + 1920
                        m_ = sm[smi % len(sm)]
                        smi += 1
                        if smi % 3 == 0:
                            sc_ = scb[(smi // 3) % 2]
                            P.copy("act", sc_[:, :], sp[:, :], reads=[sp], writes=[sc_])
                            P.tt("pool", m_[:, :], sc_[:, :], mask[:, off:off + CH], ALU.mult, reads=[sc_, mask],
                                 writes=[m_])
                        else:
                            P.tt("dve", m_[:, :], sp[:, :], mask[:, off:off + CH], ALU.mult, reads=[sp, mask],
                                 writes=[m_])
                        cur.append((n, j, m_, it + n))
                    for a_ in prev:
                        stage_c(*a_)
                    prev = cur
                for a_ in prev:
                    stage_c(*a_)
                it += NCH
            P.barrier()


def mixer_mla(P, C, l, dr, U, yT):
    with contextlib.ExitStack() as scope:
        qfull = P.sb(scope, "mqf", [128, 4, T], BF16)
        kfull = P.sb(scope, "mkf", [128, 4, T], BF16)
        qb = P.sb(scope, "mqb", [128, 3, 384], BF16)
        qbr = P.sb(scope, "mqbr", [128, 3, 384], BF16)
        kvb = P.sb(scope, "mkvb", [128, 1, 512], BF16)
        load_weight(P, C, dr["mla_q_b"][l].rearrange("(k p) c -> p k c", p=128), 3, 384, qb[:, :, :], qb)
        load_weight(P, C, dr["q_b_rot"][l].rearrange("(k p) c -> p k c", p=128), 3, 384, qbr[:, :, :], qbr)
        load_weight(P, C, dr["mla_kv_b"][l].rearrange("(k p) c -> p k c", p=128), 1, 512, kvb[:, :, :], kvb)
        qan = P.sb(scope, "mqan", [128, 3], F32)
        kvan = P.sb(scope, "mkvan", [128, 1], F32)
        P.dma("sp", qan[:, :], dr["mla_q_a_norm"][l].rearrange("(k p) -> p k", p=128), reads=[], writes=[qan], slow=True)
        P.dma("sp", kvan[:, :], dr["mla_kv_a_norm"][l].rearrange("(k p) -> p k", p=128), reads=[], writes=[kvan],
              slow=True)
        sqb = [P.sb(scope, "msq", [128, CH], BF16) for _ in range(3)]
        rr = [P.sb(scope, "mrr", [128, CH], F32) for _ in range(2)]
        sqi = 0
        with contextlib.ExitStack() as sa:
            cos, sinS = rope_load(P, C, sa, C.ROPE, 2)
            qn = P.sb(sa, "mqn", [128, 3, T], BF16)
            with contextlib.ExitStack() as sq_:
                qas = [P.sb(sq_, "mqa", [128, 3, CH], F32) for _ in range(2)]
                for n in range(NCH):
                    cs = slice(n * CH, (n + 1) * CH)
                    qa = qas[n % 2]
                    P.dma("sp", qa[:, :, :], U[O_MQA:O_MQA + 384, cs].rearrange("(k p) c -> p k c", p=128),
                          reads=[], writes=[qa])
                    ps = P.ps()
                    for k in range(3):
                        s_ = sqb[sqi % 3]
                        sqi += 1
                        P.act(s_[:, :], qa[:, k, :], AF.Square, reads=[qa], writes=[s_])
                        P.mm(ps, ps[:, :], C.ones_bf[:, :], s_[:, :], k == 0, k == 2, reads=[C.ones_bf, s_])
                    r = rr[n % 2]
                    rstd_from_sum(P, C, r[:, :], ps[:, :], r, ps, 1.0 / 384.0, 0, 128)
                    for k in range(3):
                        P.stt("dve", qn[:, k, cs], qa[:, k, :], qan[:, k:k + 1], r[:, :], ALU.mult, ALU.mult,
                              reads=[qa, qan, r], writes=[(qn, (k, n))])
                P.barrier()
            t1 = [P.sb(sa, "mt1", [128, CH], F32) for _ in range(2)]
            t2 = [P.sb(sa, "mt2", [128, CH], F32) for _ in range(2)]
            ti = 0
            rp = slice(64, 96)
            for h in range(4):
                for n in range(NCH):
                    cs = slice(n * CH, (n + 1) * CH)
                    ps = P.ps()
                    psr = P.ps()
                    for k in range(3):
                        P.mm(ps, ps[0:96, :], qb[:, k, h * 96:(h + 1) * 96], qn[:, k, cs], k == 0, k == 2,
                             reads=[qb, (qn, (k, n))])
                    for k in range(3):
                        P.mm(psr, psr[0:96, :], qbr[:, k, h * 96:(h + 1) * 96], qn[:, k, cs], k == 0, k == 2,
                             reads=[qbr, (qn, (k, n))])
                    P.copy("act", qfull[0:64, h, cs], ps[0:64, :], reads=[ps], writes=[(qfull, (h, n, 0))])
                    a_ = t1[ti % 2]
                    b_ = t2[ti % 2]
                    ti += 1
                    P.tt("dve", a_[rp, :], ps[rp, :], cos[rp, cs], ALU.mult, reads=[ps, cos], writes=[a_])
                    P.tt("dve", b_[rp, :], psr[rp, :], sinS[rp, cs], ALU.mult, reads=[psr, sinS], writes=[b_])
                    P.tt("pool", qfull[rp, h, cs], a_[rp, :], b_[rp, :], ALU.add, reads=[a_, b_],
                         writes=[(qfull, (h, n, 1))])
            for n in range(NCH):
                cs = slice(n * CH, (n + 1) * CH)
                a_ = t1[ti % 2]
                b_ = t2[ti % 2]
                ti += 1
                P.dma("sp", a_[rp, :], U[O_MKR:O_MKR + 32, cs], reads=[], writes=[a_])
                P.dma("sp", b_[rp, :], U[O_ROT + 512:O_ROT + 544, cs], reads=[], writes=[b_])
                P.tt("dve", a_[rp, :], a_[rp, :], cos[rp, cs], ALU.mult, reads=[a_, cos], writes=[a_])
                P.tt("dve", b_[rp, :], b_[rp, :], sinS[rp, cs], ALU.mult, reads=[b_, sinS], writes=[b_])
                for h in range(4):
                    P.tt("pool", kfull[rp, h, cs], a_[rp, :], b_[rp, :], ALU.add, reads=[a_, b_],
                         writes=[(kfull, (h, 1, n))])
            P.barrier()
        vtm = P.sb(scope, "mvtm", [128, 16, 256], BF16)
        with contextlib.ExitStack() as sk:
            ckv = P.sb(sk, "mckv", [128, T], F32)
            load_rows(P, ckv, ckv[:, :], U, O_MKV, 128)
            kvn = P.sb(sk, "mkvn", [128, T], BF16)
            for n in range(NCH):
                cs = slice(n * CH, (n + 1) * CH)
                ps = P.ps()
                s_ = sqb[sqi % 3]
                sqi += 1
                P.act(s_[:, :], ckv[:, cs], AF.Square, reads=[ckv], writes=[s_])
                P.mm(ps, ps[:, :], C.ones_bf[:, :], s_[:, :], True, True, reads=[C.ones_bf, s_])
                r = rr[n % 2]
                rstd_from_sum(P, C, r[:, :], ps[:, :], r, ps, 1.0 / 128.0, 0, 128)
                P.stt("dve", kvn[:, cs], ckv[:, cs], kvan[:, 0:1], r[:, :], ALU.mult, ALU.mult,
                      reads=[ckv, kvan, r], writes=[(kvn, n)])
            ci = 0
            for h in range(4):
                for n in range(NCH):
                    cs = slice(n * CH, (n + 1) * CH)
                    ps = P.ps()
                    P.mm(ps, ps[0:64, :], kvb[:, 0, h * 128:h * 128 + 64], kvn[:, cs], True, True,
                         reads=[kvb, (kvn, n)])
                    ci += 1
                    P.copy("act" if ci % 2 else "dve", kfull[0:64, h, cs], ps[0:64, :], reads=[ps],
                           writes=[(kfull, (h, 0, n))])
            for b in range(T // 128):
                ps = P.ps()
                P.mm(ps, ps[:, :], kvn[:, b * 128:(b + 1) * 128], kvb[:, 0, :], True, True, reads=[kvb, (kvn, b // 4)])
                ci += 1
                P.copy("act" if ci % 2 else "dve", vtm[:, b, :].rearrange("p (h d) -> p h d", d=64),
                       ps[:, :].rearrange("p (h e) -> p h e", e=128)[:, :, 64:128], reads=[ps], writes=[(vtm, b)])
            P.barrier()
        with contextlib.ExitStack() as st_:
            pt = [P.sb(st_, "mp", [128, CH], BF16) for _ in range(7)]
            rd = [P.sb(st_, "mrd", [128, CH], F32) for _ in range(2)]
            pi_ = 0
            it = 0
            sc = 96.0 ** -0.5
            for h in range(4):
                hp, pb = h // 2, (h % 2) * 64
                pr = slice(pb, pb + 64)
                LA = 3
                pend = []

                def stage_c(n, j, p_, it_):
                    cs = slice(n * CH, (n + 1) * CH)
                    num = P.acc[0]
                    den = P.acc[1]
                    P.mm(num, num[:, :], vtm[:, j, hp * 128:(hp + 1) * 128], p_[:, :], j == 0, j == 15,
                         reads=[vtm, p_])
                    P.mm(den, den[:, :], C.ones_bf[:, :], p_[:, :], j == 0, j == 15, reads=[C.ones_bf, p_])
                    if j != 15:
                        return
                    r = rd[it_ % 2]
                    P.op("dve", lambda e, o=r[pr, :], i=den[pr, :]: e.reciprocal(o, i), reads=[den], writes=[r])
                    P.tt("dve", yT[pr, hp, cs], num[pr, :], r[pr, :], ALU.mult, reads=[num, r],
                         writes=[(yT, (hp, n, pb))])

                G = 3
                steps = [(n, j) for n in range(NCH) for j in range(16)]
                prev = []
                for g0 in range(0, len(steps), G):
                    grp = steps[g0:g0 + G]
                    sps = []
                    for (n, j) in grp:
                        cs = slice(n * CH, (n + 1) * CH)
                        sp = P.ps()
                        P.mm(sp, sp[:, :], kfull[0:96, h, j * 128:(j + 1) * 128], qfull[0:96, h, cs], True, True,
                             reads=[kfull, qfull])
                        sps.append(sp)
                    cur = []
                    for (n, j), sp in zip(grp, sps):
                        p_ = pt[pi_ % len(pt)]
                        pi_ += 1
                        P.act(p_[:, :], sp[:, :], AF.Exp, reads=[sp], writes=[p_], scale=sc)
                        cur.append((n, j, p_, it + n))
                    for a_ in prev:
                        stage_c(*a_)
                    prev = cur
                for a_ in prev:
                    stage_c(*a_)
                it += NCH
            P.barrier()


            P.barrier()


AX = mybir.AxisListType
DEBUG_RW = None
DEBUG_CORES = None
DEBUG_TRACE = False
DEBUG_LAYERS = None


def mixer_rwkv(P, C, l, dr, U, VT, yT):
    NEG_E = -math.exp(-0.5)
    dbg = DEBUG_RW if l == 1 else None
    saved_ps = P.psums
    P.psums = saved_ps + P.acc
    P.psi = 0
    try:
        _mixer_rwkv(P, C, l, dr, U, VT, yT, dbg, NEG_E)
    finally:
        P.psums = saved_ps
        P.psi = 0


def _mixer_rwkv(P, C, l, dr, U, VT, yT, dbg, NEG_E):
    NB = T // 128
    with contextlib.ExitStack() as scope:
        cols = P.sb(scope, "wcols", [128, 2, 8], F32)
        for i, nm in enumerate(["rwkv_w0_f", "rwkv_w0_b", "rwkv_a0_f", "rwkv_a0_b", "rwkv_k_k", "rwkv_k_a"]):
            P.dma("sp", cols[:, :, i], dr[nm][l].rearrange("(t p) -> p t", p=128), reads=[], writes=[cols], slow=True)
        P.dma("sp", cols[:, :, 7], dr["rwkv_r_k"][l].rearrange("h n -> (h n)").rearrange("(t p) -> p t", p=128),
              reads=[], writes=[cols], slow=True)
        P.ts("dve", cols[:, :, 6], cols[:, :, 5], -1.0, 1.0, ALU.mult, ALU.add, reads=[cols], writes=[cols])
        gneps = P.sb(scope, "gneps", [128, 1], F32)
        P.memset("pool", gneps, gneps[:, :], 64e-5)
        lw = P.sb(scope, "lw", [128, 4, 256], BF16)
        g2b = P.sb(scope, "g2b", [128, 256], BF16)
        Lg = P.sb(scope, "Lg", [128, 256], F32)
        Lb = P.sb(scope, "Lb", [128, 256], F32)
        vtm = P.sb(scope, "wvtm", [128, NB, 256], BF16)
        with contextlib.ExitStack() as s0:
            lst = P.sb(s0, "lst", [128, 256], F32)
            for i, nm in enumerate(["rwkv_w2_f", "rwkv_w2_b", "rwkv_a2_f", "rwkv_a2_b"]):
                lp = slice(0, 64) if i % 2 == 0 else slice(64, 128)
                P.dma("sp", lst[lp, :], dr[nm][l], reads=[], writes=[lst])
                P.copy("dve", lw[lp, i, :], lst[lp, :], reads=[lst], writes=[(lw, i)])
            P.dma("sp", lst[:, :], dr["rwkv_g2"][l], reads=[], writes=[lst])
            P.copy("dve", g2b[:, :], lst[:, :], reads=[lst], writes=[g2b])
            P.dma("sp", Lg[:, :], dr["rwkv_lnx_g"][l].rearrange("(o c) -> o c", o=1).partition_broadcast(128),
                  reads=[], writes=[Lg])
            P.dma("sp", Lb[:, :], dr["rwkv_lnx_b"][l].rearrange("(o c) -> o c", o=1).partition_broadcast(128),
                  reads=[], writes=[Lb])
            vst = P.sb(s0, "wvst", [128, 4, 256], F32)
            for q4 in range(4):
                P.dma("sp", vst[:, :, :], VT[q4 * 512:(q4 + 1) * 512, 0:256].rearrange("(b p) c -> p b c", p=128),
                      reads=[], writes=[vst])
                P.copy("pool", vtm[:, q4 * 4:(q4 + 1) * 4, :], vst[:, :, :], reads=[vst], writes=[(vtm, q4)])
            P.barrier()
        MS = [P.sb(scope, "MS", [128, 256], BF16) for _ in range(2)]
        NMS = [P.sb(scope, "NMS", [128, 256], BF16) for _ in range(2)]
        for di in range(2):
            pat = [[1, 128]] if di == 0 else [[-1, 128]]
            cm = -1 if di == 0 else 1
            for j, cmp_ in enumerate((ALU.is_gt, ALU.is_ge)):
                P.op("pool", lambda e, o=MS[di][:, j * 128:(j + 1) * 128], pt=pat, c_=cmp_, m_=cm:
                     e.affine_select(o, C.ones_f[:, :], pt, c_, 0.0, base=0, channel_multiplier=m_),
                     reads=[C.ones_f], writes=[(MS[di], j)])
            P.ts("dve", NMS[di][:, :], MS[di][:, :], -1.0, None, ALU.mult, None, reads=[MS[di]], writes=[NMS[di]])
        smask = P.sb(scope, "smask", [128, CH], F32)
        P.memset("pool", smask, smask[:, :], 1.0)
        P.memset("pool", smask, smask[:, :].rearrange("p (c t) -> p c t", t=128)[:, :, 0:1], 0.0)
        identb = P.sb(scope, "identb", [128, 128], BF16)
        P.copy("dve", identb[:, :], C.ident[:, :], reads=[C.ident], writes=[identb])
        blk64 = P.sb(scope, "blk64", [128, 128], BF16)
        P.memset("pool", blk64, blk64[:, :], 0.0)
        P.memset("pool", blk64, blk64[0:64, 0:64], 1.0)
        P.memset("pool", blk64, blk64[64:128, 64:128], 1.0)
        blk2 = P.sb(scope, "blk2", [128, 2], BF16)
        P.memset("pool", blk2, blk2[:, :], 0.0)
        P.memset("pool", blk2, blk2[0:64, 0:1], 1.0)
        P.memset("pool", blk2, blk2[64:128, 1:2], 1.0)
        QpT = [P.sb(scope, "QpT", [128, NB, 128], BF16) for _ in range(2)]
        MpT = [P.sb(scope, "MpT", [128, NB, 128], BF16) for _ in range(2)]
        Gs = [P.sb(scope, "Gs", [128, NB, 64], F32) for _ in range(2)]
        pC = [P.sb(scope, "pC", [128, NB], F32) for _ in range(2)]
        yacc = P.sb(scope, "yacc", [128, NB, 128], F32)
        bon = P.sb(scope, "bon", [128, NB, 2], F32)
        for di in range(2):
            P.memset("pool", MpT[di], MpT[di][:, :, :], 0.0)
        P.barrier()
        v4 = lambda ap: ap.rearrange("p (c t) -> p c t", t=128)
        if dbg == "setup":
            return

        for hp in range(2):
            if dbg == "s1h0" and hp == 1:
                continue
            with contextlib.ExitStack() as sw:
                f32t = lambda nm: P.sb(sw, nm, [128, CH], F32)
                b16t = lambda nm: P.sb(sw, nm, [128, CH], BF16)
                kkf, nrm, af, t_ = f32t("kkf"), f32t("nrm"), f32t("af"), f32t("t_")
                adb = P.sb(sw, "adb", [128, CH], BF16)

                def issue_loads(n_):
                    st = C.stage[n_ % 2]
                    cs_ = slice(n_ * CH, (n_ + 1) * CH)
                    P.dma("sp", st[:, 0:512], U[O_RR + hp * 128:O_RR + hp * 128 + 128, cs_], reads=[],
                          writes=[(st, "r")])
                    P.dma("sp", st[:, 512:1024], U[O_RK + hp * 128:O_RK + hp * 128 + 128, cs_], reads=[],
                          writes=[(st, "k")])
                    P.dma("sp", st[:, 1024:1536], U[O_WDF:O_WDF + 128, cs_], reads=[], writes=[(st, "wd")])
                    P.dma("sp", st[:, 1536:2048], U[O_ADF:O_ADF + 128, cs_], reads=[], writes=[(st, "ad")])

                issue_loads(0)
                ld = [f32t("ld") for _ in range(2)]
                sq, thb, prod, rb, kkb = b16t("sq"), b16t("thb"), b16t("prod"), b16t("rb"), b16t("kkb")
                kd = [b16t("kd") for _ in range(2)]
                bb = [b16t("bb") for _ in range(2)]
                A = [f32t("A%d" % i) for i in range(4)]
                BSET = [(P.sb(sw, "opkr", [128, 4, 256], BF16), b16t("Kh"), b16t("Bh"), b16t("KhC"), b16t("nBhC"))
                        for _ in range(2)]
                if len(P.psums) == 8:
                    psA_bank = P.psums[-1]
                    P.psums = P.psums[:-1]
                    P.psi = 0
                psA = psA_bank
                NQ = 8
                AR = [P.sb(sw, "AR", [128, 256], BF16) for _ in range(NQ)]
                BR = [P.sb(sw, "BR", [128, 256], BF16) for _ in range(NQ)]
                MZ = [[P.sb(sw, "MZ", [128, 384], BF16) for _ in range(2)] for _ in range(NQ)]
                KBC = [P.sb(sw, "KBC", [128, 2, 128], BF16) for _ in range(NQ)]
                for qi in range(NQ):
                    P.memset("pool", KBC[qi], KBC[qi][:, :, :], 0.0)
                def phase_a(n, PP):
                    cs = slice(n * CH, (n + 1) * CH)
                    if n + 1 < NCH:
                        issue_loads(n + 1)
                    st = C.stage[n % 2]
                    rc_ap, kc_ap, wd_ap, ad_ap = st[:, 0:512], st[:, 512:1024], st[:, 1024:1536], st[:, 1536:2048]
                    PP.copy("pool", rb[:, :], rc_ap, reads=[(st, "r")], writes=[rb])
                    PP.ts("dve", kkf[:, :], kc_ap, cols[:, hp, 4:5], None, ALU.mult, None, reads=[(st, "k"), cols],
                         writes=[kkf])
                    PP.act(sq[:, :], kkf[:, :], AF.Square, reads=[kkf], writes=[sq])
                    ps = PP.ps()
                    PP.mm(ps, ps[:, :], blk64[:, :], sq[:, :], True, True, reads=[blk64, sq])
                    PP.act(nrm[:, :], ps[:, :], AF.Sqrt, reads=[ps], writes=[nrm])
                    PP.ts("dve", nrm[:, :], nrm[:, :], 1e-12, None, ALU.max, None, reads=[nrm], writes=[nrm])
                    PP.op("dve", lambda e, o=nrm[:, :]: e.reciprocal(o, o), reads=[nrm], writes=[nrm])
                    PP.tt("dve", kkf[:, :], kkf[:, :], nrm[:, :], ALU.mult, reads=[kkf, nrm], writes=[kkf])
                    PP.copy("pool", kkb[:, :], kkf[:, :], reads=[kkf], writes=[kkb])
                    PP.act(thb[:, :], wd_ap, AF.Tanh, reads=[(st, "wd")], writes=[thb])
                    PP.copy("pool", adb[:, :], ad_ap, reads=[(st, "ad")], writes=[adb])
                    for di in range(2):
                        lp = slice(0, 64) if di == 0 else slice(64, 128)
                        ps = PP.ps()
                        PP.mm(ps, ps[:, :], lw[lp, di, hp * 128:(hp + 1) * 128], thb[lp, :], True, True,
                             reads=[lw, thb])
                        PP.act(ld[di][:, :], ps[:, :], AF.Sigmoid, reads=[ps, cols], writes=[ld[di]],
                              bias=cols[:, hp, di:di + 1])
                        PP.ts("dve", ld[di][:, :], ld[di][:, :], NEG_E, None, ALU.mult, None, reads=[ld[di]],
                             writes=[ld[di]])
                        ps2 = PP.ps()
                        PP.mm(ps2, ps2[:, :], lw[lp, 2 + di, hp * 128:(hp + 1) * 128], adb[lp, :], True, True,
                             reads=[lw, adb])
                        PP.act(af[:, :], ps2[:, :], AF.Sigmoid, reads=[ps2, cols], writes=[af],
                              bias=cols[:, hp, 2 + di:3 + di])
                        PP.ts("dve", t_[:, :], af[:, :], cols[:, hp, 5:6], cols[:, hp, 6:7], ALU.mult, ALU.add,
                             reads=[af, cols], writes=[t_])
                        PP.tt("dve", kd[di][:, :], t_[:, :], kc_ap, ALU.mult, reads=[t_, (st, "k")], writes=[kd[di]])
                        PP.tt("pool", bb[di][:, :], kkf[:, :], af[:, :], ALU.mult, reads=[kkf, af], writes=[bb[di]])
                    PP.tt("pool", t_[:, :], kd[0][:, :], kd[1][:, :], ALU.add, reads=[kd[0], kd[1]], writes=[t_])
                    PP.stt("dve", prod[:, :], t_[:, :], cols[:, hp, 7:8], rc_ap, ALU.mult, ALU.mult,
                          reads=[t_, cols, (st, "r")], writes=[prod])
                    ps = PP.ps()
                    for bq in range(4):
                        PP.mm(ps, ps[:, bq * 2:bq * 2 + 2], prod[:, bq * 128:(bq + 1) * 128], blk2[:, 0:2], True, True,
                             reads=[prod, blk2])
                    PP.copy("act", bon[:, n * 4:(n + 1) * 4, :], ps[:, 0:8].rearrange("p (b h) -> p b h", h=2),
                           reads=[ps], writes=[(bon, n)])

                def phase_b(n, di, bs, PP):
                    opkr, Kh, Bh, KhC, nBhC = BSET[bs]
                    A1, A2, A3, A4 = A
                    PP.op("dve", lambda e, o=A1[:, :], m_=smask[:, :], d_=ld[di][:, :]:
                         e.tensor_tensor_scan(o, m_, d_, 0.0, ALU.mult, ALU.add),
                         reads=[smask, ld[di]], writes=[A1])
                    for cc in range(4):
                        c = n * 4 + cc
                        sl = slice(cc * 128, (cc + 1) * 128)
                        tot = A1[:, cc * 128 + 127:cc * 128 + 128]
                        PP.ts("dve", A2[:, sl], A1[:, sl], tot, -1.0, ALU.subtract, ALU.mult, reads=[A1],
                             writes=[(A2, cc)])
                        PP.act(pC[di][:, c:c + 1], tot, AF.Exp, reads=[A1], writes=[(pC[di], c)])
                    PP.tt("pool", A3[:, :], A1[:, :], ld[di][:, :], ALU.subtract, reads=[A1, ld[di]], writes=[A3])
                    if di == 1:
                        PP.tt("pool", A4[:, :], A2[:, :], ld[di][:, :], ALU.add, reads=[A2, ld[di]], writes=[A4])
                        c1, c1x, c2, ib = A4, A2, A3, A1
                    else:
                        c1, c1x, c2, ib = A1, A3, A2, A4
                    PP.act(ib[:, :], c1[:, :], AF.Exp, reads=[c1], writes=[ib], scale=-1.0)
                    PP.act(c1[:, :], c1[:, :], AF.Exp, reads=[c1], writes=[c1])
                    PP.act(c1x[:, :], c1x[:, :], AF.Exp, reads=[c1x], writes=[c1x])
                    PP.act(c2[:, :], c2[:, :], AF.Exp, reads=[c2], writes=[c2])
                    PP.tt("pool", opkr[:, :, 0:128], v4(kkb[:, :]), v4(c1x[:, :]), ALU.mult, reads=[kkb, c1x],
                         writes=[(opkr, 0)])
                    PP.tt("pool", opkr[:, :, 128:256], v4(rb[:, :]), v4(c1[:, :]), ALU.mult, reads=[rb, c1],
                         writes=[(opkr, 1)])
                    PP.tt("pool", Kh[:, :], kd[di][:, :], ib[:, :], ALU.mult, reads=[kd[di], ib], writes=[Kh])
                    PP.tt("pool", Bh[:, :], bb[di][:, :], ib[:, :], ALU.mult, reads=[bb[di], ib], writes=[Bh])
                    PP.tt("pool", KhC[:, :], kd[di][:, :], c2[:, :], ALU.mult, reads=[kd[di], c2], writes=[KhC])
                    PP.stt("dve", nBhC[:, :], bb[di][:, :], -1.0, c2[:, :], ALU.mult, ALU.mult,
                          reads=[bb[di], c2], writes=[nBhC])
                def stage1(n, di, bs, drip):
                    opkr, Kh, Bh, KhC, nBhC = BSET[bs]
                    NX = NMS[1 - di]
                    qs = [(cc, hh) for cc in range(4) for hh in (0, 1)]
                    for qi, (cc, hh) in enumerate(qs):
                        pr = slice(hh * 64, hh * 64 + 64)
                        tc = slice(cc * 128, (cc + 1) * 128)
                        psA = P.ps()
                        psB = P.ps()
                        P.mm(psA, psA[:, 0:256], Kh[pr, tc], opkr[pr, cc, :], True, True, reads=[Kh, opkr])
                        P.mm(psA, psA[:, 256:384], opkr[pr, cc, 0:128], Bh[pr, tc], True, True,
                             reads=[opkr, Bh])
                        P.mm(psB, psB[:, 0:256], Bh[pr, tc], opkr[pr, cc, :], True, True, reads=[Bh, opkr])
                        P.tt("dve", AR[qi][:, :], psA[:, 0:256], MS[di][:, :], ALU.mult,
                             reads=[psA, MS[di]], writes=[AR[qi]])
                        P.tt("dve", MZ[qi][0][:, 0:128], psA[:, 256:384], NX[:, 0:128], ALU.mult,
                             reads=[psA, NX], writes=[(MZ[qi][0], 0)])
                        P.tt("dve", BR[qi][:, :], psB[:, 0:256], NMS[di][:, :], ALU.mult,
                             reads=[psB, NMS[di]], writes=[BR[qi]])
                        P.copy("pool", MZ[qi][0][:, 256:384], BR[qi][:, 0:128], reads=[BR[qi]],
                               writes=[(MZ[qi][0], 2)])
                        drip()
                    for qi, (cc, hh) in enumerate(qs):
                        pb = hh * 64
                        pr = slice(pb, pb + 64)
                        tc = slice(cc * 128, (cc + 1) * 128)
                        c = n * 4 + cc
                        vcol = (2 * hp + hh) * 64
                        ps = P.ps()
                        P.mm(ps, ps[:, pb:pb + 64], opkr[pr, cc, 0:128], identb[pr, pb:pb + 64], True, True,
                             reads=[opkr, identb])
                        P.mm(ps, ps[:, 64 - pb:128 - pb], AR[qi][:, 0:128], vtm[:, c, vcol:vcol + 64], True,
                             True, reads=[AR[qi], vtm])
                        P.mm(ps, ps[:, 128:192], KhC[pr, tc], identb[pr, pb:pb + 64], True, True,
                             reads=[KhC, identb])
                        P.mm(ps, ps[:, 192:256], nBhC[pr, tc], identb[pr, pb:pb + 64], True, True,
                             reads=[nBhC, identb])
                        P.copy("act", MZ[qi][0][:, 128:256], ps[:, 0:128], reads=[ps], writes=[(MZ[qi][0], 1)])
                        P.copy("act", KBC[qi][:, :, pb:pb + 64],
                               ps[:, 128:256].rearrange("p (j k) -> p j k", k=64), reads=[ps],
                               writes=[KBC[qi]])
                    for lev in range(7):
                        for qi in range(NQ):
                            cur = MZ[qi][lev % 2]
                            nxt = MZ[qi][(lev + 1) % 2]
                            ps = P.ps()
                            ev_eng = "act" if qi % 2 == 0 else "dve"
                            drip()
                            if lev < 5:
                                P.mm(ps, ps[:, 0:256], cur[:, 256:384], cur[:, 0:256], True, False, reads=[cur])
                                P.mm(ps, ps[:, 128:256], identb[:, :], cur[:, 128:256], False, True,
                                     reads=[cur, identb])
                                P.mm(ps, ps[:, 256:384], cur[:, 0:128], cur[:, 256:384], True, True, reads=[cur])
                                P.copy(ev_eng, nxt[:, :], ps[:, 0:384], reads=[ps], writes=[nxt])
                            elif lev == 5:
                                P.mm(ps, ps[:, 128:256], cur[:, 256:384], cur[:, 128:256], True, False,
                                     reads=[cur])
                                P.mm(ps, ps[:, 128:256], identb[:, :], cur[:, 128:256], False, True,
                                     reads=[cur, identb])
                                P.mm(ps, ps[:, 256:384], cur[:, 0:128], cur[:, 256:384], True, True, reads=[cur])
                                P.copy(ev_eng, nxt[:, 128:384], ps[:, 128:384], reads=[ps], writes=[nxt])
                            else:
                                P.mm(ps, ps[:, 128:256], cur[:, 256:384], cur[:, 128:256], True, False,
                                     reads=[cur])
                                P.mm(ps, ps[:, 128:256], identb[:, :], cur[:, 128:256], False, True,
                                     reads=[cur, identb])
                                P.copy(ev_eng, nxt[:, 128:256], ps[:, 128:256], reads=[ps], writes=[(nxt, 1)])
                    for qi, (cc, hh) in enumerate(qs):
                        pb = hh * 64
                        pr = slice(pb, pb + 64)
                        c = n * 4 + cc
                        vcol = (2 * hp + hh) * 64
                        ucol = 64 - pb
                        zt = MZ[qi][1]
                        zf = zt[:, 128:256]
                        zu = zt[:, 128 + ucol:128 + ucol + 64]
                        ps = P.ps()
                        P.mm(ps, ps[:, 0:128], zf, BR[qi][:, 128:256], True, True, reads=[zt, BR[qi]])
                        P.mm(ps, ps[:, 128:192], AR[qi][:, 128:256], vtm[:, c, vcol:vcol + 64], True, False,
                             reads=[AR[qi], vtm])
                        P.mm(ps, ps[:, 128:192], BR[qi][:, 128:256], zu, False, True, reads=[BR[qi], zt])
                        P.mm(ps, ps[:, 192:256], KBC[qi][:, 0, :], vtm[:, c, vcol:vcol + 64], True, False,
                             reads=[KBC[qi], vtm])
                        P.mm(ps, ps[:, 192:256], KBC[qi][:, 1, :], zu, False, True, reads=[KBC[qi], zt])
                        P.mm(ps, ps[:, 256:320], zf, KBC[qi][:, 1, pb:pb + 64], True, True,
                             reads=[zt, KBC[qi]])
                        if di == 0:
                            P.copy("act", yacc[:, c, pb:pb + 64], ps[:, 128:192], reads=[ps],
                                   writes=[(yacc, (c, hh))])
                        P.copy("act", Gs[di][pr, c, :], ps[pr, 192:256], reads=[ps],
                               writes=[(Gs[di], (c, hh))])
                        P.tt("dve", QpT[di][pr, c, :], ps[pr, 0:128], opkr[pr, cc, 128:256], ALU.add,
                             reads=[ps, opkr], writes=[(QpT[di], (c, hh))])
                        if di == 1:
                            P.tt("dve", yacc[:, c, pb:pb + 64], ps[:, 128:192], yacc[:, c, pb:pb + 64],
                                 ALU.add, reads=[ps, (yacc, (c, hh))], writes=[(yacc, (c, hh))])
                        P.copy("dve", MpT[di][pr, c, pb:pb + 64], ps[pr, 256:320], reads=[ps],
                               writes=[(MpT[di], (c, hh))])

                from collections import deque
                pend = deque()

                def drip():
                    if pend:
                        nm_, a_, k_ = pend.popleft()
                        getattr(P, nm_)(*a_, **k_)

                def flush():
                    while pend:
                        drip()

                class _Def:
                    def ps(self_):
                        return psA
                    def __getattr__(self_, nm_):
                        return lambda *a_, **k_: pend.append((nm_, a_, k_))

                DD = _Def()
                phase_a(0, P)
                phase_b(0, 0, 0, P)
                for n in range(NCH):
                    phase_b(n, 1, 1, DD)
                    if dbg != "B":
                        stage1(n, 0, 0, drip)
                    flush()
                    if n + 1 < NCH:
                        phase_a(n + 1, DD)
                        phase_b(n + 1, 0, 0, DD)
                    if dbg != "B":
                        stage1(n, 1, 1, drip)
                    flush()
                P.barrier()
            if dbg in ("A", "B", "s1", "s1a", "s1b", "s1c", "s1h0"):
                continue
            with contextlib.ExitStack() as s2:
                Hf = [P.sb(s2, "Hf", [128, 64], F32) for _ in range(2)]
                Hb = [P.sb(s2, "Hb", [128, 128], BF16) for _ in range(2)]
                for di in range(2):
                    P.memset("pool", Hf[di], Hf[di][:, :], 0.0)
                    P.memset("pool", Hb[di], Hb[di][:, :], 0.0)
                for i in range(NB):
                    for di in range(2):
                        c = i if di == 0 else NB - 1 - i
                        psY = P.ps()
                        P.mm(psY, psY[:, 0:128], QpT[di][:, c, :], Hb[di][:, :], True, True, reads=[QpT[di], Hb[di]])
                        P.tt("dve", yacc[:, c, :], psY[:, 0:128], yacc[:, c, :], ALU.add, reads=[psY, yacc],
                             writes=[yacc])
                        if i == NB - 1:
                            continue
                        psH = P.ps()
                        P.mm(psH, psH[:, 0:128], MpT[di][:, c, :], Hb[di][:, :], True, True, reads=[MpT[di], Hb[di]])
                        for hh in range(2):
                            pb = hh * 64
                            pr = slice(pb, pb + 64)
                            P.stt("dve", Hf[di][pr, :], Hf[di][pr, :], pC[di][pr, c:c + 1], psH[pr, pb:pb + 64],
                                  ALU.mult, ALU.add, reads=[Hf[di], pC[di], psH], writes=[Hf[di]])
                            P.tt("pool", Hf[di][pr, :], Hf[di][pr, :], Gs[di][pr, c, :], ALU.add,
                                 reads=[Hf[di], Gs[di]], writes=[Hf[di]])
                            P.copy("act", Hb[di][pr, pb:pb + 64], Hf[di][pr, :], reads=[Hf[di]], writes=[Hb[di]])
                P.barrier()
            if dbg == "s2":
                continue
            with contextlib.ExitStack() as s3:
                tq = P.sb(s3, "tq", [128, NB, 128], F32)
                finb = P.sb(s3, "finb", [128, NB, 128], BF16)
                mu = P.sb(s3, "mu", [128, NB, 2], F32)
                var = P.sb(s3, "var", [128, NB, 2], F32)
                sgd = P.sb(s3, "sgd", [128, T], BF16)
                gl = P.sb(s3, "gl", [128, CH], F32)
                for n in range(NCH):
                    cs = slice(n * CH, (n + 1) * CH)
                    P.dma("sp", gl[:, :], U[O_GD:O_GD + 128, cs], reads=[], writes=[gl])
                    P.act(sgd[:, cs], gl[:, :], AF.Sigmoid, reads=[gl], writes=[(sgd, n)])
                if dbg == "p1":
                    P.barrier()
                    continue
                g3 = lambda t: t[:, :, :].rearrange("p b (h v) -> p b h v", v=64)
                y3 = g3(yacc)
                q3 = g3(tq)
                bc = lambda t: t[:, :, :].unsqueeze(3).to_broadcast([128, NB, 2, 64])
                P.op("dve", lambda e, o=mu[:, :, :], i=y3: e.tensor_reduce(o, i, AX.X, ALU.add), reads=[yacc], writes=[mu])
                P.ts("dve", mu[:, :, :], mu[:, :, :], 1.0 / 64.0, None, ALU.mult, None, reads=[mu], writes=[mu])
                P.tt("dve", y3, y3, bc(mu), ALU.subtract, reads=[yacc, mu], writes=[yacc])
                P.act(tq[:, :, :], yacc[:, :, :], AF.Square, reads=[yacc], writes=[tq])
                P.op("dve", lambda e, o=var[:, :, :], i=q3: e.tensor_reduce(o, i, AX.X, ALU.add), reads=[tq], writes=[var])
                P.act(var[:, :, :], var[:, :, :], AF.Sqrt, reads=[var, gneps], writes=[var], bias=gneps[:, 0:1],
                      scale=1.0 / 64.0)
                P.op("dve", lambda e, o=var[:, :, :]: e.reciprocal(o, o), reads=[var], writes=[var])
                P.tt("dve", y3, y3, bc(var), ALU.mult, reads=[yacc, var], writes=[yacc])
                lgb = Lg[:, hp * 128:(hp + 1) * 128].unsqueeze(1).to_broadcast([128, NB, 128])
                lbb = Lb[:, hp * 128:(hp + 1) * 128].unsqueeze(1).to_broadcast([128, NB, 128])
                P.tt("dve", yacc[:, :, :], yacc[:, :, :], lgb, ALU.mult, reads=[yacc, Lg], writes=[yacc])
                P.tt("dve", yacc[:, :, :], yacc[:, :, :], lbb, ALU.add, reads=[yacc, Lb], writes=[yacc])
                vv = vtm[:, :, hp * 128:(hp + 1) * 128].rearrange("p b (h v) -> p b h v", v=64)
                bonb = bon[:, :, :].unsqueeze(3).to_broadcast([128, NB, 2, 64])
                P.tt("dve", q3, vv, bonb, ALU.mult, reads=[vtm, bon], writes=[tq])
                P.tt("dve", yacc[:, :, :], yacc[:, :, :], tq[:, :, :], ALU.add, reads=[yacc, tq], writes=[yacc])
                if dbg == "p2":
                    P.barrier()
                    continue
                P.barrier()
                for b4 in range(NB // 4):
                    ps = P.ps()
                    for bq in range(4):
                        b = b4 * 4 + bq
                        P.mm(ps, ps[:, bq * 128:(bq + 1) * 128], sgd[:, b * 128:(b + 1) * 128],
                             g2b[:, hp * 128:(hp + 1) * 128], True, True, reads=[sgd, g2b])
                    P.tt("dve", finb[:, b4 * 4:(b4 + 1) * 4, :], yacc[:, b4 * 4:(b4 + 1) * 4, :],
                         ps[:, :].rearrange("p (b c) -> p b c", c=128), ALU.mult, reads=[yacc, ps],
                         writes=[(finb, b4)])
                    ps2 = P.ps()
                    for bq in range(4):
                        b = b4 * 4 + bq
                        P.mm(ps2, ps2[:, bq * 128:(bq + 1) * 128], finb[:, b, :], identb[:, :], True, True,
                             reads=[(finb, b4), identb])
                    P.copy("act", yT[:, hp, b4 * 512:(b4 + 1) * 512], ps2[:, :], reads=[ps2], writes=[(yT, (hp, b4))])
                P.barrier()


def outproj_part(P, C, xT, yT, w_out, m):
    with contextlib.ExitStack() as scope:
        wo = P.sb(scope, "wo", [128, 2, D], BF16)
        w3 = w_out.rearrange("(k p) c -> p k c", p=128)
        load_weight(P, C, w3[:, 2 * m:2 * m + 2, :], 2, D, wo[:, :, :], wo)
        for j in range(8):
            for n in range(NCH):
                cs = slice(n * CH, (n + 1) * CH)
                ps = P.ps()
                for k in range(2):
                    P.mm(ps, ps[:, :], wo[:, k, j * 128:(j + 1) * 128], yT[:, k, cs], k == 0, k == 1,
                         reads=[wo, yT])
                P.tt("dve", xT[:, j, cs], ps[:, :], xT[:, j, cs], ALU.add, reads=[ps, (xT, (j, n))],
                     writes=[(xT, (j, n))])
        P.barrier()


def outproj_load(P, C, scope, w_out, ms):
    wo = P.sb(scope, "wo", [128, 2 * len(ms), D], BF16)
    w3 = w_out.rearrange("(k p) c -> p k c", p=128)
    for slot, m in enumerate(ms):
        load_weight(P, C, w3[:, 2 * m:2 * m + 2, :], 2, D, wo[:, 2 * slot:2 * slot + 2, :], wo, dst_key=slot)
    return wo


def outproj_multi(P, C, xT, yTn, wo, ms):
    nk = 2 * len(ms)
    if True:
        for n in range(NCH):
            cs = slice(n * CH, (n + 1) * CH)
            for j in range(8):
                ps = P.ps()
                for k in range(nk):
                    P.mm(ps, ps[:, :], wo[:, k, j * 128:(j + 1) * 128], yTn[:, k, cs], k == 0, k == nk - 1,
                         reads=[wo, yTn])
                P.tt("dve", xT[:, j, cs], ps[:, :], xT[:, j, cs], ALU.add, reads=[ps, (xT, (j, n))],
                     writes=[(xT, (j, n))])
        P.barrier()


def phase_mix(P, C, xT, l, dr, U, VT, debug_y=False, mixers=("conv", "rwkv", "ret", "mla")):
    g = TT(C.gains["mix_norm"].h[:, l, :], "gm")
    g.whole = C.gains["mix_norm"].whole
    phase_inproj(P, C, xT, g, dr["w_in"][l], dr["w_rot"][l], U, VT)
    if debug_y:
        for k in range(8):
            P.memset("pool", xT, xT[:, k, :], 0.0)
        P.barrier()
    MIDX = {"conv": 0, "rwkv": 1, "ret": 2, "mla": 3}
    if "rwkv" in mixers:
        with contextlib.ExitStack() as scope:
            yT = P.sb(scope, "yT", [128, 2, T], BF16)
            mixer_rwkv(P, C, l, dr, U, VT, yT)
            if debug_y:
                for k in range(2):
                    P.copy("dve", xT[:, 2 + k, :], yT[:, k, :], reads=[yT], writes=[xT])
                P.barrier()
            else:
                outproj_part(P, C, xT, yT, dr["w_out"][l], 1)
    rest = [nm for nm in ("conv", "ret", "mla") if nm in mixers]
    if rest:
        with contextlib.ExitStack() as scope:
            yTn = P.sb(scope, "yTn", [128, 2 * len(rest), T], BF16)
            wo_all = None if debug_y else outproj_load(P, C, scope, dr["w_out"][l], [MIDX[nm] for nm in rest])
            for slot, name in enumerate(rest):
                view = TT(yTn.h[:, 2 * slot:2 * slot + 2, :], "yv_" + name)
                if name == "conv":
                    mixer_conv(P, C, l, dr, U, view)
                elif name == "ret":
                    mixer_ret(P, C, l, dr, U, VT, view)
                else:
                    mixer_mla(P, C, l, dr, U, view)
            if debug_y:
                for slot, name in enumerate(rest):
                    for k in range(2):
                        P.copy("dve", xT[:, 2 * MIDX[name] + k, :], yTn[:, 2 * slot + k, :], reads=[yTn], writes=[xT])
                P.barrier()
            else:
                outproj_multi(P, C, xT, yTn, wo_all, [MIDX[nm] for nm in rest])


W_NAMES = ["ffn1_norm", "ffn1_w_gate", "ffn1_w_up", "ffn1_w_down", "mix_norm", "w_in", "w_out", "conv_w",
           "rwkv_w0_f", "rwkv_w0_b", "rwkv_w2_f", "rwkv_w2_b", "rwkv_a0_f", "rwkv_a0_b", "rwkv_a2_f", "rwkv_a2_b",
           "rwkv_g2", "rwkv_k_k", "rwkv_k_a", "rwkv_r_k", "rwkv_lnx_g", "rwkv_lnx_b", "ret_gn_g", "mla_q_a_norm",
           "mla_q_b", "mla_kv_a_norm", "mla_kv_b", "ffn2_norm", "ffn2_w_gate", "ffn2_w_up", "ffn2_w_down",
           "final_norm"]


def build(shapes, stop_after=None, layers=(0, 1), final_norm=True):
    nc = bass.Bass("TRN2", target_bir_lowering=False)
    dr = {}
    for name, (shape, dt) in shapes.items():
        dr[name] = nc.dram_tensor(name, list(shape), dt, kind="ExternalInput").ap()
    out_d = nc.dram_tensor("out", [T, D], F32, kind="ExternalOutput").ap()
    U = nc.dram_tensor("u_scr", [UROWS, T], F32, kind="Internal").ap()
    VT = nc.dram_tensor("vt_scr", [T, 512], F32, kind="Internal").ap()
    ROPE = nc.dram_tensor("rope_scr", [4, 128, T], F32, kind="Internal").ap()
    with contextlib.ExitStack() as es:
        P = Prog(nc, es)
        C = Ctx()
        P.acc = []
        for i in range(8):
            h = es.enter_context(nc.psum_tensor("ps%d" % i, [128, 512], F32))
            t_ps = TT(h, "ps%d" % i)
            t_ps.psum = True
            (P.psums if i < 6 else P.acc).append(t_ps)
        xT = P.sb(es, "xT", [128, 8, T], F32)
        C.stage = [P.sb(es, "stage", [128, 2048], F32) for _ in range(2)]
        C.stage_i = 0
        C.ident = P.sb(es, "ident", [128, 128], F32)
        C.ones_f = P.sb(es, "ones_f", [128, 128], F32)
        C.ones_mean = P.sb(es, "ones_mean", [128, 128], BF16)
        C.eps_col = P.sb(es, "eps_col", [128, 1], F32)
        P.memset("pool", C.eps_col, C.eps_col[:, :], EPS)
        P.memset("pool", C.ones_f, C.ones_f[:, :], 1.0)
        P.memset("pool", C.ones_mean, C.ones_mean[:, :], 1.0 / D)
        P.op("pool", lambda e: e.affine_select(C.ident[:, :], C.ones_f[:, :], [[1, 128]], ALU.is_equal, 0.0,
                                               base=0, channel_multiplier=-1),
             reads=[C.ones_f], writes=[C.ident])
        C.ones_bf = P.sb(es, "ones_bf", [128, 128], BF16)
        C.ones64 = P.sb(es, "ones64", [128, 128], BF16)
        P.memset("pool", C.ones_bf, C.ones_bf[:, :], 1.0)
        P.memset("pool", C.ones64, C.ones64[:, :], 1.0 / 64.0)
        C.consts = P.sb(es, "consts", [128, 4], F32)
        with contextlib.ExitStack() as tmp:
            row = P.sb(tmp, "crow", [1, 4, 128], F32)
            one = P.sb(tmp, "cone", [1, 1], F32)
            P.memset("pool", one, one[:, :], 1.0)
            for i in range(32):
                P.memset("pool", row, row[0:1, 0, :].rearrange("o (r i) -> o r i", i=32)[:, :, i:i + 1],
                         10000.0 ** (-i / 32.0))
            for i in range(16):
                P.memset("pool", row, row[0:1, 2, :].rearrange("o (r i) -> o r i", i=16)[:, :, i:i + 1],
                         10000.0 ** (-i / 16.0))
            P.memset("pool", row, row[0:1, 1, :].rearrange("o (r i) -> o r i", i=64)[:, :, 0:32], -1.0)
            P.memset("pool", row, row[0:1, 1, :].rearrange("o (r i) -> o r i", i=64)[:, :, 32:64], 1.0)
            P.memset("pool", row, row[0:1, 3, :].rearrange("o (r i) -> o r i", i=32)[:, :, 0:16], -1.0)
            P.memset("pool", row, row[0:1, 3, :].rearrange("o (r i) -> o r i", i=32)[:, :, 16:32], 1.0)
            ps = P.ps()
            for c in range(4):
                P.mm(ps, ps[:, c:c + 1], row[0:1, c, :], one[0:1, 0:1], True, True, reads=[row, one])
            P.copy("dve", C.consts[:, :], ps[:, 0:4], reads=[ps], writes=[C.consts])
            P.barrier()
        C.ROPE = ROPE
        for idx, (ci, si_) in enumerate(((0, 1), (2, 3))):
            with contextlib.ExitStack() as tmp:
                cos_, sin_ = rope_tables(P, C, tmp, dr["positions"], C.consts[:, ci:ci + 1], C.consts[:, si_:si_ + 1])
                P.dma("sp", ROPE[2 * idx], cos_[:, :], reads=[cos_], writes=[])
                P.dma("sp", ROPE[2 * idx + 1], sin_[:, :], reads=[sin_], writes=[])
                P.barrier()
        C.gains = {}
        for nm in ("ffn1_norm", "mix_norm", "ffn2_norm"):
            g = P.sb(es, "g_" + nm, [128, DEPTH, 8], F32)
            for l in range(DEPTH):
                P.dma("sp", g[:, l, :], dr[nm][l].rearrange("(k p) -> p k", p=128), reads=[], writes=[g], slow=True)
            C.gains[nm] = g
        gfin = P.sb(es, "g_final", [128, 8], F32)
        P.dma("sp", gfin[:, :], dr["final_norm"].rearrange("(k p) -> p k", p=128), reads=[], writes=[gfin], slow=True)

        load_x(P, C, xT, dr["x"])
        done = False
        for l in layers:
            g1 = TT(C.gains["ffn1_norm"].h[:, l, :], "g1")
            g1.whole = C.gains["ffn1_norm"].whole
            phase_ffn(P, C, xT, g1, dr["ffn1_w_gate"][l], dr["ffn1_w_up"][l], dr["ffn1_w_down"][l])
            if stop_after == ("ffn1", l):
                done = True
                break
            if stop_after is not None and stop_after[0].startswith("y") and stop_after[1] == l:
                phase_mix(P, C, xT, l, dr, U, VT, debug_y=True, mixers=stop_after[0].split("_")[1:])
                done = True
                break
            if stop_after is not None and stop_after[0].startswith("m_") and stop_after[1] == l:
                phase_mix(P, C, xT, l, dr, U, VT, mixers=stop_after[0].split("_")[1:])
                done = True
                break
            phase_mix(P, C, xT, l, dr, U, VT)
            if stop_after == ("mix", l):
                done = True
                break
            g2 = TT(C.gains["ffn2_norm"].h[:, l, :], "g2")
            g2.whole = C.gains["ffn2_norm"].whole
            phase_ffn(P, C, xT, g2, dr["ffn2_w_gate"][l], dr["ffn2_w_up"][l], dr["ffn2_w_down"][l])
            if stop_after == ("ffn2", l):
                done = True
                break
        store_out(P, C, xT, gfin, out_d, do_norm=(final_norm and not done))
        P.emit()
    return nc


EXTRA = ["w_rot", "q_b_rot"]


def _host_layouts(inputs):
    w_in = inputs["w_in"]
    def rot_cols(w, c0, nheads, hd):
        half = hd // 2
        cols = []
        for h in range(nheads):
            base = c0 + h * hd
            cols += list(range(base + half, base + hd)) + list(range(base, base + half))
        return w[..., cols]
    w_rot = np.concatenate([rot_cols(w_in, O_TQ, 4, 64), rot_cols(w_in, O_TK, 4, 64), rot_cols(w_in, O_MKR, 1, 32)],
                           axis=-1)
    qb = inputs["mla_q_b"]
    cols = []
    for h in range(4):
        base = h * 96
        cols += list(range(base, base + 64)) + list(range(base + 80, base + 96)) + list(range(base + 64, base + 80))
    q_b_rot = qb[..., cols]
    return {"w_rot": np.ascontiguousarray(w_rot), "q_b_rot": np.ascontiguousarray(q_b_rot)}


def _shapes(inputs, extra):
    shapes = {"x": ((T, D), F32), "positions": ((1, T), I32)}
    for n in W_NAMES:
        shapes[n] = (inputs[n].shape, F32)
    for n in EXTRA:
        shapes[n] = (extra[n].shape, F32)
    return shapes


N_LAUNCH = 1


def kernel(_stop_after=None, **inputs):
    ncores = 8 if DEBUG_CORES is None else DEBUG_CORES
    extra = _host_layouts(inputs)
    shapes = _shapes(inputs, extra)
    if _stop_after is not None or N_LAUNCH == 1:
        plans = [((0, 1) if DEBUG_LAYERS is None else DEBUG_LAYERS, True)]
    else:
        plans = [((0,), False), ((1,), True)]
    xs = [np.ascontiguousarray(inputs["x"][c]) for c in range(ncores)]
    for layers, fin in plans:
        nc = build(shapes, stop_after=_stop_after, layers=layers, final_norm=fin)
        in_maps = []
        for c in range(ncores):
            m = {"x": xs[c],
                 "positions": np.ascontiguousarray(inputs["positions"][c:c + 1]).astype(np.int32)}
            for n in W_NAMES:
                m[n] = np.ascontiguousarray(inputs[n])
            for n in EXTRA:
                m[n] = extra[n]
            in_maps.append(m)
        if DEBUG_TRACE:
            res = run_bass_kernel_spmd(nc, in_maps, core_ids=list(range(ncores)), trace=True)
            print("EXEC_NS", res.exec_time_ns)
        else:
            res = run_bass_kernel_spmd(nc, in_maps, core_ids=list(range(ncores)))
        xs = [np.ascontiguousarray(np.asarray(r["out"])) for r in res.results]
    out = np.stack(xs + [np.zeros((T, D), np.float32)] * (8 - ncores), axis=0)
    return out.astype(np.float32)
```

```python
import contextlib
import numpy as np
import concourse.bass as bass
import concourse.mybir as mybir
from concourse.bass_utils import run_bass_kernel_spmd

F32 = mybir.dt.float32
BF16 = mybir.dt.bfloat16
I32 = mybir.dt.int32
AF = mybir.ActivationFunctionType
ALU = mybir.AluOpType

D = 1024
T = 2048
DFF = 2816
DEPTH = 2
INW = 3488
NCH = 4
CH = 512
EPS = 1e-6

COMPUTE = ("pe", "act", "dve", "pool")
ENGS = ("sp", "pe", "act", "dve", "pool")
NDS = 24
EPOCH = 2000
NEPOCH = {"pe": 16, "act": 6, "dve": 8, "pool": 3}


class Buf:
    __slots__ = ("w", "r")

    def __init__(self):
        self.w = None
        self.r = {}


class TT:
    def __init__(self, h, name):
        self.h = h
        self.name = name
        self.whole = Buf()
        self.parts = {}

    def __getitem__(self, idx):
        return self.h[idx]

    def part(self, k):
        p = self.parts.get(k)
        if p is None:
            p = self.parts[k] = Buf()
        return p


def _split(a):
    if isinstance(a, tuple):
        return a[0], a[1]
    return a, None


class Prog:
    def __init__(self, nc, es):
        self.nc = nc
        self.es = es
        self.streams = {e: [] for e in ENGS}
        self.cnt = {e: 0 for e in COMPUTE}
        self.sem = {}
        for e in COMPUTE:
            for ep in range(NEPOCH[e]):
                self.sem[("c", e, ep)] = es.enter_context(nc.semaphore("s_%s%d" % (e, ep)))
        for i in range(NDS):
            self.sem[("d", i)] = es.enter_context(nc.semaphore("d%d" % i))
        self.dcnt = [0] * NDS
        self.dnext = 0
        self.seen = {e: {} for e in ENGS}
        self.psums = []
        self.psi = 0
        self.uid = 0

    def sb(self, scope, name, shape, dt):
        self.uid += 1
        h = scope.enter_context(self.nc.sbuf_tensor("%s_%d" % (name, self.uid), list(shape), dt))
        return TT(h, name)

    def ps(self):
        t = self.psums[self.psi]
        self.psi = (self.psi + 1) % len(self.psums)
        return t

    def _deps(self, reads, writes):
        deps = {}

        def add(ev):
            if ev is None:
                return
            k, v = ev
            if deps.get(k, 0) < v:
                deps[k] = v

        for a in reads:
            t, key = _split(a)
            add(t.whole.w)
            if key is None:
                for p in t.parts.values():
                    add(p.w)
            else:
                add(t.part(key).w)
        for a in writes:
            t, key = _split(a)
            add(t.whole.w)
            for kv in t.whole.r.items():
                add(kv)
            if key is None:
                for p in t.parts.values():
                    add(p.w)
                    for kv in p.r.items():
                        add(kv)
            else:
                p = t.part(key)
                add(p.w)
                for kv in p.r.items():
                    add(kv)
        return deps

    def _commit(self, reads, writes, ev):
        k, v = ev
        for a in reads:
            t, key = _split(a)
            b = t.whole if key is None else t.part(key)
            if b.r.get(k, 0) < v:
                b.r[k] = v
        for a in writes:
            t, key = _split(a)
            if key is None:
                t.whole.w = ev
                t.whole.r = {}
                t.parts = {}
            else:
                p = t.part(key)
                p.w = ev
                p.r = {}

    def _waits(self, eng, deps):
        waits = []
        for k, v in deps.items():
            if eng == "pe" and k[0] == "c" and k[1] == "pe":
                continue
            if self.seen[eng].get(k, 0) >= v:
                continue
            self.seen[eng][k] = v
            waits.append((k, v))
        return waits

    def op(self, eng, fn, reads=(), writes=()):
        pr_ = [a for a in reads if getattr(_split(a)[0], "psum", False)]
        if pr_:
            reads = [a for a in reads if not getattr(_split(a)[0], "psum", False)]
            writes = list(writes) + pr_
        deps = self._deps(reads, writes)
        waits = self._waits(eng, deps)
        j = self.cnt[eng]
        self.cnt[eng] += 1
        ev = (("c", eng, j // EPOCH), j % EPOCH + 1)
        self.streams[eng].append((waits, fn, ev[0], 1))
        self._commit(reads, writes, ev)

    def dma(self, q, out, in_, reads=(), writes=(), slow=False):
        deps = self._deps(reads, writes)
        i = self.dnext
        self.dnext = (i + 1) % NDS
        if self.dcnt[i] > 0:
            k = ("d", i)
            if deps.get(k, 0) < self.dcnt[i]:
                deps[k] = self.dcnt[i]
        waits = self._waits(q, deps)
        self.dcnt[i] += 16
        ev = (("d", i), self.dcnt[i])
        if slow:
            fn = lambda e, o=out, s=in_: e.dma_start(out=o, in_=s, allow_slow_non_contiguous=True)
        else:
            fn = lambda e, o=out, s=in_: e.dma_start(out=o, in_=s)
        self.streams[q].append((waits, fn, ev[0], 16))
        self._commit(reads, writes, ev)

    def barrier(self):
        tot = {}
        for e in COMPUTE:
            if self.cnt[e] > 0:
                j = self.cnt[e] - 1
                tot[("c", e, j // EPOCH)] = j % EPOCH + 1
        for i in range(NDS):
            if self.dcnt[i] > 0:
                tot[("d", i)] = self.dcnt[i]
        for e in ENGS:
            waits = []
            for k, v in tot.items():
                if k[0] == "c" and k[1] == e and e == "pe":
                    continue
                if self.seen[e].get(k, 0) >= v:
                    continue
                self.seen[e][k] = v
                waits.append((k, v))
            if waits:
                self.streams[e].append((waits, None, None, 0))

    def simulate(self):
        sem = {k: 0 for k in self.sem}
        pc = {e: 0 for e in ENGS}
        progress = True
        while progress:
            progress = False
            for e in ENGS:
                st = self.streams[e]
                while pc[e] < len(st):
                    waits, fn, semk, inc = st[pc[e]]
                    if any(sem[k] < v for k, v in waits):
                        break
                    if fn is not None:
                        sem[semk] += inc
                    pc[e] += 1
                    progress = True
        bad = {e: (pc[e], len(self.streams[e])) for e in ENGS if pc[e] < len(self.streams[e])}
        if bad:
            print("DEADLOCK", bad)
            for e in bad:
                waits = self.streams[e][pc[e]][0]
                print(e, [(k, v, sem[k]) for k, v in waits if sem[k] < v])
        else:
            print("SIM OK")
        return not bad

    def emit(self):
        nc = self.nc
        import os
        if os.environ.get("SIMCHECK"):
            self.simulate()
        print("COUNTS", self.cnt, max(self.dcnt), {e: len(v) for e, v in self.streams.items()})

        def run(stream):
            def f(e):
                for waits, fn, semk, inc in stream:
                    for k, v in waits:
                        e.wait_ge(self.sem[k], v)
                    if fn is not None:
                        fn(e).then_inc(self.sem[semk], inc)
            return f

        with nc.Block() as block:
            block.sync(run(self.streams["sp"]))
            block.tensor(run(self.streams["pe"]))
            block.scalar(run(self.streams["act"]))
            block.vector(run(self.streams["dve"]))
            block.gpsimd(run(self.streams["pool"]))

    def mm(self, ps, out, lhsT, rhs, start, stop, reads, writes=None):
        self.op("pe", lambda e, o=out, l=lhsT, r=rhs, s=start, p=stop: e.matmul(o, l, r, start=s, stop=p),
                reads=reads, writes=[ps] if writes is None else writes)

    def transpose(self, ps, out, in_, ident, reads):
        self.op("pe", lambda e, o=out, i=in_, d=ident: e.transpose(o, i, d), reads=reads, writes=[ps])

    def act(self, out, in_, func, reads, writes, bias=None, scale=None, eng="act"):
        kw = {}
        if bias is not None:
            kw["bias"] = bias
        if scale is not None:
            kw["scale"] = scale
        self.op(eng, lambda e, o=out, i=in_, f=func, kw=kw: e.activation(o, i, f, **kw), reads=reads, writes=writes)

    def tt(self, eng, out, in0, in1, op, reads, writes):
        self.op(eng, lambda e, o=out, a=in0, b=in1, p=op: e.tensor_tensor(o, a, b, p), reads=reads, writes=writes)

    def ts(self, eng, out, in0, s1, s2, op0, op1, reads, writes):
        if op1 is None:
            self.op(eng, lambda e, o=out, a=in0, s=s1, p=op0: e.tensor_scalar(o, a, s, None, p),
                    reads=reads, writes=writes)
        else:
            self.op(eng, lambda e, o=out, a=in0, s=s1, s_2=s2, p=op0, q=op1: e.tensor_scalar(o, a, s, s_2, p, q),
                    reads=reads, writes=writes)

    def stt(self, eng, out, in0, scalar, in1, op0, op1, reads, writes):
        self.op(eng, lambda e, o=out, a=in0, s=scalar, b=in1, p=op0, q=op1: e.scalar_tensor_tensor(o, a, s, b, p, q),
                reads=reads, writes=writes)

    def copy(self, eng, out, in_, reads, writes):
        if eng == "act":
            self.op(eng, lambda e, o=out, i=in_: e.copy(o, i), reads=reads, writes=writes)
        else:
            self.op(eng, lambda e, o=out, i=in_: e.tensor_copy(o, i), reads=reads, writes=writes)

    def memset(self, eng, t, ap, val):
        self.op(eng, lambda e, a=ap, v=val: e.memset(a, v), reads=[], writes=[t])


class Ctx:
    pass


def load_weight(P, C, dram_ap3, nk, ncols, dst, dst_tt, dst_key=None, q="sp", cast_eng="pool"):
    per = max(1, 2048 // ncols)
    k = 0
    while k < nk:
        kk = min(per, nk - k)
        st = C.stage[C.stage_i]
        C.stage_i = (C.stage_i + 1) % len(C.stage)
        sview = st[:, 0:kk * ncols].rearrange("p (k c) -> p k c", c=ncols)
        P.dma(q, sview, dram_ap3[:, k:k + kk, :], reads=[], writes=[st])
        P.copy(cast_eng, dst[:, k:k + kk, :], sview, reads=[st],
               writes=[(dst_tt, dst_key) if dst_key is not None else dst_tt])
        k += kk


def rmsnorm_to_bf16(P, C, xT, gcol, hT, scope):
    sq = [P.sb(scope, "sq", [128, CH], BF16) for _ in range(3)]
    rstd = [P.sb(scope, "rstd", [128, CH], F32) for _ in range(2)]
    xgs = [P.sb(scope, "xg", [128, CH], F32) for _ in range(2)]
    si = 0
    for n in range(NCH):
        cs = slice(n * CH, (n + 1) * CH)
        ps = P.ps()
        for k in range(8):
            s = sq[si % 3]
            si += 1
            P.act(s[:, :], xT[:, k, cs], AF.Square, reads=[(xT, (k, n))], writes=[s])
            P.mm(ps, ps[:, :], C.ones_mean[:, :], s[:, :], k == 0, k == 7, reads=[s, C.ones_mean])
        r = rstd[n % 2]
        P.act(r[:, :], ps[:, :], AF.Sqrt, reads=[ps, C.eps_col], writes=[r], bias=C.eps_col[:, 0:1])
        P.op("dve", lambda e, o=r[:, :]: e.reciprocal(o, o), reads=[r], writes=[r])
        for k in range(8):
            if k % 3 != 2:
                P.stt("dve", hT[:, k, cs], xT[:, k, cs], gcol[:, k:k + 1], r[:, :], ALU.mult, ALU.mult,
                      reads=[(xT, (k, n)), r, gcol], writes=[(hT, (k, n))])
            else:
                xg = xgs[(n * 8 + k) % 2]
                P.act(xg[:, :], xT[:, k, cs], AF.Copy, reads=[(xT, (k, n)), gcol], writes=[xg],
                      scale=gcol[:, k:k + 1])
                P.tt("pool", hT[:, k, cs], xg[:, :], r[:, :], ALU.mult, reads=[xg, r], writes=[(hT, (k, n))])


def phase_ffn(P, C, xT, gcol, wg, wu, wd):
    with contextlib.ExitStack() as scope:
        hT = P.sb(scope, "hT", [128, 8, T], BF16)
        rmsnorm_to_bf16(P, C, xT, gcol, hT, scope)
        actT = P.sb(scope, "actT", [128, 4, T], BF16)
        wgb = [P.sb(scope, "wgb", [128, 8, 512], BF16) for _ in range(2)]
        wub = [P.sb(scope, "wub", [128, 8, 512], BF16) for _ in range(2)]
        wdb = [P.sb(scope, "wdb", [128, 4, D], BF16) for _ in range(2)]
        sg = [P.sb(scope, "sg", [128, CH], F32) for _ in range(3)]
        wg3 = wg.rearrange("(k p) c -> p k c", p=128)
        wu3 = wu.rearrange("(k p) c -> p k c", p=128)
        wd3 = wd.rearrange("(k p) c -> p k c", p=128)
        blocks = []
        f0 = 0
        while f0 < DFF:
            fw = min(512, DFF - f0)
            blocks.append((f0, fw))
            f0 += fw

        def load(i):
            f0, fw = blocks[i]
            par = i % 2
            load_weight(P, C, wg3[:, :, f0:f0 + fw], 8, fw, wgb[par][:, :, 0:fw], wgb[par])
            load_weight(P, C, wu3[:, :, f0:f0 + fw], 8, fw, wub[par][:, :, 0:fw], wub[par])
            nm = fw // 128
            load_weight(P, C, wd3[:, f0 // 128:f0 // 128 + nm, :], nm, D, wdb[par][:, 0:nm, :], wdb[par])

        load(0)
        sgi = 0
        for i, (f0, fw) in enumerate(blocks):
            if i + 1 < len(blocks):
                load(i + 1)
            par = i % 2
            nm = fw // 128
            for n in range(NCH):
                for m in range(nm):
                    cs = slice(n * CH, (n + 1) * CH)
                    psg = P.ps()
                    psu = P.ps()
                    for k in range(8):
                        P.mm(psg, psg[:, :], wgb[par][:, k, m * 128:(m + 1) * 128], hT[:, k, cs], k == 0, k == 7,
                             reads=[wgb[par], (hT, (k, n))])
                    for k in range(8):
                        P.mm(psu, psu[:, :], wub[par][:, k, m * 128:(m + 1) * 128], hT[:, k, cs], k == 0, k == 7,
                             reads=[wub[par], (hT, (k, n))])
                    s = sg[sgi % 3]
                    sgi += 1
                    P.act(s[:, :], psg[:, :], AF.Silu, reads=[psg], writes=[s])
                    P.tt("dve", actT[:, m, cs], s[:, :], psu[:, :], ALU.mult, reads=[s, psu], writes=[(actT, (m, n))])
            for n in range(NCH):
                for j in range(8):
                    cs = slice(n * CH, (n + 1) * CH)
                    pso = P.ps()
                    for kk in range(nm):
                        P.mm(pso, pso[:, :], wdb[par][:, kk, j * 128:(j + 1) * 128], actT[:, kk, cs], kk == 0,
                             kk == nm - 1, reads=[wdb[par], (actT, (kk, n))])
                    P.stt("dve", xT[:, j, cs], pso[:, :], 0.5, xT[:, j, cs], ALU.mult, ALU.add,
                          reads=[pso, (xT, (j, n))], writes=[(xT, (j, n))])
        P.barrier()


def load_x(P, C, xT, x_d):
    with contextlib.ExitStack() as scope:
        xin = [P.sb(scope, "xin", [128, D], F32) for _ in range(4)]
        for b in range(T // 128):
            xi = xin[b % 4]
            P.dma("sp", xi[:, :], x_d[b * 128:(b + 1) * 128, :], reads=[], writes=[xi])
            for g in range(2):
                ps = P.ps()
                for kk in range(4):
                    k = g * 4 + kk
                    P.transpose(ps, ps[:, kk * 128:(kk + 1) * 128], xi[:, k * 128:(k + 1) * 128], C.ident[:, :],
                                reads=[xi, C.ident])
                eng = "dve" if g == 0 else "act"
                P.copy(eng, xT[:, g * 4:(g + 1) * 4, b * 128:(b + 1) * 128],
                       ps[:, :].rearrange("p (k c) -> p k c", c=128), reads=[ps],
                       writes=[(xT, (g * 4 + kk, b // 4)) for kk in range(4)])
        P.barrier()


def store_out(P, C, xT, gcol, out_d, do_norm=True):
    with contextlib.ExitStack() as scope:
        oT = [P.sb(scope, "oT", [128, 8, CH], F32) for _ in range(2)]
        otm = [P.sb(scope, "otm", [128, D], F32) for _ in range(4)]
        sq = [P.sb(scope, "sq", [128, CH], BF16) for _ in range(3)]
        rstd = [P.sb(scope, "rstd", [128, CH], F32) for _ in range(2)]
        si = 0
        for n in range(NCH):
            cs = slice(n * CH, (n + 1) * CH)
            o = oT[n % 2]
            if do_norm:
                ps = P.ps()
                for k in range(8):
                    s = sq[si % 3]
                    si += 1
                    P.act(s[:, :], xT[:, k, cs], AF.Square, reads=[(xT, (k, n))], writes=[s])
                    P.mm(ps, ps[:, :], C.ones_mean[:, :], s[:, :], k == 0, k == 7, reads=[s, C.ones_mean])
                r = rstd[n % 2]
                P.act(r[:, :], ps[:, :], AF.Sqrt, reads=[ps, C.eps_col], writes=[r], bias=C.eps_col[:, 0:1])
                P.op("dve", lambda e, o=r[:, :]: e.reciprocal(o, o), reads=[r], writes=[r])
                for k in range(8):
                    P.stt("dve", o[:, k, :], xT[:, k, cs], gcol[:, k:k + 1], r[:, :], ALU.mult, ALU.mult,
                          reads=[(xT, (k, n)), r, gcol], writes=[(o, k)])
            else:
                for k in range(8):
                    P.copy("dve", o[:, k, :], xT[:, k, cs], reads=[(xT, (k, n))], writes=[(o, k)])
            for bb in range(4):
                b = n * 4 + bb
                ot = otm[b % 4]
                for g in range(2):
                    ps = P.ps()
                    for kk in range(4):
                        k = g * 4 + kk
                        P.transpose(ps, ps[:, kk * 128:(kk + 1) * 128], o[:, k, bb * 128:(bb + 1) * 128],
                                    C.ident[:, :], reads=[(o, k), C.ident])
                    eng = "dve" if g == 0 else "act"
                    P.copy(eng, ot[:, g * 512:(g + 1) * 512], ps[:, :], reads=[ps], writes=[(ot, g)])
                P.dma("sp", out_d[b * 128:(b + 1) * 128, :], ot[:, :], reads=[ot], writes=[])
        P.barrier()


import math
C1_2PI = 6.28125
C2_2PI = 2.0 * math.pi - 6.28125
O_CX, O_CB, O_CC = 0, 256, 512
O_RR, O_RK, O_RV = 768, 1024, 1280
O_WDF, O_WDB, O_ADF, O_ADB, O_GD = 1536, 1600, 1664, 1728, 1792
O_TQ, O_TK, O_TV, O_TG = 1920, 2176, 2432, 2688
O_MQA, O_MKV, O_MKR = 2944, 3328, 3456
O_ROT = INW
NROT = 544
UROWS = INW + NROT


def load_rows(P, dst_tt, dst_ap, U, r0, nrows, q="sp"):
    P.dma(q, dst_ap, U[r0:r0 + nrows, :], reads=[], writes=[dst_tt])


def phase_inproj(P, C, xT, gcol, w_in, w_rot, U, VT):
    with contextlib.ExitStack() as scope:
        hT = P.sb(scope, "hT", [128, 8, T], BF16)
        rmsnorm_to_bf16(P, C, xT, gcol, hT, scope)
        wb = [P.sb(scope, "wb", [128, 8, 512], BF16) for _ in range(2)]
        ev = [P.sb(scope, "ev", [128, CH], F32) for _ in range(4)]
        evi = 0
        jobs = []
        for (w, ncols, row0) in ((w_in, INW, 0), (w_rot, NROT, O_ROT)):
            w3 = w.rearrange("(k p) c -> p k c", p=128)
            c0 = 0
            while c0 < ncols:
                cw = min(512, ncols - c0)
                jobs.append((w3, c0, cw, row0))
                c0 += cw

        def load(i):
            w3, c0, cw, row0 = jobs[i]
            load_weight(P, C, w3[:, :, c0:c0 + cw], 8, cw, wb[i % 2][:, :, 0:cw], wb[i % 2])

        load(0)
        for i, (w3, c0, cw, row0) in enumerate(jobs):
            if i + 1 < len(jobs):
                load(i + 1)
            b = wb[i % 2]
            for n in range(NCH):
                cs = slice(n * CH, (n + 1) * CH)
                m0 = 0
                while m0 < cw:
                    mw = min(128, cw - m0)
                    gcol_ = c0 + m0
                    if row0 == 0 and (O_RV <= gcol_ < O_RV + 256 or O_TV <= gcol_ < O_TV + 256):
                        m0 += mw
                        continue
                    ps = P.ps()
                    for k in range(8):
                        P.mm(ps, ps[0:mw, :], b[:, k, m0:m0 + mw], hT[:, k, cs], k == 0, k == 7,
                             reads=[b, (hT, (k, n))])
                    e = ev[evi % 4]
                    evi += 1
                    P.copy("act" if evi % 2 else "dve", e[0:mw, :], ps[0:mw, :], reads=[ps], writes=[e])
                    P.dma("sp", U[row0 + c0 + m0:row0 + c0 + m0 + mw, cs], e[0:mw, :], reads=[e], writes=[])
                    m0 += mw
        w3 = w_in.rearrange("(k p) c -> p k c", p=128)
        vb = wb[len(jobs) % 2]
        load_weight(P, C, w3[:, :, O_RV:O_RV + 256], 8, 256, vb[:, :, 0:256], vb, dst_key="a")
        load_weight(P, C, w3[:, :, O_TV:O_TV + 256], 8, 256, vb[:, :, 256:512], vb, dst_key="b")
        for b in range(T // 128):
            ps = P.ps()
            for k in range(8):
                P.mm(ps, ps[:, :], hT[:, k, b * 128:(b + 1) * 128], vb[:, k, :], k == 0, k == 7,
                     reads=[vb, (hT, (k, b // 4))])
            e = ev[evi % 4]
            evi += 1
            P.copy("act" if evi % 2 else "dve", e[:, :], ps[:, :], reads=[ps], writes=[e])
            P.dma("sp", VT[b * 128:(b + 1) * 128, :], e[:, :], reads=[e], writes=[])
        P.barrier()


def mixer_conv(P, C, l, dr, U, yT):
    with contextlib.ExitStack() as scope:
        cw = P.sb(scope, "cw", [128, 2, 3], F32)
        for ct in range(2):
            P.dma("sp", cw[:, ct, :], dr["conv_w"][l][:, ct * 128:(ct + 1) * 128].rearrange("j p -> p j"),
                  reads=[], writes=[cw], slow=True)
        sets = [[P.sb(scope, nm, [128, T + (2 if nm == "cu" else 0)], F32) for nm in ("cxs", "cbs", "ccs", "cu", "cacc")]
                for _ in range(2)]
        for ct in range(2):
            xs, bs, cs_, u, acc = sets[ct]
            load_rows(P, xs, xs[:, :], U, O_CX + ct * 128, 128)
            load_rows(P, bs, bs[:, :], U, O_CB + ct * 128, 128)
            load_rows(P, cs_, cs_[:, :], U, O_CC + ct * 128, 128)
        for ct in range(2):
            xs, bs, cs_, u, acc = sets[ct]
            P.memset("pool", u, u[:, :], 0.0)
            P.tt("dve" if ct == 0 else "pool", u[:, 1:T + 1], cs_[:, :], xs[:, :], ALU.mult, reads=[cs_, xs], writes=[u])
            P.ts("dve", acc[:, :], u[:, 0:T], cw[:, ct, 0:1], None, ALU.mult, None, reads=[u, cw], writes=[acc])
            P.stt("dve", acc[:, :], u[:, 1:T + 1], cw[:, ct, 1:2], acc[:, :], ALU.mult, ALU.add,
                  reads=[u, cw, acc], writes=[acc])
            P.stt("dve", acc[:, :], u[:, 2:T + 2], cw[:, ct, 2:3], acc[:, :], ALU.mult, ALU.add,
                  reads=[u, cw, acc], writes=[acc])
            P.tt("dve" if ct == 0 else "pool", yT[:, ct, :], acc[:, :], bs[:, :], ALU.mult, reads=[acc, bs],
                 writes=[(yT, ct)])
        P.barrier()


def rope_tables(P, C, scope, pos_d, inv_ap, sgn_ap):
    cos = P.sb(scope, "cos", [128, T], F32)
    sinS = P.sb(scope, "sinS", [128, T], F32)
    with contextlib.ExitStack() as tmp:
        A = P.sb(tmp, "rpA", [128, T], I32)
        B = P.sb(tmp, "rpB", [128, T], F32)
        Cc = P.sb(tmp, "rpC", [128, T], F32)
        P.dma("sp", A[:, :], pos_d.partition_broadcast(128), reads=[], writes=[A])
        P.copy("dve", B[:, :], A[:, :], reads=[A], writes=[B])
        P.ts("dve", B[:, :], B[:, :], inv_ap, None, ALU.mult, None, reads=[B, C.consts], writes=[B])
        P.ts("dve", Cc[:, :], B[:, :], 1.0 / (2.0 * math.pi), None, ALU.mult, None, reads=[B], writes=[Cc])
        P.copy("dve", A[:, :], Cc[:, :], reads=[Cc], writes=[A])
        P.copy("dve", Cc[:, :], A[:, :], reads=[A], writes=[Cc])
        P.stt("dve", B[:, :], Cc[:, :], -C1_2PI, B[:, :], ALU.mult, ALU.add, reads=[Cc, B], writes=[B])
        P.stt("dve", B[:, :], Cc[:, :], -C2_2PI, B[:, :], ALU.mult, ALU.add, reads=[Cc, B], writes=[B])
        P.ts("dve", Cc[:, :], B[:, :], math.pi, None, ALU.is_gt, None, reads=[B], writes=[Cc])
        P.stt("dve", B[:, :], Cc[:, :], -2.0 * math.pi, B[:, :], ALU.mult, ALU.add, reads=[Cc, B], writes=[B])
        P.act(sinS[:, :], B[:, :], AF.Sin, reads=[B, C.consts], writes=[sinS], scale=sgn_ap)
        P.ts("dve", B[:, :], B[:, :], 0.5 * math.pi, None, ALU.add, None, reads=[B], writes=[B])
        P.ts("dve", Cc[:, :], B[:, :], math.pi, None, ALU.is_gt, None, reads=[B], writes=[Cc])
        P.stt("dve", B[:, :], Cc[:, :], -2.0 * math.pi, B[:, :], ALU.mult, ALU.add, reads=[Cc, B], writes=[B])
        P.act(cos[:, :], B[:, :], AF.Sin, reads=[B], writes=[cos])
        P.barrier()
    return cos, sinS


def rope_load(P, C, scope, R, idx):
    cos = P.sb(scope, "cos", [128, T], F32)
    sinS = P.sb(scope, "sinS", [128, T], F32)
    P.dma("sp", cos[:, :], R[idx], reads=[], writes=[cos])
    P.dma("sp", sinS[:, :], R[idx + 1], reads=[], writes=[sinS])
    return cos, sinS


def rstd_from_sum(P, C, r_ap, ps_ap, r_tt, ps_tt, scale, plo, phi):
    P.act(r_ap, ps_ap, AF.Sqrt, reads=[ps_tt, C.eps_col], writes=[r_tt], bias=C.eps_col[plo:phi, 0:1], scale=scale)
    P.op("dve", lambda e, o=r_ap: e.reciprocal(o, o), reads=[r_tt], writes=[r_tt])


def mixer_ret(P, C, l, dr, U, VT, yT):
    lng = [math.log(1.0 - 2.0 ** (-5.0 - h)) for h in range(4)]
    with contextlib.ExitStack() as scope:
        qT = P.sb(scope, "rqT", [128, 2, T], BF16)
        kT = P.sb(scope, "rkT", [128, 2, T], BF16)
        gng = P.sb(scope, "gng", [128, 2], F32)
        P.dma("sp", gng[:, :], dr["ret_gn_g"][l].rearrange("(t p) -> p t", p=128), reads=[], writes=[gng], slow=True)
        vtm = P.sb(scope, "rvtm", [128, 16, 256], BF16)
        NM = 3968
        with contextlib.ExitStack() as sa:
            vsts = [P.sb(sa, "rvst", [128, 4, 256], F32) for _ in range(1)]
            cos, sinS = rope_load(P, C, sa, C.ROPE, 0)
            abuf = [P.sb(sa, "ra", [128, T], F32) for _ in range(2)]
            bbuf = [P.sb(sa, "rb", [128, T], F32) for _ in range(2)]
            for (dst, o_main, o_rot, scale) in ((qT, O_TQ, O_ROT, 1.0), (kT, O_TK, O_ROT + 256, 0.125)):
                for hp in range(2):
                    a = abuf[hp]
                    b = bbuf[hp]
                    load_rows(P, a, a[:, :], U, o_main + hp * 128, 128)
                    load_rows(P, b, b[:, :], U, o_rot + hp * 128, 128)
                    P.stt("dve", a[:, :], a[:, :], scale, cos[:, :], ALU.mult, ALU.mult, reads=[a, cos], writes=[a])
                    P.stt("dve", b[:, :], b[:, :], scale, sinS[:, :], ALU.mult, ALU.mult, reads=[b, sinS], writes=[b])
                    P.tt("pool", dst[:, hp, :], a[:, :], b[:, :], ALU.add, reads=[a, b], writes=[(dst, hp)])
            for half in range(4):
                vs_ = vsts[0]
                P.dma("sp", vs_[:, :, :],
                      VT[half * 512:(half + 1) * 512, 256:512].rearrange("(b p) c -> p b c", p=128),
                      reads=[], writes=[vs_])
                P.copy("pool", vtm[:, half * 4:(half + 1) * 4, :], vs_[:, :, :], reads=[vs_], writes=[(vtm, half)])
            P.barrier()
        with contextlib.ExitStack() as sb_:
            Dt = P.sb(sb_, "rD", [128, NM], F32)
            P.op("pool", lambda e, Dt=Dt, NM=NM: e.iota(Dt[:, :], [[1, NM]], base=-1920, channel_multiplier=-1,
                                          allow_small_or_imprecise_dtypes=True), reads=[], writes=[Dt])
            P.act(Dt[:, :], Dt[:, :], AF.Abs, reads=[Dt], writes=[Dt])
            mask = P.sb(sb_, "rmask", [128, NM], F32)
            sg = P.sb(sb_, "rsg", [128, T], F32)
            sm = [P.sb(sb_, "rsm", [128, CH], BF16) for _ in range(7)]
            sq = [P.sb(sb_, "rsq", [128, CH], BF16) for _ in range(2)]
            scb = [P.sb(sb_, "rscb", [128, CH], F32) for _ in range(2)]
            rs = [P.sb(sb_, "rrs", [128, CH], F32) for _ in range(2)]
            tmp = [P.sb(sb_, "rtmp", [128, CH], F32) for _ in range(2)]
            smi = 0
            it = 0
            for h in range(4):
                hp, pb = h // 2, (h % 2) * 64
                pr = slice(pb, pb + 64)
                if h % 2 == 0:
                    load_rows(P, sg, sg[:, :], U, O_TG + hp * 128, 128)
                    P.act(sg[:, :], sg[:, :], AF.Silu, reads=[sg], writes=[sg])
                P.act(mask[:, :], Dt[:, :], AF.Exp, reads=[Dt], writes=[mask], scale=lng[h])
                LA = 3
                pend = []

                def stage_c(n, j, m_, it_):
                    cs = slice(n * CH, (n + 1) * CH)
                    num = P.acc[it_ % 2]
                    P.mm(num, num[:, :], vtm[:, j, hp * 128:(hp + 1) * 128], m_[:, :], j == 0, j == 15,
                         reads=[vtm, m_])
                    if j != 15:
                        return
                    q_ = sq[it_ % 2]
                    P.act(q_[pr, :], num[pr, :], AF.Square, reads=[num], writes=[q_])
                    st = P.ps()
                    P.mm(st, st[:, :], C.ones64[pr, :], q_[pr, :], True, True, reads=[C.ones64, q_])
                    r = rs[it_ % 2]
                    rstd_from_sum(P, C, r[pr, :], st[pr, :], r, st, 1.0, pb, pb + 64)
                    t_ = tmp[it_ % 2]
                    P.stt("dve", t_[pr, :], num[pr, :], gng[pr, hp:hp + 1], r[pr, :], ALU.mult, ALU.mult,
                          reads=[num, gng, r], writes=[t_])
                    P.tt("dve", yT[pr, hp, cs], t_[pr, :], sg[pr, cs], ALU.mult, reads=[t_, sg],
                         writes=[(yT, (hp, n, pb))])

                G = 3
                steps = [(n, j) for n in range(NCH) for j in range(16)]
                prev = []
                for g0 in range(0, len(steps), G):
                    grp = steps[g0:g0 + G]
                    sps = []
                    for (n, j) in grp:
                        cs = slice(n * CH, (n + 1) * CH)
                        sp = P.ps()
                        P.mm(sp, sp[:, :], kT[pr, hp, j * 128:(j + 1) * 128], qT[pr, hp, cs], True, True,
                             reads=[kT, qT])
                        sps.append(sp)
                    cur = []
                    for (n, j), sp in zip(grp, sps):
                        off = n * CH - j * 128 + 1920
                        m_ = sm[smi % len(sm)]
                        smi += 1
                        if smi % 3 == 0:
                            sc_ = scb[(smi // 3) % 2]
                            P.copy("act", sc_[:, :], sp[:, :], reads=[sp], writes=[sc_])
                            P.tt("pool", m_[:, :], sc_[:, :], mask[:, off:off + CH], ALU.mult, reads=[sc_, mask],
                                 writes=[m_])
                        else:
                            P.tt("dve", m_[:, :], sp[:, :], mask[:, off:off + CH], ALU.mult, reads=[sp, mask],
                                 writes=[m_])
                        cur.append((n, j, m_, it + n))
                    for a_ in prev:
                        stage_c(*a_)
                    prev = cur
                for a_ in prev:
                    stage_c(*a_)
                it += NCH
            P.barrier()


def mixer_mla(P, C, l, dr, U, yT):
    with contextlib.ExitStack() as scope:
        qfull = P.sb(scope, "mqf", [128, 4, T], BF16)
        kfull = P.sb(scope, "mkf", [128, 4, T], BF16)
        qb = P.sb(scope, "mqb", [128, 3, 384], BF16)
        qbr = P.sb(scope, "mqbr", [128, 3, 384], BF16)
        kvb = P.sb(scope, "mkvb", [128, 1, 512], BF16)
        load_weight(P, C, dr["mla_q_b"][l].rearrange("(k p) c -> p k c", p=128), 3, 384, qb[:, :, :], qb)
        load_weight(P, C, dr["q_b_rot"][l].rearrange("(k p) c -> p k c", p=128), 3, 384, qbr[:, :, :], qbr)
        load_weight(P, C, dr["mla_kv_b"][l].rearrange("(k p) c -> p k c", p=128), 1, 512, kvb[:, :, :], kvb)
        qan = P.sb(scope, "mqan", [128, 3], F32)
        kvan = P.sb(scope, "mkvan", [128, 1], F32)
        P.dma("sp", qan[:, :], dr["mla_q_a_norm"][l].rearrange("(k p) -> p k", p=128), reads=[], writes=[qan], slow=True)
        P.dma("sp", kvan[:, :], dr["mla_kv_a_norm"][l].rearrange("(k p) -> p k", p=128), reads=[], writes=[kvan],
              slow=True)
        sqb = [P.sb(scope, "msq", [128, CH], BF16) for _ in range(3)]
        rr = [P.sb(scope, "mrr", [128, CH], F32) for _ in range(2)]
        sqi = 0
        with contextlib.ExitStack() as sa:
            cos, sinS = rope_load(P, C, sa, C.ROPE, 2)
            qn = P.sb(sa, "mqn", [128, 3, T], BF16)
            with contextlib.ExitStack() as sq_:
                qas = [P.sb(sq_, "mqa", [128, 3, CH], F32) for _ in range(2)]
                for n in range(NCH):
                    cs = slice(n * CH, (n + 1) * CH)
                    qa = qas[n % 2]
                    P.dma("sp", qa[:, :, :], U[O_MQA:O_MQA + 384, cs].rearrange("(k p) c -> p k c", p=128),
                          reads=[], writes=[qa])
                    ps = P.ps()
                    for k in range(3):
                        s_ = sqb[sqi % 3]
                        sqi += 1
                        P.act(s_[:, :], qa[:, k, :], AF.Square, reads=[qa], writes=[s_])
                        P.mm(ps, ps[:, :], C.ones_bf[:, :], s_[:, :], k == 0, k == 2, reads=[C.ones_bf, s_])
                    r = rr[n % 2]
                    rstd_from_sum(P, C, r[:, :], ps[:, :], r, ps, 1.0 / 384.0, 0, 128)
                    for k in range(3):
                        P.stt("dve", qn[:, k, cs], qa[:, k, :], qan[:, k:k + 1], r[:, :], ALU.mult, ALU.mult,
                              reads=[qa, qan, r], writes=[(qn, (k, n))])
                P.barrier()
            t1 = [P.sb(sa, "mt1", [128, CH], F32) for _ in range(2)]
            t2 = [P.sb(sa, "mt2", [128, CH], F32) for _ in range(2)]
            ti = 0
            rp = slice(64, 96)
            for h in range(4):
                for n in range(NCH):
                    cs = slice(n * CH, (n + 1) * CH)
                    ps = P.ps()
                    psr = P.ps()
                    for k in range(3):
                        P.mm(ps, ps[0:96, :], qb[:, k, h * 96:(h + 1) * 96], qn[:, k, cs], k == 0, k == 2,
                             reads=[qb, (qn, (k, n))])
                    for k in range(3):
                        P.mm(psr, psr[0:96, :], qbr[:, k, h * 96:(h + 1) * 96], qn[:, k, cs], k == 0, k == 2,
                             reads=[qbr, (qn, (k, n))])
                    P.copy("act", qfull[0:64, h, cs], ps[0:64, :], reads=[ps], writes=[(qfull, (h, n, 0))])
                    a_ = t1[ti % 2]
                    b_ = t2[ti % 2]
                    ti += 1
                    P.tt("dve", a_[rp, :], ps[rp, :], cos[rp, cs], ALU.mult, reads=[ps, cos], writes=[a_])
                    P.tt("dve", b_[rp, :], psr[rp, :], sinS[rp, cs], ALU.mult, reads=[psr, sinS], writes=[b_])
                    P.tt("pool", qfull[rp, h, cs], a_[rp, :], b_[rp, :], ALU.add, reads=[a_, b_],
                         writes=[(qfull, (h, n, 1))])
            for n in range(NCH):
                cs = slice(n * CH, (n + 1) * CH)
                a_ = t1[ti % 2]
                b_ = t2[ti % 2]
                ti += 1
                P.dma("sp", a_[rp, :], U[O_MKR:O_MKR + 32, cs], reads=[], writes=[a_])
                P.dma("sp", b_[rp, :], U[O_ROT + 512:O_ROT + 544, cs], reads=[], writes=[b_])
                P.tt("dve", a_[rp, :], a_[rp, :], cos[rp, cs], ALU.mult, reads=[a_, cos], writes=[a_])
                P.tt("dve", b_[rp, :], b_[rp, :], sinS[rp, cs], ALU.mult, reads=[b_, sinS], writes=[b_])
                for h in range(4):
                    P.tt("pool", kfull[rp, h, cs], a_[rp, :], b_[rp, :], ALU.add, reads=[a_, b_],
                         writes=[(kfull, (h, 1, n))])
            P.barrier()
        vtm = P.sb(scope, "mvtm", [128, 16, 256], BF16)
        with contextlib.ExitStack() as sk:
            ckv = P.sb(sk, "mckv", [128, T], F32)
            load_rows(P, ckv, ckv[:, :], U, O_MKV, 128)
            kvn = P.sb(sk, "mkvn", [128, T], BF16)
            for n in range(NCH):
                cs = slice(n * CH, (n + 1) * CH)
                ps = P.ps()
                s_ = sqb[sqi % 3]
                sqi += 1
                P.act(s_[:, :], ckv[:, cs], AF.Square, reads=[ckv], writes=[s_])
                P.mm(ps, ps[:, :], C.ones_bf[:, :], s_[:, :], True, True, reads=[C.ones_bf, s_])
                r = rr[n % 2]
                rstd_from_sum(P, C, r[:, :], ps[:, :], r, ps, 1.0 / 128.0, 0, 128)
                P.stt("dve", kvn[:, cs], ckv[:, cs], kvan[:, 0:1], r[:, :], ALU.mult, ALU.mult,
                      reads=[ckv, kvan, r], writes=[(kvn, n)])
            ci = 0
            for h in range(4):
                for n in range(NCH):
                    cs = slice(n * CH, (n + 1) * CH)
                    ps = P.ps()
                    P.mm(ps, ps[0:64, :], kvb[:, 0, h * 128:h * 128 + 64], kvn[:, cs], True, True,
                         reads=[kvb, (kvn, n)])
                    ci += 1
                    P.copy("act" if ci % 2 else "dve", kfull[0:64, h, cs], ps[0:64, :], reads=[ps],
                           writes=[(kfull, (h, 0, n))])
            for b in range(T // 128):
                ps = P.ps()
                P.mm(ps, ps[:, :], kvn[:, b * 128:(b + 1) * 128], kvb[:, 0, :], True, True, reads=[kvb, (kvn, b // 4)])
                ci += 1
                P.copy("act" if ci % 2 else "dve", vtm[:, b, :].rearrange("p (h d) -> p h d", d=64),
                       ps[:, :].rearrange("p (h e) -> p h e", e=128)[:, :, 64:128], reads=[ps], writes=[(vtm, b)])
            P.barrier()
        with contextlib.ExitStack() as st_:
            pt = [P.sb(st_, "mp", [128, CH], BF16) for _ in range(7)]
            rd = [P.sb(st_, "mrd", [128, CH], F32) for _ in range(2)]
            pi_ = 0
            it = 0
            sc = 96.0 ** -0.5
            for h in range(4):
                hp, pb = h // 2, (h % 2) * 64
                pr = slice(pb, pb + 64)
                LA = 3
                pend = []

                def stage_c(n, j, p_, it_):
                    cs = slice(n * CH, (n + 1) * CH)
                    num = P.acc[0]
                    den = P.acc[1]
                    P.mm(num, num[:, :], vtm[:, j, hp * 128:(hp + 1) * 128], p_[:, :], j == 0, j == 15,
                         reads=[vtm, p_])
                    P.mm(den, den[:, :], C.ones_bf[:, :], p_[:, :], j == 0, j == 15, reads=[C.ones_bf, p_])
                    if j != 15:
                        return
                    r = rd[it_ % 2]
                    P.op("dve", lambda e, o=r[pr, :], i=den[pr, :]: e.reciprocal(o, i), reads=[den], writes=[r])
                    P.tt("dve", yT[pr, hp, cs], num[pr, :], r[pr, :], ALU.mult, reads=[num, r],
                         writes=[(yT, (hp, n, pb))])

                G = 3
                steps = [(n, j) for n in range(NCH) for j in range(16)]
                prev = []
                for g0 in range(0, len(steps), G):
                    grp = steps[g0:g0 + G]
                    sps = []
                    for (n, j) in grp:
                        cs = slice(n * CH, (n + 1) * CH)
                        sp = P.ps()
                        P.mm(sp, sp[:, :], kfull[0:96, h, j * 128:(j + 1) * 128], qfull[0:96, h, cs], True, True,
                             reads=[kfull, qfull])
                        sps.append(sp)
                    cur = []
                    for (n, j), sp in zip(grp, sps):
                        p_ = pt[pi_ % len(pt)]
                        pi_ += 1
                        P.act(p_[:, :], sp[:, :], AF.Exp, reads=[sp], writes=[p_], scale=sc)
                        cur.append((n, j, p_, it + n))
                    for a_ in prev:
                        stage_c(*a_)
                    prev = cur
                for a_ in prev:
                    stage_c(*a_)
                it += NCH
            P.barrier()


            P.barrier()


AX = mybir.AxisListType
DEBUG_RW = None
DEBUG_CORES = None
DEBUG_TRACE = False
DEBUG_LAYERS = None


def mixer_rwkv(P, C, l, dr, U, VT, yT):
    NEG_E = -math.exp(-0.5)
    dbg = DEBUG_RW if l == 1 else None
    saved_ps = P.psums
    P.psums = saved_ps + P.acc
    P.psi = 0
    try:
        _mixer_rwkv(P, C, l, dr, U, VT, yT, dbg, NEG_E)
    finally:
        P.psums = saved_ps
        P.psi = 0


def _mixer_rwkv(P, C, l, dr, U, VT, yT, dbg, NEG_E):
    NB = T // 128
    with contextlib.ExitStack() as scope:
        cols = P.sb(scope, "wcols", [128, 2, 8], F32)
        for i, nm in enumerate(["rwkv_w0_f", "rwkv_w0_b", "rwkv_a0_f", "rwkv_a0_b", "rwkv_k_k", "rwkv_k_a"]):
            P.dma("sp", cols[:, :, i], dr[nm][l].rearrange("(t p) -> p t", p=128), reads=[], writes=[cols], slow=True)
        P.dma("sp", cols[:, :, 7], dr["rwkv_r_k"][l].rearrange("h n -> (h n)").rearrange("(t p) -> p t", p=128),
              reads=[], writes=[cols], slow=True)
        P.ts("dve", cols[:, :, 6], cols[:, :, 5], -1.0, 1.0, ALU.mult, ALU.add, reads=[cols], writes=[cols])
        gneps = P.sb(scope, "gneps", [128, 1], F32)
        P.memset("pool", gneps, gneps[:, :], 64e-5)
        lw = P.sb(scope, "lw", [128, 4, 256], BF16)
        g2b = P.sb(scope, "g2b", [128, 256], BF16)
        Lg = P.sb(scope, "Lg", [128, 256], F32)
        Lb = P.sb(scope, "Lb", [128, 256], F32)
        vtm = P.sb(scope, "wvtm", [128, NB, 256], BF16)
        with contextlib.ExitStack() as s0:
            lst = P.sb(s0, "lst", [128, 256], F32)
            for i, nm in enumerate(["rwkv_w2_f", "rwkv_w2_b", "rwkv_a2_f", "rwkv_a2_b"]):
                lp = slice(0, 64) if i % 2 == 0 else slice(64, 128)
                P.dma("sp", lst[lp, :], dr[nm][l], reads=[], writes=[lst])
                P.copy("dve", lw[lp, i, :], lst[lp, :], reads=[lst], writes=[(lw, i)])
            P.dma("sp", lst[:, :], dr["rwkv_g2"][l], reads=[], writes=[lst])
            P.copy("dve", g2b[:, :], lst[:, :], reads=[lst], writes=[g2b])
            P.dma("sp", Lg[:, :], dr["rwkv_lnx_g"][l].rearrange("(o c) -> o c", o=1).partition_broadcast(128),
                  reads=[], writes=[Lg])
            P.dma("sp", Lb[:, :], dr["rwkv_lnx_b"][l].rearrange("(o c) -> o c", o=1).partition_broadcast(128),
                  reads=[], writes=[Lb])
            vst = P.sb(s0, "wvst", [128, 4, 256], F32)
            for q4 in range(4):
                P.dma("sp", vst[:, :, :], VT[q4 * 512:(q4 + 1) * 512, 0:256].rearrange("(b p) c -> p b c", p=128),
                      reads=[], writes=[vst])
                P.copy("pool", vtm[:, q4 * 4:(q4 + 1) * 4, :], vst[:, :, :], reads=[vst], writes=[(vtm, q4)])
            P.barrier()
        MS = [P.sb(scope, "MS", [128, 256], BF16) for _ in range(2)]
        NMS = [P.sb(scope, "NMS", [128, 256], BF16) for _ in range(2)]
        for di in range(2):
            pat = [[1, 128]] if di == 0 else [[-1, 128]]
            cm = -1 if di == 0 else 1
            for j, cmp_ in enumerate((ALU.is_gt, ALU.is_ge)):
                P.op("pool", lambda e, o=MS[di][:, j * 128:(j + 1) * 128], pt=pat, c_=cmp_, m_=cm:
                     e.affine_select(o, C.ones_f[:, :], pt, c_, 0.0, base=0, channel_multiplier=m_),
                     reads=[C.ones_f], writes=[(MS[di], j)])
            P.ts("dve", NMS[di][:, :], MS[di][:, :], -1.0, None, ALU.mult, None, reads=[MS[di]], writes=[NMS[di]])
        smask = P.sb(scope, "smask", [128, CH], F32)
        P.memset("pool", smask, smask[:, :], 1.0)
        P.memset("pool", smask, smask[:, :].rearrange("p (c t) -> p c t", t=128)[:, :, 0:1], 0.0)
        identb = P.sb(scope, "identb", [128, 128], BF16)
        P.copy("dve", identb[:, :], C.ident[:, :], reads=[C.ident], writes=[identb])
        blk64 = P.sb(scope, "blk64", [128, 128], BF16)
        P.memset("pool", blk64, blk64[:, :], 0.0)
        P.memset("pool", blk64, blk64[0:64, 0:64], 1.0)
        P.memset("pool", blk64, blk64[64:128, 64:128], 1.0)
        blk2 = P.sb(scope, "blk2", [128, 2], BF16)
        P.memset("pool", blk2, blk2[:, :], 0.0)
        P.memset("pool", blk2, blk2[0:64, 0:1], 1.0)
        P.memset("pool", blk2, blk2[64:128, 1:2], 1.0)
        QpT = [P.sb(scope, "QpT", [128, NB, 128], BF16) for _ in range(2)]
        MpT = [P.sb(scope, "MpT", [128, NB, 128], BF16) for _ in range(2)]
        Gs = [P.sb(scope, "Gs", [128, NB, 64], F32) for _ in range(2)]
        pC = [P.sb(scope, "pC", [128, NB], F32) for _ in range(2)]
        yacc = P.sb(scope, "yacc", [128, NB, 128], F32)
        bon = P.sb(scope, "bon", [128, NB, 2], F32)
        for di in range(2):
            P.memset("pool", MpT[di], MpT[di][:, :, :], 0.0)
        P.barrier()
        v4 = lambda ap: ap.rearrange("p (c t) -> p c t", t=128)
        if dbg == "setup":
            return

        for hp in range(2):
            if dbg == "s1h0" and hp == 1:
                continue
            with contextlib.ExitStack() as sw:
                f32t = lambda nm: P.sb(sw, nm, [128, CH], F32)
                b16t = lambda nm: P.sb(sw, nm, [128, CH], BF16)
                kkf, nrm, af, t_ = f32t("kkf"), f32t("nrm"), f32t("af"), f32t("t_")
                adb = P.sb(sw, "adb", [128, CH], BF16)

                def issue_loads(n_):
                    st = C.stage[n_ % 2]
                    cs_ = slice(n_ * CH, (n_ + 1) * CH)
                    P.dma("sp", st[:, 0:512], U[O_RR + hp * 128:O_RR + hp * 128 + 128, cs_], reads=[],
                          writes=[(st, "r")])
                    P.dma("sp", st[:, 512:1024], U[O_RK + hp * 128:O_RK + hp * 128 + 128, cs_], reads=[],
                          writes=[(st, "k")])
                    P.dma("sp", st[:, 1024:1536], U[O_WDF:O_WDF + 128, cs_], reads=[], writes=[(st, "wd")])
                    P.dma("sp", st[:, 1536:2048], U[O_ADF:O_ADF + 128, cs_], reads=[], writes=[(st, "ad")])

                issue_loads(0)
                ld = [f32t("ld") for _ in range(2)]
                sq, thb, prod, rb, kkb = b16t("sq"), b16t("thb"), b16t("prod"), b16t("rb"), b16t("kkb")
                kd = [b16t("kd") for _ in range(2)]
                bb = [b16t("bb") for _ in range(2)]
                A = [f32t("A%d" % i) for i in range(4)]
                BSET = [(P.sb(sw, "opkr", [128, 4, 256], BF16), b16t("Kh"), b16t("Bh"), b16t("KhC"), b16t("nBhC"))
                        for _ in range(2)]
                if len(P.psums) == 8:
                    psA_bank = P.psums[-1]
                    P.psums = P.psums[:-1]
                    P.psi = 0
                psA = psA_bank
                NQ = 8
                AR = [P.sb(sw, "AR", [128, 256], BF16) for _ in range(NQ)]
                BR = [P.sb(sw, "BR", [128, 256], BF16) for _ in range(NQ)]
                MZ = [[P.sb(sw, "MZ", [128, 384], BF16) for _ in range(2)] for _ in range(NQ)]
                KBC = [P.sb(sw, "KBC", [128, 2, 128], BF16) for _ in range(NQ)]
                for qi in range(NQ):
                    P.memset("pool", KBC[qi], KBC[qi][:, :, :], 0.0)
                def phase_a(n, PP):
                    cs = slice(n * CH, (n + 1) * CH)
                    if n + 1 < NCH:
                        issue_loads(n + 1)
                    st = C.stage[n % 2]
                    rc_ap, kc_ap, wd_ap, ad_ap = st[:, 0:512], st[:, 512:1024], st[:, 1024:1536], st[:, 1536:2048]
                    PP.copy("pool", rb[:, :], rc_ap, reads=[(st, "r")], writes=[rb])
                    PP.ts("dve", kkf[:, :], kc_ap, cols[:, hp, 4:5], None, ALU.mult, None, reads=[(st, "k"), cols],
                         writes=[kkf])
                    PP.act(sq[:, :], kkf[:, :], AF.Square, reads=[kkf], writes=[sq])
                    ps = PP.ps()
                    PP.mm(ps, ps[:, :], blk64[:, :], sq[:, :], True, True, reads=[blk64, sq])
                    PP.act(nrm[:, :], ps[:, :], AF.Sqrt, reads=[ps], writes=[nrm])
                    PP.ts("dve", nrm[:, :], nrm[:, :], 1e-12, None, ALU.max, None, reads=[nrm], writes=[nrm])
                    PP.op("dve", lambda e, o=nrm[:, :]: e.reciprocal(o, o), reads=[nrm], writes=[nrm])
                    PP.tt("dve", kkf[:, :], kkf[:, :], nrm[:, :], ALU.mult, reads=[kkf, nrm], writes=[kkf])
                    PP.copy("pool", kkb[:, :], kkf[:, :], reads=[kkf], writes=[kkb])
                    PP.act(thb[:, :], wd_ap, AF.Tanh, reads=[(st, "wd")], writes=[thb])
                    PP.copy("pool", adb[:, :], ad_ap, reads=[(st, "ad")], writes=[adb])
                    for di in range(2):
                        lp = slice(0, 64) if di == 0 else slice(64, 128)
                        ps = PP.ps()
                        PP.mm(ps, ps[:, :], lw[lp, di, hp * 128:(hp + 1) * 128], thb[lp, :], True, True,
                             reads=[lw, thb])
                        PP.act(ld[di][:, :], ps[:, :], AF.Sigmoid, reads=[ps, cols], writes=[ld[di]],
                              bias=cols[:, hp, di:di + 1])
                        PP.ts("dve", ld[di][:, :], ld[di][:, :], NEG_E, None, ALU.mult, None, reads=[ld[di]],
                             writes=[ld[di]])
                        ps2 = PP.ps()
                        PP.mm(ps2, ps2[:, :], lw[lp, 2 + di, hp * 128:(hp + 1) * 128], adb[lp, :], True, True,
                             reads=[lw, adb])
                        PP.act(af[:, :], ps2[:, :], AF.Sigmoid, reads=[ps2, cols], writes=[af],
                              bias=cols[:, hp, 2 + di:3 + di])
                        PP.ts("dve", t_[:, :], af[:, :], cols[:, hp, 5:6], cols[:, hp, 6:7], ALU.mult, ALU.add,
                             reads=[af, cols], writes=[t_])
                        PP.tt("dve", kd[di][:, :], t_[:, :], kc_ap, ALU.mult, reads=[t_, (st, "k")], writes=[kd[di]])
                        PP.tt("pool", bb[di][:, :], kkf[:, :], af[:, :], ALU.mult, reads=[kkf, af], writes=[bb[di]])
                    PP.tt("pool", t_[:, :], kd[0][:, :], kd[1][:, :], ALU.add, reads=[kd[0], kd[1]], writes=[t_])
                    PP.stt("dve", prod[:, :], t_[:, :], cols[:, hp, 7:8], rc_ap, ALU.mult, ALU.mult,
                          reads=[t_, cols, (st, "r")], writes=[prod])
                    ps = PP.ps()
                    for bq in range(4):
                        PP.mm(ps, ps[:, bq * 2:bq * 2 + 2], prod[:, bq * 128:(bq + 1) * 128], blk2[:, 0:2], True, True,
                             reads=[prod, blk2])
                    PP.copy("act", bon[:, n * 4:(n + 1) * 4, :], ps[:, 0:8].rearrange("p (b h) -> p b h", h=2),
                           reads=[ps], writes=[(bon, n)])

                def phase_b(n, di, bs, PP):
                    opkr, Kh, Bh, KhC, nBhC = BSET[bs]
                    A1, A2, A3, A4 = A
                    PP.op("dve", lambda e, o=A1[:, :], m_=smask[:, :], d_=ld[di][:, :]:
                         e.tensor_tensor_scan(o, m_, d_, 0.0, ALU.mult, ALU.add),
                         reads=[smask, ld[di]], writes=[A1])
                    for cc in range(4):
                        c = n * 4 + cc
                        sl = slice(cc * 128, (cc + 1) * 128)
                        tot = A1[:, cc * 128 + 127:cc * 128 + 128]
                        PP.ts("dve", A2[:, sl], A1[:, sl], tot, -1.0, ALU.subtract, ALU.mult, reads=[A1],
                             writes=[(A2, cc)])
                        PP.act(pC[di][:, c:c + 1], tot, AF.Exp, reads=[A1], writes=[(pC[di], c)])
                    PP.tt("pool", A3[:, :], A1[:, :], ld[di][:, :], ALU.subtract, reads=[A1, ld[di]], writes=[A3])
                    if di == 1:
                        PP.tt("pool", A4[:, :], A2[:, :], ld[di][:, :], ALU.add, reads=[A2, ld[di]], writes=[A4])
                        c1, c1x, c2, ib = A4, A2, A3, A1
                    else:
                        c1, c1x, c2, ib = A1, A3, A2, A4
                    PP.act(ib[:, :], c1[:, :], AF.Exp, reads=[c1], writes=[ib], scale=-1.0)
                    PP.act(c1[:, :], c1[:, :], AF.Exp, reads=[c1], writes=[c1])
                    PP.act(c1x[:, :], c1x[:, :], AF.Exp, reads=[c1x], writes=[c1x])
                    PP.act(c2[:, :], c2[:, :], AF.Exp, reads=[c2], writes=[c2])
                    PP.tt("pool", opkr[:, :, 0:128], v4(kkb[:, :]), v4(c1x[:, :]), ALU.mult, reads=[kkb, c1x],
                         writes=[(opkr, 0)])
                    PP.tt("pool", opkr[:, :, 128:256], v4(rb[:, :]), v4(c1[:, :]), ALU.mult, reads=[rb, c1],
                         writes=[(opkr, 1)])
                    PP.tt("pool", Kh[:, :], kd[di][:, :], ib[:, :], ALU.mult, reads=[kd[di], ib], writes=[Kh])
                    PP.tt("pool", Bh[:, :], bb[di][:, :], ib[:, :], ALU.mult, reads=[bb[di], ib], writes=[Bh])
                    PP.tt("pool", KhC[:, :], kd[di][:, :], c2[:, :], ALU.mult, reads=[kd[di], c2], writes=[KhC])
                    PP.stt("dve", nBhC[:, :], bb[di][:, :], -1.0, c2[:, :], ALU.mult, ALU.mult,
                          reads=[bb[di], c2], writes=[nBhC])
                def stage1(n, di, bs, drip):
                    opkr, Kh, Bh, KhC, nBhC = BSET[bs]
                    NX = NMS[1 - di]
                    qs = [(cc, hh) for cc in range(4) for hh in (0, 1)]
                    for qi, (cc, hh) in enumerate(qs):
                        pr = slice(hh * 64, hh * 64 + 64)
                        tc = slice(cc * 128, (cc + 1) * 128)
                        psA = P.ps()
                        psB = P.ps()
                        P.mm(psA, psA[:, 0:256], Kh[pr, tc], opkr[pr, cc, :], True, True, reads=[Kh, opkr])
                        P.mm(psA, psA[:, 256:384], opkr[pr, cc, 0:128], Bh[pr, tc], True, True,
                             reads=[opkr, Bh])
                        P.mm(psB, psB[:, 0:256], Bh[pr, tc], opkr[pr, cc, :], True, True, reads=[Bh, opkr])
                        P.tt("dve", AR[qi][:, :], psA[:, 0:256], MS[di][:, :], ALU.mult,
                             reads=[psA, MS[di]], writes=[AR[qi]])
                        P.tt("dve", MZ[qi][0][:, 0:128], psA[:, 256:384], NX[:, 0:128], ALU.mult,
                             reads=[psA, NX], writes=[(MZ[qi][0], 0)])
                        P.tt("dve", BR[qi][:, :], psB[:, 0:256], NMS[di][:, :], ALU.mult,
                             reads=[psB, NMS[di]], writes=[BR[qi]])
                        P.copy("pool", MZ[qi][0][:, 256:384], BR[qi][:, 0:128], reads=[BR[qi]],
                               writes=[(MZ[qi][0], 2)])
                        drip()
                    for qi, (cc, hh) in enumerate(qs):
                        pb = hh * 64
                        pr = slice(pb, pb + 64)
                        tc = slice(cc * 128, (cc + 1) * 128)
                        c = n * 4 + cc
                        vcol = (2 * hp + hh) * 64
                        ps = P.ps()
                        P.mm(ps, ps[:, pb:pb + 64], opkr[pr, cc, 0:128], identb[pr, pb:pb + 64], True, True,
                             reads=[opkr, identb])
                        P.mm(ps, ps[:, 64 - pb:128 - pb], AR[qi][:, 0:128], vtm[:, c, vcol:vcol + 64], True,
                             True, reads=[AR[qi], vtm])
                        P.mm(ps, ps[:, 128:192], KhC[pr, tc], identb[pr, pb:pb + 64], True, True,
                             reads=[KhC, identb])
                        P.mm(ps, ps[:, 192:256], nBhC[pr, tc], identb[pr, pb:pb + 64], True, True,
                             reads=[nBhC, identb])
                        P.copy("act", MZ[qi][0][:, 128:256], ps[:, 0:128], reads=[ps], writes=[(MZ[qi][0], 1)])
                        P.copy("act", KBC[qi][:, :, pb:pb + 64],
                               ps[:, 128:256].rearrange("p (j k) -> p j k", k=64), reads=[ps],
                               writes=[KBC[qi]])
                    for lev in range(7):
                        for qi in range(NQ):
                            cur = MZ[qi][lev % 2]
                            nxt = MZ[qi][(lev + 1) % 2]
                            ps = P.ps()
                            ev_eng = "act" if qi % 2 == 0 else "dve"
                            drip()
                            if lev < 6:
                                P.mm(ps, ps[:, 0:256], cur[:, 256:384], cur[:, 0:256], True, False, reads=[cur])
                                P.mm(ps, ps[:, 128:256], identb[:, :], cur[:, 128:256], False, True,
                                     reads=[cur, identb])
                                P.mm(ps, ps[:, 256:384], cur[:, 0:128], cur[:, 256:384], True, True, reads=[cur])
                                P.copy(ev_eng, nxt[:, :], ps[:, 0:384], reads=[ps], writes=[nxt])
                            else:
                                P.mm(ps, ps[:, 128:256], cur[:, 256:384], cur[:, 128:256], True, False,
                                     reads=[cur])
                                P.mm(ps, ps[:, 128:256], identb[:, :], cur[:, 128:256], False, True,
                                     reads=[cur, identb])
                                P.copy(ev_eng, nxt[:, 128:256], ps[:, 128:256], reads=[ps], writes=[(nxt, 1)])
                    for qi, (cc, hh) in enumerate(qs):
                        pb = hh * 64
                        pr = slice(pb, pb + 64)
                        c = n * 4 + cc
                        vcol = (2 * hp + hh) * 64
                        ucol = 64 - pb
                        zt = MZ[qi][1]
                        zf = zt[:, 128:256]
                        zu = zt[:, 128 + ucol:128 + ucol + 64]
                        ps = P.ps()
                        P.mm(ps, ps[:, 0:128], zf, BR[qi][:, 128:256], True, True, reads=[zt, BR[qi]])
                        P.mm(ps, ps[:, 128:192], AR[qi][:, 128:256], vtm[:, c, vcol:vcol + 64], True, False,
                             reads=[AR[qi], vtm])
                        P.mm(ps, ps[:, 128:192], BR[qi][:, 128:256], zu, False, True, reads=[BR[qi], zt])
                        P.mm(ps, ps[:, 192:256], KBC[qi][:, 0, :], vtm[:, c, vcol:vcol + 64], True, False,
                             reads=[KBC[qi], vtm])
                        P.mm(ps, ps[:, 192:256], KBC[qi][:, 1, :], zu, False, True, reads=[KBC[qi], zt])
                        P.mm(ps, ps[:, 256:320], zf, KBC[qi][:, 1, pb:pb + 64], True, True,
                             reads=[zt, KBC[qi]])
                        if di == 0:
                            P.copy("act", yacc[:, c, pb:pb + 64], ps[:, 128:192], reads=[ps],
                                   writes=[(yacc, (c, hh))])
                        P.copy("act", Gs[di][pr, c, :], ps[pr, 192:256], reads=[ps],
                               writes=[(Gs[di], (c, hh))])
                        P.tt("dve", QpT[di][pr, c, :], ps[pr, 0:128], opkr[pr, cc, 128:256], ALU.add,
                             reads=[ps, opkr], writes=[(QpT[di], (c, hh))])
                        if di == 1:
                            P.tt("dve", yacc[:, c, pb:pb + 64], ps[:, 128:192], yacc[:, c, pb:pb + 64],
                                 ALU.add, reads=[ps, (yacc, (c, hh))], writes=[(yacc, (c, hh))])
                        P.copy("dve", MpT[di][pr, c, pb:pb + 64], ps[pr, 256:320], reads=[ps],
                               writes=[(MpT[di], (c, hh))])

                from collections import deque
                pend = deque()

                dcnt_ = [0]

                def drip():
                    dcnt_[0] += 1
                    if dcnt_[0] % 3 == 0:
                        return
                    if pend:
                        nm_, a_, k_ = pend.popleft()
                        getattr(P, nm_)(*a_, **k_)

                def flush():
                    while pend:
                        drip()

                class _Def:
                    def ps(self_):
                        return psA
                    def __getattr__(self_, nm_):
                        return lambda *a_, **k_: pend.append((nm_, a_, k_))

                DD = _Def()
                phase_a(0, P)
                phase_b(0, 0, 0, P)
                for n in range(NCH):
                    phase_b(n, 1, 1, DD)
                    if dbg != "B":
                        stage1(n, 0, 0, drip)
                    flush()
                    if n + 1 < NCH:
                        phase_a(n + 1, DD)
                        phase_b(n + 1, 0, 0, DD)
                    if dbg != "B":
                        stage1(n, 1, 1, drip)
                    flush()
                P.barrier()
            if dbg in ("A", "B", "s1", "s1a", "s1b", "s1c", "s1h0"):
                continue
            with contextlib.ExitStack() as s2:
                Hf = [P.sb(s2, "Hf", [128, 64], F32) for _ in range(2)]
                Hb = [P.sb(s2, "Hb", [128, 128], BF16) for _ in range(2)]
                for di in range(2):
                    P.memset("pool", Hf[di], Hf[di][:, :], 0.0)
                    P.memset("pool", Hb[di], Hb[di][:, :], 0.0)
                for i in range(NB):
                    for di in range(2):
                        c = i if di == 0 else NB - 1 - i
                        psY = P.ps()
                        P.mm(psY, psY[:, 0:128], QpT[di][:, c, :], Hb[di][:, :], True, True, reads=[QpT[di], Hb[di]])
                        P.tt("dve", yacc[:, c, :], psY[:, 0:128], yacc[:, c, :], ALU.add, reads=[psY, yacc],
                             writes=[yacc])
                        if i == NB - 1:
                            continue
                        psH = P.ps()
                        P.mm(psH, psH[:, 0:128], MpT[di][:, c, :], Hb[di][:, :], True, True, reads=[MpT[di], Hb[di]])
                        for hh in range(2):
                            pb = hh * 64
                            pr = slice(pb, pb + 64)
                            P.stt("dve", Hf[di][pr, :], Hf[di][pr, :], pC[di][pr, c:c + 1], psH[pr, pb:pb + 64],
                                  ALU.mult, ALU.add, reads=[Hf[di], pC[di], psH], writes=[Hf[di]])
                            P.tt("pool", Hf[di][pr, :], Hf[di][pr, :], Gs[di][pr, c, :], ALU.add,
                                 reads=[Hf[di], Gs[di]], writes=[Hf[di]])
                            P.copy("act", Hb[di][pr, pb:pb + 64], Hf[di][pr, :], reads=[Hf[di]], writes=[Hb[di]])
                P.barrier()
            if dbg == "s2":
                continue
            with contextlib.ExitStack() as s3:
                tq = P.sb(s3, "tq", [128, NB, 128], F32)
                finb = P.sb(s3, "finb", [128, NB, 128], BF16)
                mu = P.sb(s3, "mu", [128, NB, 2], F32)
                var = P.sb(s3, "var", [128, NB, 2], F32)
                sgd = P.sb(s3, "sgd", [128, T], BF16)
                gl = P.sb(s3, "gl", [128, CH], F32)
                for n in range(NCH):
                    cs = slice(n * CH, (n + 1) * CH)
                    P.dma("sp", gl[:, :], U[O_GD:O_GD + 128, cs], reads=[], writes=[gl])
                    P.act(sgd[:, cs], gl[:, :], AF.Sigmoid, reads=[gl], writes=[(sgd, n)])
                if dbg == "p1":
                    P.barrier()
                    continue
                g3 = lambda t: t[:, :, :].rearrange("p b (h v) -> p b h v", v=64)
                y3 = g3(yacc)
                q3 = g3(tq)
                bc = lambda t: t[:, :, :].unsqueeze(3).to_broadcast([128, NB, 2, 64])
                P.op("dve", lambda e, o=mu[:, :, :], i=y3: e.tensor_reduce(o, i, AX.X, ALU.add), reads=[yacc], writes=[mu])
                P.ts("dve", mu[:, :, :], mu[:, :, :], 1.0 / 64.0, None, ALU.mult, None, reads=[mu], writes=[mu])
                P.tt("dve", y3, y3, bc(mu), ALU.subtract, reads=[yacc, mu], writes=[yacc])
                P.act(tq[:, :, :], yacc[:, :, :], AF.Square, reads=[yacc], writes=[tq])
                P.op("dve", lambda e, o=var[:, :, :], i=q3: e.tensor_reduce(o, i, AX.X, ALU.add), reads=[tq], writes=[var])
                P.act(var[:, :, :], var[:, :, :], AF.Sqrt, reads=[var, gneps], writes=[var], bias=gneps[:, 0:1],
                      scale=1.0 / 64.0)
                P.op("dve", lambda e, o=var[:, :, :]: e.reciprocal(o, o), reads=[var], writes=[var])
                P.tt("dve", y3, y3, bc(var), ALU.mult, reads=[yacc, var], writes=[yacc])
                lgb = Lg[:, hp * 128:(hp + 1) * 128].unsqueeze(1).to_broadcast([128, NB, 128])
                lbb = Lb[:, hp * 128:(hp + 1) * 128].unsqueeze(1).to_broadcast([128, NB, 128])
                P.tt("dve", yacc[:, :, :], yacc[:, :, :], lgb, ALU.mult, reads=[yacc, Lg], writes=[yacc])
                P.tt("dve", yacc[:, :, :], yacc[:, :, :], lbb, ALU.add, reads=[yacc, Lb], writes=[yacc])
                vv = vtm[:, :, hp * 128:(hp + 1) * 128].rearrange("p b (h v) -> p b h v", v=64)
                bonb = bon[:, :, :].unsqueeze(3).to_broadcast([128, NB, 2, 64])
                P.tt("dve", q3, vv, bonb, ALU.mult, reads=[vtm, bon], writes=[tq])
                P.tt("dve", yacc[:, :, :], yacc[:, :, :], tq[:, :, :], ALU.add, reads=[yacc, tq], writes=[yacc])
                if dbg == "p2":
                    P.barrier()
                    continue
                P.barrier()
                for b4 in range(NB // 4):
                    ps = P.ps()
                    for bq in range(4):
                        b = b4 * 4 + bq
                        P.mm(ps, ps[:, bq * 128:(bq + 1) * 128], sgd[:, b * 128:(b + 1) * 128],
                             g2b[:, hp * 128:(hp + 1) * 128], True, True, reads=[sgd, g2b])
                    P.tt("dve", finb[:, b4 * 4:(b4 + 1) * 4, :], yacc[:, b4 * 4:(b4 + 1) * 4, :],
                         ps[:, :].rearrange("p (b c) -> p b c", c=128), ALU.mult, reads=[yacc, ps],
                         writes=[(finb, b4)])
                    ps2 = P.ps()
                    for bq in range(4):
                        b = b4 * 4 + bq
                        P.mm(ps2, ps2[:, bq * 128:(bq + 1) * 128], finb[:, b, :], identb[:, :], True, True,
                             reads=[(finb, b4), identb])
                    P.copy("act", yT[:, hp, b4 * 512:(b4 + 1) * 512], ps2[:, :], reads=[ps2], writes=[(yT, (hp, b4))])
                P.barrier()


def outproj_part(P, C, xT, yT, w_out, m):
    with contextlib.ExitStack() as scope:
        wo = P.sb(scope, "wo", [128, 2, D], BF16)
        w3 = w_out.rearrange("(k p) c -> p k c", p=128)
        load_weight(P, C, w3[:, 2 * m:2 * m + 2, :], 2, D, wo[:, :, :], wo)
        for j in range(8):
            for n in range(NCH):
                cs = slice(n * CH, (n + 1) * CH)
                ps = P.ps()
                for k in range(2):
                    P.mm(ps, ps[:, :], wo[:, k, j * 128:(j + 1) * 128], yT[:, k, cs], k == 0, k == 1,
                         reads=[wo, yT])
                P.tt("dve", xT[:, j, cs], ps[:, :], xT[:, j, cs], ALU.add, reads=[ps, (xT, (j, n))],
                     writes=[(xT, (j, n))])
        P.barrier()


def outproj_load(P, C, scope, w_out, ms):
    wo = P.sb(scope, "wo", [128, 2 * len(ms), D], BF16)
    w3 = w_out.rearrange("(k p) c -> p k c", p=128)
    for slot, m in enumerate(ms):
        load_weight(P, C, w3[:, 2 * m:2 * m + 2, :], 2, D, wo[:, 2 * slot:2 * slot + 2, :], wo, dst_key=slot)
    return wo


def outproj_multi(P, C, xT, yTn, wo, ms):
    nk = 2 * len(ms)
    if True:
        for n in range(NCH):
            cs = slice(n * CH, (n + 1) * CH)
            for j in range(8):
                ps = P.ps()
                for k in range(nk):
                    P.mm(ps, ps[:, :], wo[:, k, j * 128:(j + 1) * 128], yTn[:, k, cs], k == 0, k == nk - 1,
                         reads=[wo, yTn])
                P.tt("dve", xT[:, j, cs], ps[:, :], xT[:, j, cs], ALU.add, reads=[ps, (xT, (j, n))],
                     writes=[(xT, (j, n))])
        P.barrier()


def phase_mix(P, C, xT, l, dr, U, VT, debug_y=False, mixers=("conv", "rwkv", "ret", "mla")):
    g = TT(C.gains["mix_norm"].h[:, l, :], "gm")
    g.whole = C.gains["mix_norm"].whole
    phase_inproj(P, C, xT, g, dr["w_in"][l], dr["w_rot"][l], U, VT)
    if debug_y:
        for k in range(8):
            P.memset("pool", xT, xT[:, k, :], 0.0)
        P.barrier()
    MIDX = {"conv": 0, "rwkv": 1, "ret": 2, "mla": 3}
    if "rwkv" in mixers:
        with contextlib.ExitStack() as scope:
            yT = P.sb(scope, "yT", [128, 2, T], BF16)
            mixer_rwkv(P, C, l, dr, U, VT, yT)
            if debug_y:
                for k in range(2):
                    P.copy("dve", xT[:, 2 + k, :], yT[:, k, :], reads=[yT], writes=[xT])
                P.barrier()
            else:
                outproj_part(P, C, xT, yT, dr["w_out"][l], 1)
    rest = [nm for nm in ("conv", "ret", "mla") if nm in mixers]
    if rest:
        with contextlib.ExitStack() as scope:
            yTn = P.sb(scope, "yTn", [128, 2 * len(rest), T], BF16)
            wo_all = None if debug_y else outproj_load(P, C, scope, dr["w_out"][l], [MIDX[nm] for nm in rest])
            for slot, name in enumerate(rest):
                view = TT(yTn.h[:, 2 * slot:2 * slot + 2, :], "yv_" + name)
                if name == "conv":
                    mixer_conv(P, C, l, dr, U, view)
                elif name == "ret":
                    mixer_ret(P, C, l, dr, U, VT, view)
                else:
                    mixer_mla(P, C, l, dr, U, view)
            if debug_y:
                for slot, name in enumerate(rest):
                    for k in range(2):
                        P.copy("dve", xT[:, 2 * MIDX[name] + k, :], yTn[:, 2 * slot + k, :], reads=[yTn], writes=[xT])
                P.barrier()
            else:
                outproj_multi(P, C, xT, yTn, wo_all, [MIDX[nm] for nm in rest])


W_NAMES = ["ffn1_norm", "ffn1_w_gate", "ffn1_w_up", "ffn1_w_down", "mix_norm", "w_in", "w_out", "conv_w",
           "rwkv_w0_f", "rwkv_w0_b", "rwkv_w2_f", "rwkv_w2_b", "rwkv_a0_f", "rwkv_a0_b", "rwkv_a2_f", "rwkv_a2_b",
           "rwkv_g2", "rwkv_k_k", "rwkv_k_a", "rwkv_r_k", "rwkv_lnx_g", "rwkv_lnx_b", "ret_gn_g", "mla_q_a_norm",
           "mla_q_b", "mla_kv_a_norm", "mla_kv_b", "ffn2_norm", "ffn2_w_gate", "ffn2_w_up", "ffn2_w_down",
           "final_norm"]


def build(shapes, stop_after=None, layers=(0, 1), final_norm=True):
    nc = bass.Bass("TRN2", target_bir_lowering=False)
    dr = {}
    for name, (shape, dt) in shapes.items():
        dr[name] = nc.dram_tensor(name, list(shape), dt, kind="ExternalInput").ap()
    out_d = nc.dram_tensor("out", [T, D], F32, kind="ExternalOutput").ap()
    U = nc.dram_tensor("u_scr", [UROWS, T], F32, kind="Internal").ap()
    VT = nc.dram_tensor("vt_scr", [T, 512], F32, kind="Internal").ap()
    ROPE = nc.dram_tensor("rope_scr", [4, 128, T], F32, kind="Internal").ap()
    with contextlib.ExitStack() as es:
        P = Prog(nc, es)
        C = Ctx()
        P.acc = []
        for i in range(8):
            h = es.enter_context(nc.psum_tensor("ps%d" % i, [128, 512], F32))
            t_ps = TT(h, "ps%d" % i)
            t_ps.psum = True
            (P.psums if i < 6 else P.acc).append(t_ps)
        xT = P.sb(es, "xT", [128, 8, T], F32)
        C.stage = [P.sb(es, "stage", [128, 2048], F32) for _ in range(2)]
        C.stage_i = 0
        C.ident = P.sb(es, "ident", [128, 128], F32)
        C.ones_f = P.sb(es, "ones_f", [128, 128], F32)
        C.ones_mean = P.sb(es, "ones_mean", [128, 128], BF16)
        C.eps_col = P.sb(es, "eps_col", [128, 1], F32)
        P.memset("pool", C.eps_col, C.eps_col[:, :], EPS)
        P.memset("pool", C.ones_f, C.ones_f[:, :], 1.0)
        P.memset("pool", C.ones_mean, C.ones_mean[:, :], 1.0 / D)
        P.op("pool", lambda e: e.affine_select(C.ident[:, :], C.ones_f[:, :], [[1, 128]], ALU.is_equal, 0.0,
                                               base=0, channel_multiplier=-1),
             reads=[C.ones_f], writes=[C.ident])
        C.ones_bf = P.sb(es, "ones_bf", [128, 128], BF16)
        C.ones64 = P.sb(es, "ones64", [128, 128], BF16)
        P.memset("pool", C.ones_bf, C.ones_bf[:, :], 1.0)
        P.memset("pool", C.ones64, C.ones64[:, :], 1.0 / 64.0)
        C.consts = P.sb(es, "consts", [128, 4], F32)
        with contextlib.ExitStack() as tmp:
            row = P.sb(tmp, "crow", [1, 4, 128], F32)
            one = P.sb(tmp, "cone", [1, 1], F32)
            P.memset("pool", one, one[:, :], 1.0)
            for i in range(32):
                P.memset("pool", row, row[0:1, 0, :].rearrange("o (r i) -> o r i", i=32)[:, :, i:i + 1],
                         10000.0 ** (-i / 32.0))
            for i in range(16):
                P.memset("pool", row, row[0:1, 2, :].rearrange("o (r i) -> o r i", i=16)[:, :, i:i + 1],
                         10000.0 ** (-i / 16.0))
            P.memset("pool", row, row[0:1, 1, :].rearrange("o (r i) -> o r i", i=64)[:, :, 0:32], -1.0)
            P.memset("pool", row, row[0:1, 1, :].rearrange("o (r i) -> o r i", i=64)[:, :, 32:64], 1.0)
            P.memset("pool", row, row[0:1, 3, :].rearrange("o (r i) -> o r i", i=32)[:, :, 0:16], -1.0)
            P.memset("pool", row, row[0:1, 3, :].rearrange("o (r i) -> o r i", i=32)[:, :, 16:32], 1.0)
            ps = P.ps()
            for c in range(4):
                P.mm(ps, ps[:, c:c + 1], row[0:1, c, :], one[0:1, 0:1], True, True, reads=[row, one])
            P.copy("dve", C.consts[:, :], ps[:, 0:4], reads=[ps], writes=[C.consts])
            P.barrier()
        C.ROPE = ROPE
        for idx, (ci, si_) in enumerate(((0, 1), (2, 3))):
            with contextlib.ExitStack() as tmp:
                cos_, sin_ = rope_tables(P, C, tmp, dr["positions"], C.consts[:, ci:ci + 1], C.consts[:, si_:si_ + 1])
                P.dma("sp", ROPE[2 * idx], cos_[:, :], reads=[cos_], writes=[])
                P.dma("sp", ROPE[2 * idx + 1], sin_[:, :], reads=[sin_], writes=[])
                P.barrier()
        C.gains = {}
        for nm in ("ffn1_norm", "mix_norm", "ffn2_norm"):
            g = P.sb(es, "g_" + nm, [128, DEPTH, 8], F32)
            for l in range(DEPTH):
                P.dma("sp", g[:, l, :], dr[nm][l].rearrange("(k p) -> p k", p=128), reads=[], writes=[g], slow=True)
            C.gains[nm] = g
        gfin = P.sb(es, "g_final", [128, 8], F32)
        P.dma("sp", gfin[:, :], dr["final_norm"].rearrange("(k p) -> p k", p=128), reads=[], writes=[gfin], slow=True)

        load_x(P, C, xT, dr["x"])
        done = False
        for l in layers:
            g1 = TT(C.gains["ffn1_norm"].h[:, l, :], "g1")
            g1.whole = C.gains["ffn1_norm"].whole
            phase_ffn(P, C, xT, g1, dr["ffn1_w_gate"][l], dr["ffn1_w_up"][l], dr["ffn1_w_down"][l])
            if stop_after == ("ffn1", l):
                done = True
                break
            if stop_after is not None and stop_after[0].startswith("y") and stop_after[1] == l:
                phase_mix(P, C, xT, l, dr, U, VT, debug_y=True, mixers=stop_after[0].split("_")[1:])
                done = True
                break
            if stop_after is not None and stop_after[0].startswith("m_") and stop_after[1] == l:
                phase_mix(P, C, xT, l, dr, U, VT, mixers=stop_after[0].split("_")[1:])
                done = True
                break
            phase_mix(P, C, xT, l, dr, U, VT)
            if stop_after == ("mix", l):
                done = True
                break
            g2 = TT(C.gains["ffn2_norm"].h[:, l, :], "g2")
            g2.whole = C.gains["ffn2_norm"].whole
            phase_ffn(P, C, xT, g2, dr["ffn2_w_gate"][l], dr["ffn2_w_up"][l], dr["ffn2_w_down"][l])
            if stop_after == ("ffn2", l):
                done = True
                break
        store_out(P, C, xT, gfin, out_d, do_norm=(final_norm and not done))
        P.emit()
    return nc


EXTRA = ["w_rot", "q_b_rot"]


def _host_layouts(inputs):
    w_in = inputs["w_in"]
    def rot_cols(w, c0, nheads, hd):
        half = hd // 2
        cols = []
        for h in range(nheads):
            base = c0 + h * hd
            cols += list(range(base + half, base + hd)) + list(range(base, base + half))
        return w[..., cols]
    w_rot = np.concatenate([rot_cols(w_in, O_TQ, 4, 64), rot_cols(w_in, O_TK, 4, 64), rot_cols(w_in, O_MKR, 1, 32)],
                           axis=-1)
    qb = inputs["mla_q_b"]
    cols = []
    for h in range(4):
        base = h * 96
        cols += list(range(base, base + 64)) + list(range(base + 80, base + 96)) + list(range(base + 64, base + 80))
    q_b_rot = qb[..., cols]
    return {"w_rot": np.ascontiguousarray(w_rot), "q_b_rot": np.ascontiguousarray(q_b_rot)}


def _shapes(inputs, extra):
    shapes = {"x": ((T, D), F32), "positions": ((1, T), I32)}
    for n in W_NAMES:
        shapes[n] = (inputs[n].shape, F32)
    for n in EXTRA:
        shapes[n] = (extra[n].shape, F32)
    return shapes


N_LAUNCH = 1


def kernel(_stop_after=None, **inputs):
    ncores = 8 if DEBUG_CORES is None else DEBUG_CORES
    extra = _host_layouts(inputs)
    shapes = _shapes(inputs, extra)
    if _stop_after is not None or N_LAUNCH == 1:
        plans = [((0, 1) if DEBUG_LAYERS is None else DEBUG_LAYERS, True)]
    else:
        plans = [((0,), False), ((1,), True)]
    xs = [np.ascontiguousarray(inputs["x"][c]) for c in range(ncores)]
    for layers, fin in plans:
        nc = build(shapes, stop_after=_stop_after, layers=layers, final_norm=fin)
        in_maps = []
        for c in range(ncores):
            m = {"x": xs[c],
                 "positions": np.ascontiguousarray(inputs["positions"][c:c + 1]).astype(np.int32)}
            for n in W_NAMES:
                m[n] = np.ascontiguousarray(inputs[n])
            for n in EXTRA:
                m[n] = extra[n]
            in_maps.append(m)
        if DEBUG_TRACE:
            res = run_bass_kernel_spmd(nc, in_maps, core_ids=list(range(ncores)), trace=True)
            print("EXEC_NS", res.exec_time_ns)
        else:
            res = run_bass_kernel_spmd(nc, in_maps, core_ids=list(range(ncores)))
        xs = [np.ascontiguousarray(np.asarray(r["out"])) for r in res.results]
    out = np.stack(xs + [np.zeros((T, D), np.float32)] * (8 - ncores), axis=0)
    return out.astype(np.float32)
```

```python
import contextlib
import numpy as np
import concourse.bass as bass
import concourse.mybir as mybir
from concourse.bass_utils import run_bass_kernel_spmd

F32 = mybir.dt.float32
BF16 = mybir.dt.bfloat16
I32 = mybir.dt.int32
AF = mybir.ActivationFunctionType
ALU = mybir.AluOpType

D = 1024
T = 2048
DFF = 2816
DEPTH = 2
INW = 3488
NCH = 4
CH = 512
EPS = 1e-6

COMPUTE = ("pe", "act", "dve", "pool")
ENGS = ("sp", "pe", "act", "dve", "pool")
NDS = 24
EPOCH = 2000
NEPOCH = {"pe": 16, "act": 6, "dve": 8, "pool": 3}


class Buf:
    __slots__ = ("w", "r")

    def __init__(self):
        self.w = None
        self.r = {}


class TT:
    def __init__(self, h, name):
        self.h = h
        self.name = name
        self.whole = Buf()
        self.parts = {}

    def __getitem__(self, idx):
        return self.h[idx]

    def part(self, k):
        p = self.parts.get(k)
        if p is None:
            p = self.parts[k] = Buf()
        return p


def _split(a):
    if isinstance(a, tuple):
        return a[0], a[1]
    return a, None


class Prog:
    def __init__(self, nc, es):
        self.nc = nc
        self.es = es
        self.streams = {e: [] for e in ENGS}
        self.cnt = {e: 0 for e in COMPUTE}
        self.sem = {}
        for e in COMPUTE:
            for ep in range(NEPOCH[e]):
                self.sem[("c", e, ep)] = es.enter_context(nc.semaphore("s_%s%d" % (e, ep)))
        for i in range(NDS):
            self.sem[("d", i)] = es.enter_context(nc.semaphore("d%d" % i))
        self.dcnt = [0] * NDS
        self.dnext = 0
        self.seen = {e: {} for e in ENGS}
        self.psums = []
        self.psi = 0
        self.uid = 0

    def sb(self, scope, name, shape, dt):
        self.uid += 1
        h = scope.enter_context(self.nc.sbuf_tensor("%s_%d" % (name, self.uid), list(shape), dt))
        return TT(h, name)

    def ps(self):
        t = self.psums[self.psi]
        self.psi = (self.psi + 1) % len(self.psums)
        return t

    def _deps(self, reads, writes):
        deps = {}

        def add(ev):
            if ev is None:
                return
            k, v = ev
            if deps.get(k, 0) < v:
                deps[k] = v

        for a in reads:
            t, key = _split(a)
            add(t.whole.w)
            if key is None:
                for p in t.parts.values():
                    add(p.w)
            else:
                add(t.part(key).w)
        for a in writes:
            t, key = _split(a)
            add(t.whole.w)
            for kv in t.whole.r.items():
                add(kv)
            if key is None:
                for p in t.parts.values():
                    add(p.w)
                    for kv in p.r.items():
                        add(kv)
            else:
                p = t.part(key)
                add(p.w)
                for kv in p.r.items():
                    add(kv)
        return deps

    def _commit(self, reads, writes, ev):
        k, v = ev
        for a in reads:
            t, key = _split(a)
            b = t.whole if key is None else t.part(key)
            if b.r.get(k, 0) < v:
                b.r[k] = v
        for a in writes:
            t, key = _split(a)
            if key is None:
                t.whole.w = ev
                t.whole.r = {}
                t.parts = {}
            else:
                p = t.part(key)
                p.w = ev
                p.r = {}

    def _waits(self, eng, deps):
        waits = []
        for k, v in deps.items():
            if eng == "pe" and k[0] == "c" and k[1] == "pe":
                continue
            if self.seen[eng].get(k, 0) >= v:
                continue
            self.seen[eng][k] = v
            waits.append((k, v))
        return waits

    def op(self, eng, fn, reads=(), writes=()):
        pr_ = [a for a in reads if getattr(_split(a)[0], "psum", False)]
        if pr_:
            reads = [a for a in reads if not getattr(_split(a)[0], "psum", False)]
            writes = list(writes) + pr_
        deps = self._deps(reads, writes)
        waits = self._waits(eng, deps)
        j = self.cnt[eng]
        self.cnt[eng] += 1
        ev = (("c", eng, j // EPOCH), j % EPOCH + 1)
        self.streams[eng].append((waits, fn, ev[0], 1))
        self._commit(reads, writes, ev)

    def dma(self, q, out, in_, reads=(), writes=(), slow=False):
        deps = self._deps(reads, writes)
        i = self.dnext
        self.dnext = (i + 1) % NDS
        if self.dcnt[i] > 0:
            k = ("d", i)
            if deps.get(k, 0) < self.dcnt[i]:
                deps[k] = self.dcnt[i]
        waits = self._waits(q, deps)
        self.dcnt[i] += 16
        ev = (("d", i), self.dcnt[i])
        if slow:
            fn = lambda e, o=out, s=in_: e.dma_start(out=o, in_=s, allow_slow_non_contiguous=True)
        else:
            fn = lambda e, o=out, s=in_: e.dma_start(out=o, in_=s)
        self.streams[q].append((waits, fn, ev[0], 16))
        self._commit(reads, writes, ev)

    def barrier(self):
        tot = {}
        for e in COMPUTE:
            if self.cnt[e] > 0:
                j = self.cnt[e] - 1
                tot[("c", e, j // EPOCH)] = j % EPOCH + 1
        for i in range(NDS):
            if self.dcnt[i] > 0:
                tot[("d", i)] = self.dcnt[i]
        for e in ENGS:
            waits = []
            for k, v in tot.items():
                if k[0] == "c" and k[1] == e and e == "pe":
                    continue
                if self.seen[e].get(k, 0) >= v:
                    continue
                self.seen[e][k] = v
                waits.append((k, v))
            if waits:
                self.streams[e].append((waits, None, None, 0))

    def simulate(self):
        sem = {k: 0 for k in self.sem}
        pc = {e: 0 for e in ENGS}
        progress = True
        while progress:
            progress = False
            for e in ENGS:
                st = self.streams[e]
                while pc[e] < len(st):
                    waits, fn, semk, inc = st[pc[e]]
                    if any(sem[k] < v for k, v in waits):
                        break
                    if fn is not None:
                        sem[semk] += inc
                    pc[e] += 1
                    progress = True
        bad = {e: (pc[e], len(self.streams[e])) for e in ENGS if pc[e] < len(self.streams[e])}
        if bad:
            print("DEADLOCK", bad)
            for e in bad:
                waits = self.streams[e][pc[e]][0]
                print(e, [(k, v, sem[k]) for k, v in waits if sem[k] < v])
        else:
            print("SIM OK")
        return not bad

    def emit(self):
        nc = self.nc
        import os
        if os.environ.get("SIMCHECK"):
            self.simulate()
        print("COUNTS", self.cnt, max(self.dcnt), {e: len(v) for e, v in self.streams.items()})

        def run(stream):
            def f(e):
                for waits, fn, semk, inc in stream:
                    for k, v in waits:
                        e.wait_ge(self.sem[k], v)
                    if fn is not None:
                        fn(e).then_inc(self.sem[semk], inc)
            return f

        with nc.Block() as block:
            block.sync(run(self.streams["sp"]))
            block.tensor(run(self.streams["pe"]))
            block.scalar(run(self.streams["act"]))
            block.vector(run(self.streams["dve"]))
            block.gpsimd(run(self.streams["pool"]))

    def mm(self, ps, out, lhsT, rhs, start, stop, reads, writes=None):
        self.op("pe", lambda e, o=out, l=lhsT, r=rhs, s=start, p=stop: e.matmul(o, l, r, start=s, stop=p),
                reads=reads, writes=[ps] if writes is None else writes)

    def transpose(self, ps, out, in_, ident, reads):
        self.op("pe", lambda e, o=out, i=in_, d=ident: e.transpose(o, i, d), reads=reads, writes=[ps])

    def act(self, out, in_, func, reads, writes, bias=None, scale=None, eng="act"):
        kw = {}
        if bias is not None:
            kw["bias"] = bias
        if scale is not None:
            kw["scale"] = scale
        self.op(eng, lambda e, o=out, i=in_, f=func, kw=kw: e.activation(o, i, f, **kw), reads=reads, writes=writes)

    def tt(self, eng, out, in0, in1, op, reads, writes):
        self.op(eng, lambda e, o=out, a=in0, b=in1, p=op: e.tensor_tensor(o, a, b, p), reads=reads, writes=writes)

    def ts(self, eng, out, in0, s1, s2, op0, op1, reads, writes):
        if op1 is None:
            self.op(eng, lambda e, o=out, a=in0, s=s1, p=op0: e.tensor_scalar(o, a, s, None, p),
                    reads=reads, writes=writes)
        else:
            self.op(eng, lambda e, o=out, a=in0, s=s1, s_2=s2, p=op0, q=op1: e.tensor_scalar(o, a, s, s_2, p, q),
                    reads=reads, writes=writes)

    def stt(self, eng, out, in0, scalar, in1, op0, op1, reads, writes):
        self.op(eng, lambda e, o=out, a=in0, s=scalar, b=in1, p=op0, q=op1: e.scalar_tensor_tensor(o, a, s, b, p, q),
                reads=reads, writes=writes)

    def copy(self, eng, out, in_, reads, writes):
        if eng == "act":
            self.op(eng, lambda e, o=out, i=in_: e.copy(o, i), reads=reads, writes=writes)
        else:
            self.op(eng, lambda e, o=out, i=in_: e.tensor_copy(o, i), reads=reads, writes=writes)

    def memset(self, eng, t, ap, val):
        self.op(eng, lambda e, a=ap, v=val: e.memset(a, v), reads=[], writes=[t])


class Ctx:
    pass


def load_weight(P, C, dram_ap3, nk, ncols, dst, dst_tt, dst_key=None, q="sp", cast_eng="pool"):
    per = max(1, 2048 // ncols)
    k = 0
    while k < nk:
        kk = min(per, nk - k)
        st = C.stage[C.stage_i]
        C.stage_i = (C.stage_i + 1) % len(C.stage)
        sview = st[:, 0:kk * ncols].rearrange("p (k c) -> p k c", c=ncols)
        P.dma(q, sview, dram_ap3[:, k:k + kk, :], reads=[], writes=[st])
        P.copy(cast_eng, dst[:, k:k + kk, :], sview, reads=[st],
               writes=[(dst_tt, dst_key) if dst_key is not None else dst_tt])
        k += kk


def rmsnorm_to_bf16(P, C, xT, gcol, hT, scope):
    sq = [P.sb(scope, "sq", [128, CH], BF16) for _ in range(3)]
    rstd = [P.sb(scope, "rstd", [128, CH], F32) for _ in range(2)]
    xgs = [P.sb(scope, "xg", [128, CH], F32) for _ in range(2)]
    si = 0
    for n in range(NCH):
        cs = slice(n * CH, (n + 1) * CH)
        ps = P.ps()
        for k in range(8):
            s = sq[si % 3]
            si += 1
            P.act(s[:, :], xT[:, k, cs], AF.Square, reads=[(xT, (k, n))], writes=[s])
            P.mm(ps, ps[:, :], C.ones_mean[:, :], s[:, :], k == 0, k == 7, reads=[s, C.ones_mean])
        r = rstd[n % 2]
        P.act(r[:, :], ps[:, :], AF.Sqrt, reads=[ps, C.eps_col], writes=[r], bias=C.eps_col[:, 0:1])
        P.op("dve", lambda e, o=r[:, :]: e.reciprocal(o, o), reads=[r], writes=[r])
        for k in range(8):
            if k % 3 != 2:
                P.stt("dve", hT[:, k, cs], xT[:, k, cs], gcol[:, k:k + 1], r[:, :], ALU.mult, ALU.mult,
                      reads=[(xT, (k, n)), r, gcol], writes=[(hT, (k, n))])
            else:
                xg = xgs[(n * 8 + k) % 2]
                P.act(xg[:, :], xT[:, k, cs], AF.Copy, reads=[(xT, (k, n)), gcol], writes=[xg],
                      scale=gcol[:, k:k + 1])
                P.tt("pool", hT[:, k, cs], xg[:, :], r[:, :], ALU.mult, reads=[xg, r], writes=[(hT, (k, n))])


def phase_ffn(P, C, xT, gcol, wg, wu, wd):
    with contextlib.ExitStack() as scope:
        hT = P.sb(scope, "hT", [128, 8, T], BF16)
        rmsnorm_to_bf16(P, C, xT, gcol, hT, scope)
        actT = P.sb(scope, "actT", [128, 4, T], BF16)
        wgb = [P.sb(scope, "wgb", [128, 8, 512], BF16) for _ in range(2)]
        wub = [P.sb(scope, "wub", [128, 8, 512], BF16) for _ in range(2)]
        wdb = [P.sb(scope, "wdb", [128, 4, D], BF16) for _ in range(2)]
        sg = [P.sb(scope, "sg", [128, CH], F32) for _ in range(3)]
        wg3 = wg.rearrange("(k p) c -> p k c", p=128)
        wu3 = wu.rearrange("(k p) c -> p k c", p=128)
        wd3 = wd.rearrange("(k p) c -> p k c", p=128)
        blocks = []
        f0 = 0
        while f0 < DFF:
            fw = min(512, DFF - f0)
            blocks.append((f0, fw))
            f0 += fw

        def load(i):
            f0, fw = blocks[i]
            par = i % 2
            load_weight(P, C, wg3[:, :, f0:f0 + fw], 8, fw, wgb[par][:, :, 0:fw], wgb[par])
            load_weight(P, C, wu3[:, :, f0:f0 + fw], 8, fw, wub[par][:, :, 0:fw], wub[par])
            nm = fw // 128
            load_weight(P, C, wd3[:, f0 // 128:f0 // 128 + nm, :], nm, D, wdb[par][:, 0:nm, :], wdb[par])

        load(0)
        sgi = 0
        for i, (f0, fw) in enumerate(blocks):
            if i + 1 < len(blocks):
                load(i + 1)
            par = i % 2
            nm = fw // 128
            for n in range(NCH):
                for m in range(nm):
                    cs = slice(n * CH, (n + 1) * CH)
                    psg = P.ps()
                    psu = P.ps()
                    for k in range(8):
                        P.mm(psg, psg[:, :], wgb[par][:, k, m * 128:(m + 1) * 128], hT[:, k, cs], k == 0, k == 7,
                             reads=[wgb[par], (hT, (k, n))])
                    for k in range(8):
                        P.mm(psu, psu[:, :], wub[par][:, k, m * 128:(m + 1) * 128], hT[:, k, cs], k == 0, k == 7,
                             reads=[wub[par], (hT, (k, n))])
                    s = sg[sgi % 3]
                    sgi += 1
                    P.act(s[:, :], psg[:, :], AF.Silu, reads=[psg], writes=[s])
                    P.tt("dve", actT[:, m, cs], s[:, :], psu[:, :], ALU.mult, reads=[s, psu], writes=[(actT, (m, n))])
            for n in range(NCH):
                for j in range(8):
                    cs = slice(n * CH, (n + 1) * CH)
                    pso = P.ps()
                    for kk in range(nm):
                        P.mm(pso, pso[:, :], wdb[par][:, kk, j * 128:(j + 1) * 128], actT[:, kk, cs], kk == 0,
                             kk == nm - 1, reads=[wdb[par], (actT, (kk, n))])
                    P.stt("dve", xT[:, j, cs], pso[:, :], 0.5, xT[:, j, cs], ALU.mult, ALU.add,
                          reads=[pso, (xT, (j, n))], writes=[(xT, (j, n))])
        P.barrier()


def load_x(P, C, xT, x_d):
    with contextlib.ExitStack() as scope:
        xin = [P.sb(scope, "xin", [128, D], F32) for _ in range(4)]
        for b in range(T // 128):
            xi = xin[b % 4]
            P.dma("sp", xi[:, :], x_d[b * 128:(b + 1) * 128, :], reads=[], writes=[xi])
            for g in range(2):
                ps = P.ps()
                for kk in range(4):
                    k = g * 4 + kk
                    P.transpose(ps, ps[:, kk * 128:(kk + 1) * 128], xi[:, k * 128:(k + 1) * 128], C.ident[:, :],
                                reads=[xi, C.ident])
                eng = "dve" if g == 0 else "act"
                P.copy(eng, xT[:, g * 4:(g + 1) * 4, b * 128:(b + 1) * 128],
                       ps[:, :].rearrange("p (k c) -> p k c", c=128), reads=[ps],
                       writes=[(xT, (g * 4 + kk, b // 4)) for kk in range(4)])
        P.barrier()


def store_out(P, C, xT, gcol, out_d, do_norm=True):
    with contextlib.ExitStack() as scope:
        oT = [P.sb(scope, "oT", [128, 8, CH], F32) for _ in range(2)]
        otm = [P.sb(scope, "otm", [128, D], F32) for _ in range(4)]
        sq = [P.sb(scope, "sq", [128, CH], BF16) for _ in range(3)]
        rstd = [P.sb(scope, "rstd", [128, CH], F32) for _ in range(2)]
        si = 0
        for n in range(NCH):
            cs = slice(n * CH, (n + 1) * CH)
            o = oT[n % 2]
            if do_norm:
                ps = P.ps()
                for k in range(8):
                    s = sq[si % 3]
                    si += 1
                    P.act(s[:, :], xT[:, k, cs], AF.Square, reads=[(xT, (k, n))], writes=[s])
                    P.mm(ps, ps[:, :], C.ones_mean[:, :], s[:, :], k == 0, k == 7, reads=[s, C.ones_mean])
                r = rstd[n % 2]
                P.act(r[:, :], ps[:, :], AF.Sqrt, reads=[ps, C.eps_col], writes=[r], bias=C.eps_col[:, 0:1])
                P.op("dve", lambda e, o=r[:, :]: e.reciprocal(o, o), reads=[r], writes=[r])
                for k in range(8):
                    P.stt("dve", o[:, k, :], xT[:, k, cs], gcol[:, k:k + 1], r[:, :], ALU.mult, ALU.mult,
                          reads=[(xT, (k, n)), r, gcol], writes=[(o, k)])
            else:
                for k in range(8):
                    P.copy("dve", o[:, k, :], xT[:, k, cs], reads=[(xT, (k, n))], writes=[(o, k)])
            for bb in range(4):
                b = n * 4 + bb
                ot = otm[b % 4]
                for g in range(2):
                    ps = P.ps()
                    for kk in range(4):
                        k = g * 4 + kk
                        P.transpose(ps, ps[:, kk * 128:(kk + 1) * 128], o[:, k, bb * 128:(bb + 1) * 128],
                                    C.ident[:, :], reads=[(o, k), C.ident])
                    eng = "dve" if g == 0 else "act"
                    P.copy(eng, ot[:, g * 512:(g + 1) * 512], ps[:, :], reads=[ps], writes=[(ot, g)])
                P.dma("sp", out_d[b * 128:(b + 1) * 128, :], ot[:, :], reads=[ot], writes=[])
        P.barrier()


import math
C1_2PI = 6.28125
C2_2PI = 2.0 * math.pi - 6.28125
O_CX, O_CB, O_CC = 0, 256, 512
O_RR, O_RK, O_RV = 768, 1024, 1280
O_WDF, O_WDB, O_ADF, O_ADB, O_GD = 1536, 1600, 1664, 1728, 1792
O_TQ, O_TK, O_TV, O_TG = 1920, 2176, 2432, 2688
O_MQA, O_MKV, O_MKR = 2944, 3328, 3456
O_ROT = INW
NROT = 544
UROWS = INW + NROT


def load_rows(P, dst_tt, dst_ap, U, r0, nrows, q="sp"):
    P.dma(q, dst_ap, U[r0:r0 + nrows, :], reads=[], writes=[dst_tt])


def phase_inproj(P, C, xT, gcol, w_in, w_rot, U, VT):
    with contextlib.ExitStack() as scope:
        hT = P.sb(scope, "hT", [128, 8, T], BF16)
        rmsnorm_to_bf16(P, C, xT, gcol, hT, scope)
        wb = [P.sb(scope, "wb", [128, 8, 512], BF16) for _ in range(2)]
        ev = [P.sb(scope, "ev", [128, CH], F32) for _ in range(4)]
        evi = 0
        jobs = []
        for (w, ncols, row0) in ((w_in, INW, 0), (w_rot, NROT, O_ROT)):
            w3 = w.rearrange("(k p) c -> p k c", p=128)
            c0 = 0
            while c0 < ncols:
                cw = min(512, ncols - c0)
                jobs.append((w3, c0, cw, row0))
                c0 += cw

        def load(i):
            w3, c0, cw, row0 = jobs[i]
            load_weight(P, C, w3[:, :, c0:c0 + cw], 8, cw, wb[i % 2][:, :, 0:cw], wb[i % 2])

        load(0)
        for i, (w3, c0, cw, row0) in enumerate(jobs):
            if i + 1 < len(jobs):
                load(i + 1)
            b = wb[i % 2]
            for n in range(NCH):
                cs = slice(n * CH, (n + 1) * CH)
                m0 = 0
                while m0 < cw:
                    mw = min(128, cw - m0)
                    gcol_ = c0 + m0
                    if row0 == 0 and (O_RV <= gcol_ < O_RV + 256 or O_TV <= gcol_ < O_TV + 256):
                        m0 += mw
                        continue
                    ps = P.ps()
                    for k in range(8):
                        P.mm(ps, ps[0:mw, :], b[:, k, m0:m0 + mw], hT[:, k, cs], k == 0, k == 7,
                             reads=[b, (hT, (k, n))])
                    e = ev[evi % 4]
                    evi += 1
                    P.copy("act" if evi % 2 else "dve", e[0:mw, :], ps[0:mw, :], reads=[ps], writes=[e])
                    P.dma("sp", U[row0 + c0 + m0:row0 + c0 + m0 + mw, cs], e[0:mw, :], reads=[e], writes=[])
                    m0 += mw
        w3 = w_in.rearrange("(k p) c -> p k c", p=128)
        vb = wb[len(jobs) % 2]
        load_weight(P, C, w3[:, :, O_RV:O_RV + 256], 8, 256, vb[:, :, 0:256], vb, dst_key="a")
        load_weight(P, C, w3[:, :, O_TV:O_TV + 256], 8, 256, vb[:, :, 256:512], vb, dst_key="b")
        for b in range(T // 128):
            ps = P.ps()
            for k in range(8):
                P.mm(ps, ps[:, :], hT[:, k, b * 128:(b + 1) * 128], vb[:, k, :], k == 0, k == 7,
                     reads=[vb, (hT, (k, b // 4))])
            e = ev[evi % 4]
            evi += 1
            P.copy("act" if evi % 2 else "dve", e[:, :], ps[:, :], reads=[ps], writes=[e])
            P.dma("sp", VT[b * 128:(b + 1) * 128, :], e[:, :], reads=[e], writes=[])
        P.barrier()


def mixer_conv(P, C, l, dr, U, yT):
    with contextlib.ExitStack() as scope:
        cw = P.sb(scope, "cw", [128, 2, 3], F32)
        for ct in range(2):
            P.dma("sp", cw[:, ct, :], dr["conv_w"][l][:, ct * 128:(ct + 1) * 128].rearrange("j p -> p j"),
                  reads=[], writes=[cw], slow=True)
        sets = [[P.sb(scope, nm, [128, T + (2 if nm == "cu" else 0)], F32) for nm in ("cxs", "cbs", "ccs", "cu", "cacc")]
                for _ in range(2)]
        for ct in range(2):
            xs, bs, cs_, u, acc = sets[ct]
            load_rows(P, xs, xs[:, :], U, O_CX + ct * 128, 128)
            load_rows(P, bs, bs[:, :], U, O_CB + ct * 128, 128)
            load_rows(P, cs_, cs_[:, :], U, O_CC + ct * 128, 128)
        for ct in range(2):
            xs, bs, cs_, u, acc = sets[ct]
            P.memset("pool", u, u[:, :], 0.0)
            P.tt("dve" if ct == 0 else "pool", u[:, 1:T + 1], cs_[:, :], xs[:, :], ALU.mult, reads=[cs_, xs], writes=[u])
            P.ts("dve", acc[:, :], u[:, 0:T], cw[:, ct, 0:1], None, ALU.mult, None, reads=[u, cw], writes=[acc])
            P.stt("dve", acc[:, :], u[:, 1:T + 1], cw[:, ct, 1:2], acc[:, :], ALU.mult, ALU.add,
                  reads=[u, cw, acc], writes=[acc])
            P.stt("dve", acc[:, :], u[:, 2:T + 2], cw[:, ct, 2:3], acc[:, :], ALU.mult, ALU.add,
                  reads=[u, cw, acc], writes=[acc])
            P.tt("dve" if ct == 0 else "pool", yT[:, ct, :], acc[:, :], bs[:, :], ALU.mult, reads=[acc, bs],
                 writes=[(yT, ct)])
        P.barrier()


def rope_tables(P, C, scope, pos_d, inv_ap, sgn_ap):
    cos = P.sb(scope, "cos", [128, T], F32)
    sinS = P.sb(scope, "sinS", [128, T], F32)
    with contextlib.ExitStack() as tmp:
        A = P.sb(tmp, "rpA", [128, T], I32)
        B = P.sb(tmp, "rpB", [128, T], F32)
        Cc = P.sb(tmp, "rpC", [128, T], F32)
        P.dma("sp", A[:, :], pos_d.partition_broadcast(128), reads=[], writes=[A])
        P.copy("dve", B[:, :], A[:, :], reads=[A], writes=[B])
        P.ts("dve", B[:, :], B[:, :], inv_ap, None, ALU.mult, None, reads=[B, C.consts], writes=[B])
        P.ts("dve", Cc[:, :], B[:, :], 1.0 / (2.0 * math.pi), None, ALU.mult, None, reads=[B], writes=[Cc])
        P.copy("dve", A[:, :], Cc[:, :], reads=[Cc], writes=[A])
        P.copy("dve", Cc[:, :], A[:, :], reads=[A], writes=[Cc])
        P.stt("dve", B[:, :], Cc[:, :], -C1_2PI, B[:, :], ALU.mult, ALU.add, reads=[Cc, B], writes=[B])
        P.stt("dve", B[:, :], Cc[:, :], -C2_2PI, B[:, :], ALU.mult, ALU.add, reads=[Cc, B], writes=[B])
        P.ts("dve", Cc[:, :], B[:, :], math.pi, None, ALU.is_gt, None, reads=[B], writes=[Cc])
        P.stt("dve", B[:, :], Cc[:, :], -2.0 * math.pi, B[:, :], ALU.mult, ALU.add, reads=[Cc, B], writes=[B])
        P.act(sinS[:, :], B[:, :], AF.Sin, reads=[B, C.consts], writes=[sinS], scale=sgn_ap)
        P.ts("dve", B[:, :], B[:, :], 0.5 * math.pi, None, ALU.add, None, reads=[B], writes=[B])
        P.ts("dve", Cc[:, :], B[:, :], math.pi, None, ALU.is_gt, None, reads=[B], writes=[Cc])
        P.stt("dve", B[:, :], Cc[:, :], -2.0 * math.pi, B[:, :], ALU.mult, ALU.add, reads=[Cc, B], writes=[B])
        P.act(cos[:, :], B[:, :], AF.Sin, reads=[B], writes=[cos])
        P.barrier()
    return cos, sinS


def rope_load(P, C, scope, R, idx):
    cos = P.sb(scope, "cos", [128, T], F32)
    sinS = P.sb(scope, "sinS", [128, T], F32)
    P.dma("sp", cos[:, :], R[idx], reads=[], writes=[cos])
    P.dma("sp", sinS[:, :], R[idx + 1], reads=[], writes=[sinS])
    return cos, sinS


def rstd_from_sum(P, C, r_ap, ps_ap, r_tt, ps_tt, scale, plo, phi):
    P.act(r_ap, ps_ap, AF.Sqrt, reads=[ps_tt, C.eps_col], writes=[r_tt], bias=C.eps_col[plo:phi, 0:1], scale=scale)
    P.op("dve", lambda e, o=r_ap: e.reciprocal(o, o), reads=[r_tt], writes=[r_tt])


def mixer_ret(P, C, l, dr, U, VT, yT):
    lng = [math.log(1.0 - 2.0 ** (-5.0 - h)) for h in range(4)]
    with contextlib.ExitStack() as scope:
        qT = P.sb(scope, "rqT", [128, 2, T], BF16)
        kT = P.sb(scope, "rkT", [128, 2, T], BF16)
        gng = P.sb(scope, "gng", [128, 2], F32)
        P.dma("sp", gng[:, :], dr["ret_gn_g"][l].rearrange("(t p) -> p t", p=128), reads=[], writes=[gng], slow=True)
        vtm = P.sb(scope, "rvtm", [128, 16, 256], BF16)
        NM = 3968
        with contextlib.ExitStack() as sa:
            vsts = [P.sb(sa, "rvst", [128, 4, 256], F32) for _ in range(1)]
            cos, sinS = rope_load(P, C, sa, C.ROPE, 0)
            abuf = [P.sb(sa, "ra", [128, T], F32) for _ in range(2)]
            bbuf = [P.sb(sa, "rb", [128, T], F32) for _ in range(2)]
            for (dst, o_main, o_rot, scale) in ((qT, O_TQ, O_ROT, 1.0), (kT, O_TK, O_ROT + 256, 0.125)):
                for hp in range(2):
                    a = abuf[hp]
                    b = bbuf[hp]
                    load_rows(P, a, a[:, :], U, o_main + hp * 128, 128)
                    load_rows(P, b, b[:, :], U, o_rot + hp * 128, 128)
                    P.stt("dve", a[:, :], a[:, :], scale, cos[:, :], ALU.mult, ALU.mult, reads=[a, cos], writes=[a])
                    P.stt("dve", b[:, :], b[:, :], scale, sinS[:, :], ALU.mult, ALU.mult, reads=[b, sinS], writes=[b])
                    P.tt("pool", dst[:, hp, :], a[:, :], b[:, :], ALU.add, reads=[a, b], writes=[(dst, hp)])
            for half in range(4):
                vs_ = vsts[0]
                P.dma("sp", vs_[:, :, :],
                      VT[half * 512:(half + 1) * 512, 256:512].rearrange("(b p) c -> p b c", p=128),
                      reads=[], writes=[vs_])
                P.copy("pool", vtm[:, half * 4:(half + 1) * 4, :], vs_[:, :, :], reads=[vs_], writes=[(vtm, half)])
            P.barrier()
        with contextlib.ExitStack() as sb_:
            Dt = P.sb(sb_, "rD", [128, NM], F32)
            P.op("pool", lambda e, Dt=Dt, NM=NM: e.iota(Dt[:, :], [[1, NM]], base=-1920, channel_multiplier=-1,
                                          allow_small_or_imprecise_dtypes=True), reads=[], writes=[Dt])
            P.act(Dt[:, :], Dt[:, :], AF.Abs, reads=[Dt], writes=[Dt])
            mask = P.sb(sb_, "rmask", [128, NM], F32)
            sg = P.sb(sb_, "rsg", [128, T], F32)
            sm = [P.sb(sb_, "rsm", [128, CH], BF16) for _ in range(7)]
            sq = [P.sb(sb_, "rsq", [128, CH], BF16) for _ in range(2)]
            scb = [P.sb(sb_, "rscb", [128, CH], F32) for _ in range(2)]
            rs = [P.sb(sb_, "rrs", [128, CH], F32) for _ in range(2)]
            tmp = [P.sb(sb_, "rtmp", [128, CH], F32) for _ in range(2)]
            smi = 0
            it = 0
            for h in range(4):
                hp, pb = h // 2, (h % 2) * 64
                pr = slice(pb, pb + 64)
                if h % 2 == 0:
                    load_rows(P, sg, sg[:, :], U, O_TG + hp * 128, 128)
                    P.act(sg[:, :], sg[:, :], AF.Silu, reads=[sg], writes=[sg])
                P.act(mask[:, :], Dt[:, :], AF.Exp, reads=[Dt], writes=[mask], scale=lng[h])
                LA = 3
                pend = []

                def stage_c(n, j, m_, it_):
                    cs = slice(n * CH, (n + 1) * CH)
                    num = P.acc[it_ % 2]
                    P.mm(num, num[:, :], vtm[:, j, hp * 128:(hp + 1) * 128], m_[:, :], j == 0, j == 15,
                         reads=[vtm, m_])
                    if j != 15:
                        return
                    q_ = sq[it_ % 2]
                    P.act(q_[pr, :], num[pr, :], AF.Square, reads=[num], writes=[q_])
                    st = P.ps()
                    P.mm(st, st[:, :], C.ones64[pr, :], q_[pr, :], True, True, reads=[C.ones64, q_])
                    r = rs[it_ % 2]
                    rstd_from_sum(P, C, r[pr, :], st[pr, :], r, st, 1.0, pb, pb + 64)
                    t_ = tmp[it_ % 2]
                    P.stt("dve", t_[pr, :], num[pr, :], gng[pr, hp:hp + 1], r[pr, :], ALU.mult, ALU.mult,
                          reads=[num, gng, r], writes=[t_])
                    P.tt("dve", yT[pr, hp, cs], t_[pr, :], sg[pr, cs], ALU.mult, reads=[t_, sg],
                         writes=[(yT, (hp, n, pb))])

                G = 3
                steps = [(n, j) for n in range(NCH) for j in range(16)]
                prev = []
                for g0 in range(0, len(steps), G):
                    grp = steps[g0:g0 + G]
                    sps = []
                    for (n, j) in grp:
                        cs = slice(n * CH, (n + 1) * CH)
                        sp = P.ps()
                        P.mm(sp, sp[:, :], kT[pr, hp, j * 128:(j + 1) * 128], qT[pr, hp, cs], True, True,
                             reads=[kT, qT])
                        sps.append(sp)
                    cur = []
                    for (n, j), sp in zip(grp, sps):
                        off = n * CH - j * 128 + 1920
                        m_ = sm[smi % len(sm)]
                        smi += 1
                        if smi % 3 == 0:
                            sc_ = scb[(smi // 3) % 2]
                            P.copy("act", sc_[:, :], sp[:, :], reads=[sp], writes=[sc_])
                            P.tt("pool", m_[:, :], sc_[:, :], mask[:, off:off + CH], ALU.mult, reads=[sc_, mask],
                                 writes=[m_])
                        else:
                            P.tt("dve", m_[:, :], sp[:, :], mask[:, off:off + CH], ALU.mult, reads=[sp, mask],
                                 writes=[m_])
                        cur.append((n, j, m_, it + n))
                    for a_ in prev:
                        stage_c(*a_)
                    prev = cur
                for a_ in prev:
                    stage_c(*a_)
                it += NCH
            P.barrier()


def mixer_mla(P, C, l, dr, U, yT):
    with contextlib.ExitStack() as scope:
        qfull = P.sb(scope, "mqf", [128, 4, T], BF16)
        kfull = P.sb(scope, "mkf", [128, 4, T], BF16)
        qb = P.sb(scope, "mqb", [128, 3, 384], BF16)
        qbr = P.sb(scope, "mqbr", [128, 3, 384], BF16)
        kvb = P.sb(scope, "mkvb", [128, 1, 512], BF16)
        load_weight(P, C, dr["mla_q_b"][l].rearrange("(k p) c -> p k c", p=128), 3, 384, qb[:, :, :], qb)
        load_weight(P, C, dr["q_b_rot"][l].rearrange("(k p) c -> p k c", p=128), 3, 384, qbr[:, :, :], qbr)
        load_weight(P, C, dr["mla_kv_b"][l].rearrange("(k p) c -> p k c", p=128), 1, 512, kvb[:, :, :], kvb)
        qan = P.sb(scope, "mqan", [128, 3], F32)
        kvan = P.sb(scope, "mkvan", [128, 1], F32)
        P.dma("sp", qan[:, :], dr["mla_q_a_norm"][l].rearrange("(k p) -> p k", p=128), reads=[], writes=[qan], slow=True)
        P.dma("sp", kvan[:, :], dr["mla_kv_a_norm"][l].rearrange("(k p) -> p k", p=128), reads=[], writes=[kvan],
              slow=True)
        sqb = [P.sb(scope, "msq", [128, CH], BF16) for _ in range(3)]
        rr = [P.sb(scope, "mrr", [128, CH], F32) for _ in range(2)]
        sqi = 0
        with contextlib.ExitStack() as sa:
            cos, sinS = rope_load(P, C, sa, C.ROPE, 2)
            qn = P.sb(sa, "mqn", [128, 3, T], BF16)
            with contextlib.ExitStack() as sq_:
                qas = [P.sb(sq_, "mqa", [128, 3, CH], F32) for _ in range(2)]
                for n in range(NCH):
                    cs = slice(n * CH, (n + 1) * CH)
                    qa = qas[n % 2]
                    P.dma("sp", qa[:, :, :], U[O_MQA:O_MQA + 384, cs].rearrange("(k p) c -> p k c", p=128),
                          reads=[], writes=[qa])
                    ps = P.ps()
                    for k in range(3):
                        s_ = sqb[sqi % 3]
                        sqi += 1
                        P.act(s_[:, :], qa[:, k, :], AF.Square, reads=[qa], writes=[s_])
                        P.mm(ps, ps[:, :], C.ones_bf[:, :], s_[:, :], k == 0, k == 2, reads=[C.ones_bf, s_])
                    r = rr[n % 2]
                    rstd_from_sum(P, C, r[:, :], ps[:, :], r, ps, 1.0 / 384.0, 0, 128)
                    for k in range(3):
                        P.stt("dve", qn[:, k, cs], qa[:, k, :], qan[:, k:k + 1], r[:, :], ALU.mult, ALU.mult,
                              reads=[qa, qan, r], writes=[(qn, (k, n))])
                P.barrier()
            t1 = [P.sb(sa, "mt1", [128, CH], F32) for _ in range(2)]
            t2 = [P.sb(sa, "mt2", [128, CH], F32) for _ in range(2)]
            ti = 0
            rp = slice(64, 96)
            for h in range(4):
                for n in range(NCH):
                    cs = slice(n * CH, (n + 1) * CH)
                    ps = P.ps()
                    psr = P.ps()
                    for k in range(3):
                        P.mm(ps, ps[0:96, :], qb[:, k, h * 96:(h + 1) * 96], qn[:, k, cs], k == 0, k == 2,
                             reads=[qb, (qn, (k, n))])
                    for k in range(3):
                        P.mm(psr, psr[0:96, :], qbr[:, k, h * 96:(h + 1) * 96], qn[:, k, cs], k == 0, k == 2,
                             reads=[qbr, (qn, (k, n))])
                    P.copy("act", qfull[0:64, h, cs], ps[0:64, :], reads=[ps], writes=[(qfull, (h, n, 0))])
                    a_ = t1[ti % 2]
                    b_ = t2[ti % 2]
                    ti += 1
                    P.tt("dve", a_[rp, :], ps[rp, :], cos[rp, cs], ALU.mult, reads=[ps, cos], writes=[a_])
                    P.tt("dve", b_[rp, :], psr[rp, :], sinS[rp, cs], ALU.mult, reads=[psr, sinS], writes=[b_])
                    P.tt("pool", qfull[rp, h, cs], a_[rp, :], b_[rp, :], ALU.add, reads=[a_, b_],
                         writes=[(qfull, (h, n, 1))])
            for n in range(NCH):
                cs = slice(n * CH, (n + 1) * CH)
                a_ = t1[ti % 2]
                b_ = t2[ti % 2]
                ti += 1
                P.dma("sp", a_[rp, :], U[O_MKR:O_MKR + 32, cs], reads=[], writes=[a_])
                P.dma("sp", b_[rp, :], U[O_ROT + 512:O_ROT + 544, cs], reads=[], writes=[b_])
                P.tt("dve", a_[rp, :], a_[rp, :], cos[rp, cs], ALU.mult, reads=[a_, cos], writes=[a_])
                P.tt("dve", b_[rp, :], b_[rp, :], sinS[rp, cs], ALU.mult, reads=[b_, sinS], writes=[b_])
                for h in range(4):
                    P.tt("pool", kfull[rp, h, cs], a_[rp, :], b_[rp, :], ALU.add, reads=[a_, b_],
                         writes=[(kfull, (h, 1, n))])
            P.barrier()
        vtm = P.sb(scope, "mvtm", [128, 16, 256], BF16)
        with contextlib.ExitStack() as sk:
            ckv = P.sb(sk, "mckv", [128, T], F32)
            load_rows(P, ckv, ckv[:, :], U, O_MKV, 128)
            kvn = P.sb(sk, "mkvn", [128, T], BF16)
            for n in range(NCH):
                cs = slice(n * CH, (n + 1) * CH)
                ps = P.ps()
                s_ = sqb[sqi % 3]
                sqi += 1
                P.act(s_[:, :], ckv[:, cs], AF.Square, reads=[ckv], writes=[s_])
                P.mm(ps, ps[:, :], C.ones_bf[:, :], s_[:, :], True, True, reads=[C.ones_bf, s_])
                r = rr[n % 2]
                rstd_from_sum(P, C, r[:, :], ps[:, :], r, ps, 1.0 / 128.0, 0, 128)
                P.stt("dve", kvn[:, cs], ckv[:, cs], kvan[:, 0:1], r[:, :], ALU.mult, ALU.mult,
                      reads=[ckv, kvan, r], writes=[(kvn, n)])
            ci = 0
            for h in range(4):
                for n in range(NCH):
                    cs = slice(n * CH, (n + 1) * CH)
                    ps = P.ps()
                    P.mm(ps, ps[0:64, :], kvb[:, 0, h * 128:h * 128 + 64], kvn[:, cs], True, True,
                         reads=[kvb, (kvn, n)])
                    ci += 1
                    P.copy("act" if ci % 2 else "dve", kfull[0:64, h, cs], ps[0:64, :], reads=[ps],
                           writes=[(kfull, (h, 0, n))])
            for b in range(T // 128):
                ps = P.ps()
                P.mm(ps, ps[:, :], kvn[:, b * 128:(b + 1) * 128], kvb[:, 0, :], True, True, reads=[kvb, (kvn, b // 4)])
                ci += 1
                P.copy("act" if ci % 2 else "dve", vtm[:, b, :].rearrange("p (h d) -> p h d", d=64),
                       ps[:, :].rearrange("p (h e) -> p h e", e=128)[:, :, 64:128], reads=[ps], writes=[(vtm, b)])
            P.barrier()
        with contextlib.ExitStack() as st_:
            pt = [P.sb(st_, "mp", [128, CH], BF16) for _ in range(7)]
            rd = [P.sb(st_, "mrd", [128, CH], F32) for _ in range(2)]
            pi_ = 0
            it = 0
            sc = 96.0 ** -0.5
            for h in range(4):
                hp, pb = h // 2, (h % 2) * 64
                pr = slice(pb, pb + 64)
                LA = 3
                pend = []

                def stage_c(n, j, p_, it_):
                    cs = slice(n * CH, (n + 1) * CH)
                    num = P.acc[0]
                    den = P.acc[1]
                    P.mm(num, num[:, :], vtm[:, j, hp * 128:(hp + 1) * 128], p_[:, :], j == 0, j == 15,
                         reads=[vtm, p_])
                    P.mm(den, den[:, :], C.ones_bf[:, :], p_[:, :], j == 0, j == 15, reads=[C.ones_bf, p_])
                    if j != 15:
                        return
                    r = rd[it_ % 2]
                    P.op("dve", lambda e, o=r[pr, :], i=den[pr, :]: e.reciprocal(o, i), reads=[den], writes=[r])
                    P.tt("dve", yT[pr, hp, cs], num[pr, :], r[pr, :], ALU.mult, reads=[num, r],
                         writes=[(yT, (hp, n, pb))])

                G = 3
                steps = [(n, j) for n in range(NCH) for j in range(16)]
                prev = []
                for g0 in range(0, len(steps), G):
                    grp = steps[g0:g0 + G]
                    sps = []
                    for (n, j) in grp:
                        cs = slice(n * CH, (n + 1) * CH)
                        sp = P.ps()
                        P.mm(sp, sp[:, :], kfull[0:96, h, j * 128:(j + 1) * 128], qfull[0:96, h, cs], True, True,
                             reads=[kfull, qfull])
                        sps.append(sp)
                    cur = []
                    for (n, j), sp in zip(grp, sps):
                        p_ = pt[pi_ % len(pt)]
                        pi_ += 1
                        P.act(p_[:, :], sp[:, :], AF.Exp, reads=[sp], writes=[p_], scale=sc)
                        cur.append((n, j, p_, it + n))
                    for a_ in prev:
                        stage_c(*a_)
                    prev = cur
                for a_ in prev:
                    stage_c(*a_)
                it += NCH
            P.barrier()


            P.barrier()


AX = mybir.AxisListType
DEBUG_RW = None
DEBUG_CORES = None
DEBUG_TRACE = False
DEBUG_LAYERS = None


def mixer_rwkv(P, C, l, dr, U, VT, yT):
    NEG_E = -math.exp(-0.5)
    dbg = DEBUG_RW if l == 1 else None
    saved_ps = P.psums
    P.psums = saved_ps + P.acc
    P.psi = 0
    try:
        _mixer_rwkv(P, C, l, dr, U, VT, yT, dbg, NEG_E)
    finally:
        P.psums = saved_ps
        P.psi = 0


def _mixer_rwkv(P, C, l, dr, U, VT, yT, dbg, NEG_E):
    NB = T // 128
    with contextlib.ExitStack() as scope:
        cols = P.sb(scope, "wcols", [128, 2, 8], F32)
        for i, nm in enumerate(["rwkv_w0_f", "rwkv_w0_b", "rwkv_a0_f", "rwkv_a0_b", "rwkv_k_k", "rwkv_k_a"]):
            P.dma("sp", cols[:, :, i], dr[nm][l].rearrange("(t p) -> p t", p=128), reads=[], writes=[cols], slow=True)
        P.dma("sp", cols[:, :, 7], dr["rwkv_r_k"][l].rearrange("h n -> (h n)").rearrange("(t p) -> p t", p=128),
              reads=[], writes=[cols], slow=True)
        P.ts("dve", cols[:, :, 6], cols[:, :, 5], -1.0, 1.0, ALU.mult, ALU.add, reads=[cols], writes=[cols])
        gneps = P.sb(scope, "gneps", [128, 1], F32)
        P.memset("pool", gneps, gneps[:, :], 64e-5)
        lw = P.sb(scope, "lw", [128, 4, 256], BF16)
        g2b = P.sb(scope, "g2b", [128, 256], BF16)
        Lg = P.sb(scope, "Lg", [128, 256], F32)
        Lb = P.sb(scope, "Lb", [128, 256], F32)
        vtm = P.sb(scope, "wvtm", [128, NB, 256], BF16)
        with contextlib.ExitStack() as s0:
            lst = P.sb(s0, "lst", [128, 256], F32)
            for i, nm in enumerate(["rwkv_w2_f", "rwkv_w2_b", "rwkv_a2_f", "rwkv_a2_b"]):
                lp = slice(0, 64) if i % 2 == 0 else slice(64, 128)
                P.dma("sp", lst[lp, :], dr[nm][l], reads=[], writes=[lst])
                P.copy("dve", lw[lp, i, :], lst[lp, :], reads=[lst], writes=[(lw, i)])
            P.dma("sp", lst[:, :], dr["rwkv_g2"][l], reads=[], writes=[lst])
            P.copy("dve", g2b[:, :], lst[:, :], reads=[lst], writes=[g2b])
            P.dma("sp", Lg[:, :], dr["rwkv_lnx_g"][l].rearrange("(o c) -> o c", o=1).partition_broadcast(128),
                  reads=[], writes=[Lg])
            P.dma("sp", Lb[:, :], dr["rwkv_lnx_b"][l].rearrange("(o c) -> o c", o=1).partition_broadcast(128),
                  reads=[], writes=[Lb])
            vst = P.sb(s0, "wvst", [128, 4, 256], F32)
            for q4 in range(4):
                P.dma("sp", vst[:, :, :], VT[q4 * 512:(q4 + 1) * 512, 0:256].rearrange("(b p) c -> p b c", p=128),
                      reads=[], writes=[vst])
                P.copy("pool", vtm[:, q4 * 4:(q4 + 1) * 4, :], vst[:, :, :], reads=[vst], writes=[(vtm, q4)])
            P.barrier()
        MS = [P.sb(scope, "MS", [128, 256], BF16) for _ in range(2)]
        NMS = [P.sb(scope, "NMS", [128, 256], BF16) for _ in range(2)]
        for di in range(2):
            pat = [[1, 128]] if di == 0 else [[-1, 128]]
            cm = -1 if di == 0 else 1
            for j, cmp_ in enumerate((ALU.is_gt, ALU.is_ge)):
                P.op("pool", lambda e, o=MS[di][:, j * 128:(j + 1) * 128], pt=pat, c_=cmp_, m_=cm:
                     e.affine_select(o, C.ones_f[:, :], pt, c_, 0.0, base=0, channel_multiplier=m_),
                     reads=[C.ones_f], writes=[(MS[di], j)])
            P.ts("dve", NMS[di][:, :], MS[di][:, :], -1.0, None, ALU.mult, None, reads=[MS[di]], writes=[NMS[di]])
        smask = P.sb(scope, "smask", [128, CH], F32)
        P.memset("pool", smask, smask[:, :], 1.0)
        P.memset("pool", smask, smask[:, :].rearrange("p (c t) -> p c t", t=128)[:, :, 0:1], 0.0)
        identb = P.sb(scope, "identb", [128, 128], BF16)
        P.copy("dve", identb[:, :], C.ident[:, :], reads=[C.ident], writes=[identb])
        blk64 = P.sb(scope, "blk64", [128, 128], BF16)
        P.memset("pool", blk64, blk64[:, :], 0.0)
        P.memset("pool", blk64, blk64[0:64, 0:64], 1.0)
        P.memset("pool", blk64, blk64[64:128, 64:128], 1.0)
        blk2 = P.sb(scope, "blk2", [128, 2], BF16)
        P.memset("pool", blk2, blk2[:, :], 0.0)
        P.memset("pool", blk2, blk2[0:64, 0:1], 1.0)
        P.memset("pool", blk2, blk2[64:128, 1:2], 1.0)
        QpT = [P.sb(scope, "QpT", [128, NB, 128], BF16) for _ in range(2)]
        MpT = [P.sb(scope, "MpT", [128, NB, 128], BF16) for _ in range(2)]
        Gs = [P.sb(scope, "Gs", [128, NB, 64], F32) for _ in range(2)]
        pC = [P.sb(scope, "pC", [128, NB], F32) for _ in range(2)]
        yacc = P.sb(scope, "yacc", [128, NB, 128], F32)
        bon = P.sb(scope, "bon", [128, NB, 2], F32)
        for di in range(2):
            P.memset("pool", MpT[di], MpT[di][:, :, :], 0.0)
        P.barrier()
        v4 = lambda ap: ap.rearrange("p (c t) -> p c t", t=128)
        if dbg == "setup":
            return

        for hp in range(2):
            if dbg == "s1h0" and hp == 1:
                continue
            with contextlib.ExitStack() as sw:
                f32t = lambda nm: P.sb(sw, nm, [128, CH], F32)
                b16t = lambda nm: P.sb(sw, nm, [128, CH], BF16)
                kkf, nrm, af, t_ = f32t("kkf"), f32t("nrm"), f32t("af"), f32t("t_")
                adb = P.sb(sw, "adb", [128, CH], BF16)

                def issue_loads(n_):
                    st = C.stage[n_ % 2]
                    cs_ = slice(n_ * CH, (n_ + 1) * CH)
                    P.dma("sp", st[:, 0:512], U[O_RR + hp * 128:O_RR + hp * 128 + 128, cs_], reads=[],
                          writes=[(st, "r")])
                    P.dma("sp", st[:, 512:1024], U[O_RK + hp * 128:O_RK + hp * 128 + 128, cs_], reads=[],
                          writes=[(st, "k")])
                    P.dma("sp", st[:, 1024:1536], U[O_WDF:O_WDF + 128, cs_], reads=[], writes=[(st, "wd")])
                    P.dma("sp", st[:, 1536:2048], U[O_ADF:O_ADF + 128, cs_], reads=[], writes=[(st, "ad")])

                issue_loads(0)
                ld = [f32t("ld") for _ in range(2)]
                sq, thb, prod, rb, kkb = b16t("sq"), b16t("thb"), b16t("prod"), b16t("rb"), b16t("kkb")
                kd = [b16t("kd") for _ in range(2)]
                bb = [b16t("bb") for _ in range(2)]
                A = [f32t("A%d" % i) for i in range(4)]
                BSET = [(P.sb(sw, "opkr", [128, 4, 256], BF16), b16t("Kh"), b16t("Bh"), b16t("KhC"), b16t("nBhC"))
                        for _ in range(2)]
                if len(P.psums) == 8:
                    psA_bank = P.psums[-1]
                    P.psums = P.psums[:-1]
                    P.psi = 0
                psA = psA_bank
                NQ = 8
                AR = [P.sb(sw, "AR", [128, 256], BF16) for _ in range(NQ)]
                BR = [P.sb(sw, "BR", [128, 256], BF16) for _ in range(NQ)]
                MZ = [[P.sb(sw, "MZ", [128, 384], BF16) for _ in range(2)] for _ in range(NQ)]
                KBC = [P.sb(sw, "KBC", [128, 2, 128], BF16) for _ in range(NQ)]
                for qi in range(NQ):
                    P.memset("pool", KBC[qi], KBC[qi][:, :, :], 0.0)
                def phase_a(n, PP):
                    cs = slice(n * CH, (n + 1) * CH)
                    if n + 1 < NCH:
                        issue_loads(n + 1)
                    st = C.stage[n % 2]
                    rc_ap, kc_ap, wd_ap, ad_ap = st[:, 0:512], st[:, 512:1024], st[:, 1024:1536], st[:, 1536:2048]
                    PP.copy("pool", rb[:, :], rc_ap, reads=[(st, "r")], writes=[rb])
                    PP.ts("dve", kkf[:, :], kc_ap, cols[:, hp, 4:5], None, ALU.mult, None, reads=[(st, "k"), cols],
                         writes=[kkf])
                    PP.act(sq[:, :], kkf[:, :], AF.Square, reads=[kkf], writes=[sq])
                    ps = PP.ps()
                    PP.mm(ps, ps[:, :], blk64[:, :], sq[:, :], True, True, reads=[blk64, sq])
                    PP.act(nrm[:, :], ps[:, :], AF.Sqrt, reads=[ps], writes=[nrm])
                    PP.ts("dve", nrm[:, :], nrm[:, :], 1e-12, None, ALU.max, None, reads=[nrm], writes=[nrm])
                    PP.op("dve", lambda e, o=nrm[:, :]: e.reciprocal(o, o), reads=[nrm], writes=[nrm])
                    PP.tt("dve", kkf[:, :], kkf[:, :], nrm[:, :], ALU.mult, reads=[kkf, nrm], writes=[kkf])
                    PP.copy("pool", kkb[:, :], kkf[:, :], reads=[kkf], writes=[kkb])
                    PP.act(thb[:, :], wd_ap, AF.Tanh, reads=[(st, "wd")], writes=[thb])
                    PP.copy("pool", adb[:, :], ad_ap, reads=[(st, "ad")], writes=[adb])
                    for di in range(2):
                        lp = slice(0, 64) if di == 0 else slice(64, 128)
                        ps = PP.ps()
                        PP.mm(ps, ps[:, :], lw[lp, di, hp * 128:(hp + 1) * 128], thb[lp, :], True, True,
                             reads=[lw, thb])
                        PP.act(ld[di][:, :], ps[:, :], AF.Sigmoid, reads=[ps, cols], writes=[ld[di]],
                              bias=cols[:, hp, di:di + 1])
                        PP.ts("dve", ld[di][:, :], ld[di][:, :], NEG_E, None, ALU.mult, None, reads=[ld[di]],
                             writes=[ld[di]])
                        ps2 = PP.ps()
                        PP.mm(ps2, ps2[:, :], lw[lp, 2 + di, hp * 128:(hp + 1) * 128], adb[lp, :], True, True,
                             reads=[lw, adb])
                        PP.act(af[:, :], ps2[:, :], AF.Sigmoid, reads=[ps2, cols], writes=[af],
                              bias=cols[:, hp, 2 + di:3 + di])
                        PP.ts("dve", t_[:, :], af[:, :], cols[:, hp, 5:6], cols[:, hp, 6:7], ALU.mult, ALU.add,
                             reads=[af, cols], writes=[t_])
                        PP.tt("dve", kd[di][:, :], t_[:, :], kc_ap, ALU.mult, reads=[t_, (st, "k")], writes=[kd[di]])
                        PP.tt("pool", bb[di][:, :], kkf[:, :], af[:, :], ALU.mult, reads=[kkf, af], writes=[bb[di]])
                    PP.tt("pool", t_[:, :], kd[0][:, :], kd[1][:, :], ALU.add, reads=[kd[0], kd[1]], writes=[t_])
                    PP.stt("dve", prod[:, :], t_[:, :], cols[:, hp, 7:8], rc_ap, ALU.mult, ALU.mult,
                          reads=[t_, cols, (st, "r")], writes=[prod])
                    ps = PP.ps()
                    for bq in range(4):
                        PP.mm(ps, ps[:, bq * 2:bq * 2 + 2], prod[:, bq * 128:(bq + 1) * 128], blk2[:, 0:2], True, True,
                             reads=[prod, blk2])
                    PP.copy("act", bon[:, n * 4:(n + 1) * 4, :], ps[:, 0:8].rearrange("p (b h) -> p b h", h=2),
                           reads=[ps], writes=[(bon, n)])

                def phase_b(n, di, bs, PP):
                    opkr, Kh, Bh, KhC, nBhC = BSET[bs]
                    A1, A2, A3, A4 = A
                    PP.op("dve", lambda e, o=A1[:, :], m_=smask[:, :], d_=ld[di][:, :]:
                         e.tensor_tensor_scan(o, m_, d_, 0.0, ALU.mult, ALU.add),
                         reads=[smask, ld[di]], writes=[A1])
                    for cc in range(4):
                        c = n * 4 + cc
                        sl = slice(cc * 128, (cc + 1) * 128)
                        tot = A1[:, cc * 128 + 127:cc * 128 + 128]
                        PP.ts("dve", A2[:, sl], A1[:, sl], tot, -1.0, ALU.subtract, ALU.mult, reads=[A1],
                             writes=[(A2, cc)])
                        PP.act(pC[di][:, c:c + 1], tot, AF.Exp, reads=[A1], writes=[(pC[di], c)])
                    PP.tt("pool", A3[:, :], A1[:, :], ld[di][:, :], ALU.subtract, reads=[A1, ld[di]], writes=[A3])
                    if di == 1:
                        PP.tt("pool", A4[:, :], A2[:, :], ld[di][:, :], ALU.add, reads=[A2, ld[di]], writes=[A4])
                        c1, c1x, c2, ib = A4, A2, A3, A1
                    else:
                        c1, c1x, c2, ib = A1, A3, A2, A4
                    PP.act(ib[:, :], c1[:, :], AF.Exp, reads=[c1], writes=[ib], scale=-1.0)
                    PP.act(c1[:, :], c1[:, :], AF.Exp, reads=[c1], writes=[c1])
                    PP.act(c1x[:, :], c1x[:, :], AF.Exp, reads=[c1x], writes=[c1x])
                    PP.act(c2[:, :], c2[:, :], AF.Exp, reads=[c2], writes=[c2])
                    PP.tt("pool", opkr[:, :, 0:128], v4(kkb[:, :]), v4(c1x[:, :]), ALU.mult, reads=[kkb, c1x],
                         writes=[(opkr, 0)])
                    PP.tt("pool", opkr[:, :, 128:256], v4(rb[:, :]), v4(c1[:, :]), ALU.mult, reads=[rb, c1],
                         writes=[(opkr, 1)])
                    PP.tt("pool", Kh[:, :], kd[di][:, :], ib[:, :], ALU.mult, reads=[kd[di], ib], writes=[Kh])
                    PP.tt("pool", Bh[:, :], bb[di][:, :], ib[:, :], ALU.mult, reads=[bb[di], ib], writes=[Bh])
                    PP.tt("pool", KhC[:, :], kd[di][:, :], c2[:, :], ALU.mult, reads=[kd[di], c2], writes=[KhC])
                    PP.stt("dve", nBhC[:, :], bb[di][:, :], -1.0, c2[:, :], ALU.mult, ALU.mult,
                          reads=[bb[di], c2], writes=[nBhC])
                def stage1(n, di, bs, drip):
                    opkr, Kh, Bh, KhC, nBhC = BSET[bs]
                    NX = NMS[1 - di]
                    qs = [(cc, hh) for cc in range(4) for hh in (0, 1)]
                    for qi, (cc, hh) in enumerate(qs):
                        pr = slice(hh * 64, hh * 64 + 64)
                        tc = slice(cc * 128, (cc + 1) * 128)
                        psA = P.ps()
                        psB = P.ps()
                        P.mm(psA, psA[:, 0:256], Kh[pr, tc], opkr[pr, cc, :], True, True, reads=[Kh, opkr])
                        P.mm(psA, psA[:, 256:384], opkr[pr, cc, 0:128], Bh[pr, tc], True, True,
                             reads=[opkr, Bh])
                        P.mm(psB, psB[:, 0:256], Bh[pr, tc], opkr[pr, cc, :], True, True, reads=[Bh, opkr])
                        P.tt("dve", AR[qi][:, :], psA[:, 0:256], MS[di][:, :], ALU.mult,
                             reads=[psA, MS[di]], writes=[AR[qi]])
                        P.tt("dve", MZ[qi][0][:, 0:128], psA[:, 256:384], NX[:, 0:128], ALU.mult,
                             reads=[psA, NX], writes=[(MZ[qi][0], 0)])
                        P.tt("dve", BR[qi][:, :], psB[:, 0:256], NMS[di][:, :], ALU.mult,
                             reads=[psB, NMS[di]], writes=[BR[qi]])
                        P.copy("pool", MZ[qi][0][:, 256:384], BR[qi][:, 0:128], reads=[BR[qi]],
                               writes=[(MZ[qi][0], 2)])
                        drip()
                    for qi, (cc, hh) in enumerate(qs):
                        pb = hh * 64
                        pr = slice(pb, pb + 64)
                        tc = slice(cc * 128, (cc + 1) * 128)
                        c = n * 4 + cc
                        vcol = (2 * hp + hh) * 64
                        ps = P.ps()
                        P.mm(ps, ps[:, pb:pb + 64], opkr[pr, cc, 0:128], identb[pr, pb:pb + 64], True, True,
                             reads=[opkr, identb])
                        P.mm(ps, ps[:, 64 - pb:128 - pb], AR[qi][:, 0:128], vtm[:, c, vcol:vcol + 64], True,
                             True, reads=[AR[qi], vtm])
                        P.mm(ps, ps[:, 128:192], KhC[pr, tc], identb[pr, pb:pb + 64], True, True,
                             reads=[KhC, identb])
                        P.mm(ps, ps[:, 192:256], nBhC[pr, tc], identb[pr, pb:pb + 64], True, True,
                             reads=[nBhC, identb])
                        P.copy("act", MZ[qi][0][:, 128:256], ps[:, 0:128], reads=[ps], writes=[(MZ[qi][0], 1)])
                        P.copy("act", KBC[qi][:, :, pb:pb + 64],
                               ps[:, 128:256].rearrange("p (j k) -> p j k", k=64), reads=[ps],
                               writes=[KBC[qi]])
                    for lev in range(7):
                        for qi in range(NQ):
                            cur = MZ[qi][lev % 2]
                            nxt = MZ[qi][(lev + 1) % 2]
                            ps = P.ps()
                            ev_eng = "act" if qi % 2 == 0 else "dve"
                            drip()
                            if lev < 5:
                                P.mm(ps, ps[:, 0:256], cur[:, 256:384], cur[:, 0:256], True, False, reads=[cur])
                                P.mm(ps, ps[:, 128:256], identb[:, :], cur[:, 128:256], False, True,
                                     reads=[cur, identb])
                                P.mm(ps, ps[:, 256:384], cur[:, 0:128], cur[:, 256:384], True, True, reads=[cur])
                                P.copy(ev_eng, nxt[:, :], ps[:, 0:384], reads=[ps], writes=[nxt])
                            elif lev == 5:
                                P.mm(ps, ps[:, 128:256], cur[:, 256:384], cur[:, 128:256], True, False,
                                     reads=[cur])
                                P.mm(ps, ps[:, 128:256], identb[:, :], cur[:, 128:256], False, True,
                                     reads=[cur, identb])
                                P.mm(ps, ps[:, 256:384], cur[:, 0:128], cur[:, 256:384], True, True, reads=[cur])
                                P.copy(ev_eng, nxt[:, 128:384], ps[:, 128:384], reads=[ps], writes=[nxt])
                            else:
                                P.mm(ps, ps[:, 128:256], cur[:, 256:384], cur[:, 128:256], True, False,
                                     reads=[cur])
                                P.mm(ps, ps[:, 128:256], identb[:, :], cur[:, 128:256], False, True,
                                     reads=[cur, identb])
                                P.copy(ev_eng, nxt[:, 128:256], ps[:, 128:256], reads=[ps], writes=[(nxt, 1)])
                    for qi, (cc, hh) in enumerate(qs):
                        pb = hh * 64
                        pr = slice(pb, pb + 64)
                        c = n * 4 + cc
                        vcol = (2 * hp + hh) * 64
                        ucol = 64 - pb
                        zt = MZ[qi][1]
                        zf = zt[:, 128:256]
                        zu = zt[:, 128 + ucol:128 + ucol + 64]
                        ps = P.ps()
                        P.mm(ps, ps[:, 0:128], zf, BR[qi][:, 128:256], True, True, reads=[zt, BR[qi]])
                        P.mm(ps, ps[:, 128:192], AR[qi][:, 128:256], vtm[:, c, vcol:vcol + 64], True, False,
                             reads=[AR[qi], vtm])
                        P.mm(ps, ps[:, 128:192], BR[qi][:, 128:256], zu, False, True, reads=[BR[qi], zt])
                        P.mm(ps, ps[:, 192:256], KBC[qi][:, 0, :], vtm[:, c, vcol:vcol + 64], True, False,
                             reads=[KBC[qi], vtm])
                        P.mm(ps, ps[:, 192:256], KBC[qi][:, 1, :], zu, False, True, reads=[KBC[qi], zt])
                        P.mm(ps, ps[:, 256:320], zf, KBC[qi][:, 1, pb:pb + 64], True, True,
                             reads=[zt, KBC[qi]])
                        if di == 0:
                            P.copy("act", yacc[:, c, pb:pb + 64], ps[:, 128:192], reads=[ps],
                                   writes=[(yacc, (c, hh))])
                        P.copy("act", Gs[di][pr, c, :], ps[pr, 192:256], reads=[ps],
                               writes=[(Gs[di], (c, hh))])
                        P.tt("dve", QpT[di][pr, c, :], ps[pr, 0:128], opkr[pr, cc, 128:256], ALU.add,
                             reads=[ps, opkr], writes=[(QpT[di], (c, hh))])
                        if di == 1:
                            P.tt("dve", yacc[:, c, pb:pb + 64], ps[:, 128:192], yacc[:, c, pb:pb + 64],
                                 ALU.add, reads=[ps, (yacc, (c, hh))], writes=[(yacc, (c, hh))])
                        P.copy("dve", MpT[di][pr, c, pb:pb + 64], ps[pr, 256:320], reads=[ps],
                               writes=[(MpT[di], (c, hh))])

                from collections import deque
                pend = deque()

                dcnt_ = [0]

                def drip():
                    dcnt_[0] += 1
                    if dcnt_[0] % 3 == 0:
                        return
                    if pend:
                        nm_, a_, k_ = pend.popleft()
                        getattr(P, nm_)(*a_, **k_)

                def flush():
                    while pend:
                        drip()

                class _Def:
                    def ps(self_):
                        return psA
                    def __getattr__(self_, nm_):
                        return lambda *a_, **k_: pend.append((nm_, a_, k_))

                DD = _Def()
                phase_a(0, P)
                phase_b(0, 0, 0, P)
                for n in range(NCH):
                    phase_b(n, 1, 1, DD)
                    if dbg != "B":
                        stage1(n, 0, 0, drip)
                    flush()
                    if n + 1 < NCH:
                        phase_a(n + 1, DD)
                        phase_b(n + 1, 0, 0, DD)
                    if dbg != "B":
                        stage1(n, 1, 1, drip)
                    flush()
                P.barrier()
            if dbg in ("A", "B", "s1", "s1a", "s1b", "s1c", "s1h0"):
                continue
            with contextlib.ExitStack() as s2:
                Hf = [P.sb(s2, "Hf", [128, 64], F32) for _ in range(2)]
                Hb = [P.sb(s2, "Hb", [128, 128], BF16) for _ in range(2)]
                for di in range(2):
                    P.memset("pool", Hf[di], Hf[di][:, :], 0.0)
                    P.memset("pool", Hb[di], Hb[di][:, :], 0.0)
                for i in range(NB):
                    for di in range(2):
                        c = i if di == 0 else NB - 1 - i
                        psY = P.ps()
                        P.mm(psY, psY[:, 0:128], QpT[di][:, c, :], Hb[di][:, :], True, True, reads=[QpT[di], Hb[di]])
                        P.tt("dve", yacc[:, c, :], psY[:, 0:128], yacc[:, c, :], ALU.add, reads=[psY, yacc],
                             writes=[yacc])
                        if i == NB - 1:
                            continue
                        psH = P.ps()
                        P.mm(psH, psH[:, 0:128], MpT[di][:, c, :], Hb[di][:, :], True, True, reads=[MpT[di], Hb[di]])
                        for hh in range(2):
                            pb = hh * 64
                            pr = slice(pb, pb + 64)
                            P.stt("dve", Hf[di][pr, :], Hf[di][pr, :], pC[di][pr, c:c + 1], psH[pr, pb:pb + 64],
                                  ALU.mult, ALU.add, reads=[Hf[di], pC[di], psH], writes=[Hf[di]])
                            P.tt("pool", Hf[di][pr, :], Hf[di][pr, :], Gs[di][pr, c, :], ALU.add,
                                 reads=[Hf[di], Gs[di]], writes=[Hf[di]])
                            P.copy("act", Hb[di][pr, pb:pb + 64], Hf[di][pr, :], reads=[Hf[di]], writes=[Hb[di]])
                P.barrier()
            if dbg == "s2":
                continue
            with contextlib.ExitStack() as s3:
                tq = P.sb(s3, "tq", [128, NB, 128], F32)
                finb = P.sb(s3, "finb", [128, NB, 128], BF16)
                mu = P.sb(s3, "mu", [128, NB, 2], F32)
                var = P.sb(s3, "var", [128, NB, 2], F32)
                sgd = P.sb(s3, "sgd", [128, T], BF16)
                gl = P.sb(s3, "gl", [128, CH], F32)
                for n in range(NCH):
                    cs = slice(n * CH, (n + 1) * CH)
                    P.dma("sp", gl[:, :], U[O_GD:O_GD + 128, cs], reads=[], writes=[gl])
                    P.act(sgd[:, cs], gl[:, :], AF.Sigmoid, reads=[gl], writes=[(sgd, n)])
                if dbg == "p1":
                    P.barrier()
                    continue
                g3 = lambda t: t[:, :, :].rearrange("p b (h v) -> p b h v", v=64)
                y3 = g3(yacc)
                q3 = g3(tq)
                bc = lambda t: t[:, :, :].unsqueeze(3).to_broadcast([128, NB, 2, 64])
                P.op("dve", lambda e, o=mu[:, :, :], i=y3: e.tensor_reduce(o, i, AX.X, ALU.add), reads=[yacc], writes=[mu])
                P.ts("dve", mu[:, :, :], mu[:, :, :], 1.0 / 64.0, None, ALU.mult, None, reads=[mu], writes=[mu])
                P.tt("dve", y3, y3, bc(mu), ALU.subtract, reads=[yacc, mu], writes=[yacc])
                P.act(tq[:, :, :], yacc[:, :, :], AF.Square, reads=[yacc], writes=[tq])
                P.op("dve", lambda e, o=var[:, :, :], i=q3: e.tensor_reduce(o, i, AX.X, ALU.add), reads=[tq], writes=[var])
                P.act(var[:, :, :], var[:, :, :], AF.Sqrt, reads=[var, gneps], writes=[var], bias=gneps[:, 0:1],
                      scale=1.0 / 64.0)
                P.op("dve", lambda e, o=var[:, :, :]: e.reciprocal(o, o), reads=[var], writes=[var])
                P.tt("dve", y3, y3, bc(var), ALU.mult, reads=[yacc, var], writes=[yacc])
                lgb = Lg[:, hp * 128:(hp + 1) * 128].unsqueeze(1).to_broadcast([128, NB, 128])
                lbb = Lb[:, hp * 128:(hp + 1) * 128].unsqueeze(1).to_broadcast([128, NB, 128])
                P.tt("dve", yacc[:, :, :], yacc[:, :, :], lgb, ALU.mult, reads=[yacc, Lg], writes=[yacc])
                P.tt("dve", yacc[:, :, :], yacc[:, :, :], lbb, ALU.add, reads=[yacc, Lb], writes=[yacc])
                vv = vtm[:, :, hp * 128:(hp + 1) * 128].rearrange("p b (h v) -> p b h v", v=64)
                bonb = bon[:, :, :].unsqueeze(3).to_broadcast([128, NB, 2, 64])
                P.tt("dve", q3, vv, bonb, ALU.mult, reads=[vtm, bon], writes=[tq])
                P.tt("dve", yacc[:, :, :], yacc[:, :, :], tq[:, :, :], ALU.add, reads=[yacc, tq], writes=[yacc])
                if dbg == "p2":
                    P.barrier()
                    continue
                P.barrier()
                for b4 in range(NB // 4):
                    ps = P.ps()
                    for bq in range(4):
                        b = b4 * 4 + bq
                        P.mm(ps, ps[:, bq * 128:(bq + 1) * 128], sgd[:, b * 128:(b + 1) * 128],
                             g2b[:, hp * 128:(hp + 1) * 128], True, True, reads=[sgd, g2b])
                    P.tt("dve", finb[:, b4 * 4:(b4 + 1) * 4, :], yacc[:, b4 * 4:(b4 + 1) * 4, :],
                         ps[:, :].rearrange("p (b c) -> p b c", c=128), ALU.mult, reads=[yacc, ps],
                         writes=[(finb, b4)])
                    ps2 = P.ps()
                    for bq in range(4):
                        b = b4 * 4 + bq
                        P.mm(ps2, ps2[:, bq * 128:(bq + 1) * 128], finb[:, b, :], identb[:, :], True, True,
                             reads=[(finb, b4), identb])
                    P.copy("act", yT[:, hp, b4 * 512:(b4 + 1) * 512], ps2[:, :], reads=[ps2], writes=[(yT, (hp, b4))])
                P.barrier()


def outproj_part(P, C, xT, yT, w_out, m):
    with contextlib.ExitStack() as scope:
        wo = P.sb(scope, "wo", [128, 2, D], BF16)
        w3 = w_out.rearrange("(k p) c -> p k c", p=128)
        load_weight(P, C, w3[:, 2 * m:2 * m + 2, :], 2, D, wo[:, :, :], wo)
        for j in range(8):
            for n in range(NCH):
                cs = slice(n * CH, (n + 1) * CH)
                ps = P.ps()
                for k in range(2):
                    P.mm(ps, ps[:, :], wo[:, k, j * 128:(j + 1) * 128], yT[:, k, cs], k == 0, k == 1,
                         reads=[wo, yT])
                P.tt("dve", xT[:, j, cs], ps[:, :], xT[:, j, cs], ALU.add, reads=[ps, (xT, (j, n))],
                     writes=[(xT, (j, n))])
        P.barrier()


def outproj_load(P, C, scope, w_out, ms):
    wo = P.sb(scope, "wo", [128, 2 * len(ms), D], BF16)
    w3 = w_out.rearrange("(k p) c -> p k c", p=128)
    for slot, m in enumerate(ms):
        load_weight(P, C, w3[:, 2 * m:2 * m + 2, :], 2, D, wo[:, 2 * slot:2 * slot + 2, :], wo, dst_key=slot)
    return wo


def outproj_multi(P, C, xT, yTn, wo, ms):
    nk = 2 * len(ms)
    if True:
        for n in range(NCH):
            cs = slice(n * CH, (n + 1) * CH)
            for j in range(8):
                ps = P.ps()
                for k in range(nk):
                    P.mm(ps, ps[:, :], wo[:, k, j * 128:(j + 1) * 128], yTn[:, k, cs], k == 0, k == nk - 1,
                         reads=[wo, yTn])
                P.tt("dve", xT[:, j, cs], ps[:, :], xT[:, j, cs], ALU.add, reads=[ps, (xT, (j, n))],
                     writes=[(xT, (j, n))])
        P.barrier()


def phase_mix(P, C, xT, l, dr, U, VT, debug_y=False, mixers=("conv", "rwkv", "ret", "mla")):
    g = TT(C.gains["mix_norm"].h[:, l, :], "gm")
    g.whole = C.gains["mix_norm"].whole
    phase_inproj(P, C, xT, g, dr["w_in"][l], dr["w_rot"][l], U, VT)
    if debug_y:
        for k in range(8):
            P.memset("pool", xT, xT[:, k, :], 0.0)
        P.barrier()
    MIDX = {"conv": 0, "rwkv": 1, "ret": 2, "mla": 3}
    if "rwkv" in mixers:
        with contextlib.ExitStack() as scope:
            yT = P.sb(scope, "yT", [128, 2, T], BF16)
            mixer_rwkv(P, C, l, dr, U, VT, yT)
            if debug_y:
                for k in range(2):
                    P.copy("dve", xT[:, 2 + k, :], yT[:, k, :], reads=[yT], writes=[xT])
                P.barrier()
            else:
                outproj_part(P, C, xT, yT, dr["w_out"][l], 1)
    rest = [nm for nm in ("conv", "ret", "mla") if nm in mixers]
    if rest:
        with contextlib.ExitStack() as scope:
            yTn = P.sb(scope, "yTn", [128, 2 * len(rest), T], BF16)
            wo_all = None if debug_y else outproj_load(P, C, scope, dr["w_out"][l], [MIDX[nm] for nm in rest])
            for slot, name in enumerate(rest):
                view = TT(yTn.h[:, 2 * slot:2 * slot + 2, :], "yv_" + name)
                if name == "conv":
                    mixer_conv(P, C, l, dr, U, view)
                elif name == "ret":
                    mixer_ret(P, C, l, dr, U, VT, view)
                else:
                    mixer_mla(P, C, l, dr, U, view)
            if debug_y:
                for slot, name in enumerate(rest):
                    for k in range(2):
                        P.copy("dve", xT[:, 2 * MIDX[name] + k, :], yTn[:, 2 * slot + k, :], reads=[yTn], writes=[xT])
                P.barrier()
            else:
                outproj_multi(P, C, xT, yTn, wo_all, [MIDX[nm] for nm in rest])


W_NAMES = ["ffn1_norm", "ffn1_w_gate", "ffn1_w_up", "ffn1_w_down", "mix_norm", "w_in", "w_out", "conv_w",
           "rwkv_w0_f", "rwkv_w0_b", "rwkv_w2_f", "rwkv_w2_b", "rwkv_a0_f", "rwkv_a0_b", "rwkv_a2_f", "rwkv_a2_b",
           "rwkv_g2", "rwkv_k_k", "rwkv_k_a", "rwkv_r_k", "rwkv_lnx_g", "rwkv_lnx_b", "ret_gn_g", "mla_q_a_norm",
           "mla_q_b", "mla_kv_a_norm", "mla_kv_b", "ffn2_norm", "ffn2_w_gate", "ffn2_w_up", "ffn2_w_down",
           "final_norm"]


def build(shapes, stop_after=None, layers=(0, 1), final_norm=True):
    nc = bass.Bass("TRN2", target_bir_lowering=False)
    dr = {}
    for name, (shape, dt) in shapes.items():
        dr[name] = nc.dram_tensor(name, list(shape), dt, kind="ExternalInput").ap()
    out_d = nc.dram_tensor("out", [T, D], F32, kind="ExternalOutput").ap()
    U = nc.dram_tensor("u_scr", [UROWS, T], F32, kind="Internal").ap()
    VT = nc.dram_tensor("vt_scr", [T, 512], F32, kind="Internal").ap()
    ROPE = nc.dram_tensor("rope_scr", [4, 128, T], F32, kind="Internal").ap()
    with contextlib.ExitStack() as es:
        P = Prog(nc, es)
        C = Ctx()
        P.acc = []
        for i in range(8):
            h = es.enter_context(nc.psum_tensor("ps%d" % i, [128, 512], F32))
            t_ps = TT(h, "ps%d" % i)
            t_ps.psum = True
            (P.psums if i < 6 else P.acc).append(t_ps)
        xT = P.sb(es, "xT", [128, 8, T], F32)
        C.stage = [P.sb(es, "stage", [128, 2048], F32) for _ in range(2)]
        C.stage_i = 0
        C.ident = P.sb(es, "ident", [128, 128], F32)
        C.ones_f = P.sb(es, "ones_f", [128, 128], F32)
        C.ones_mean = P.sb(es, "ones_mean", [128, 128], BF16)
        C.eps_col = P.sb(es, "eps_col", [128, 1], F32)
        P.memset("pool", C.eps_col, C.eps_col[:, :], EPS)
        P.memset("pool", C.ones_f, C.ones_f[:, :], 1.0)
        P.memset("pool", C.ones_mean, C.ones_mean[:, :], 1.0 / D)
        P.op("pool", lambda e: e.affine_select(C.ident[:, :], C.ones_f[:, :], [[1, 128]], ALU.is_equal, 0.0,
                                               base=0, channel_multiplier=-1),
             reads=[C.ones_f], writes=[C.ident])
        C.ones_bf = P.sb(es, "ones_bf", [128, 128], BF16)
        C.ones64 = P.sb(es, "ones64", [128, 128], BF16)
        P.memset("pool", C.ones_bf, C.ones_bf[:, :], 1.0)
        P.memset("pool", C.ones64, C.ones64[:, :], 1.0 / 64.0)
        C.consts = P.sb(es, "consts", [128, 4], F32)
        with contextlib.ExitStack() as tmp:
            row = P.sb(tmp, "crow", [1, 4, 128], F32)
            one = P.sb(tmp, "cone", [1, 1], F32)
            P.memset("pool", one, one[:, :], 1.0)
            for i in range(32):
                P.memset("pool", row, row[0:1, 0, :].rearrange("o (r i) -> o r i", i=32)[:, :, i:i + 1],
                         10000.0 ** (-i / 32.0))
            for i in range(16):
                P.memset("pool", row, row[0:1, 2, :].rearrange("o (r i) -> o r i", i=16)[:, :, i:i + 1],
                         10000.0 ** (-i / 16.0))
            P.memset("pool", row, row[0:1, 1, :].rearrange("o (r i) -> o r i", i=64)[:, :, 0:32], -1.0)
            P.memset("pool", row, row[0:1, 1, :].rearrange("o (r i) -> o r i", i=64)[:, :, 32:64], 1.0)
            P.memset("pool", row, row[0:1, 3, :].rearrange("o (r i) -> o r i", i=32)[:, :, 0:16], -1.0)
            P.memset("pool", row, row[0:1, 3, :].rearrange("o (r i) -> o r i", i=32)[:, :, 16:32], 1.0)
            ps = P.ps()
            for c in range(4):
                P.mm(ps, ps[:, c:c + 1], row[0:1, c, :], one[0:1, 0:1], True, True, reads=[row, one])
            P.copy("dve", C.consts[:, :], ps[:, 0:4], reads=[ps], writes=[C.consts])
            P.barrier()
        C.ROPE = ROPE
        for idx, (ci, si_) in enumerate(((0, 1), (2, 3))):
            with contextlib.ExitStack() as tmp:
                cos_, sin_ = rope_tables(P, C, tmp, dr["positions"], C.consts[:, ci:ci + 1], C.consts[:, si_:si_ + 1])
                P.dma("sp", ROPE[2 * idx], cos_[:, :], reads=[cos_], writes=[])
                P.dma("sp", ROPE[2 * idx + 1], sin_[:, :], reads=[sin_], writes=[])
                P.barrier()
        C.gains = {}
        for nm in ("ffn1_norm", "mix_norm", "ffn2_norm"):
            g = P.sb(es, "g_" + nm, [128, DEPTH, 8], F32)
            for l in range(DEPTH):
                P.dma("sp", g[:, l, :], dr[nm][l].rearrange("(k p) -> p k", p=128), reads=[], writes=[g], slow=True)
            C.gains[nm] = g
        gfin = P.sb(es, "g_final", [128, 8], F32)
        P.dma("sp", gfin[:, :], dr["final_norm"].rearrange("(k p) -> p k", p=128), reads=[], writes=[gfin], slow=True)

        load_x(P, C, xT, dr["x"])
        done = False
        for l in layers:
            g1 = TT(C.gains["ffn1_norm"].h[:, l, :], "g1")
            g1.whole = C.gains["ffn1_norm"].whole
            phase_ffn(P, C, xT, g1, dr["ffn1_w_gate"][l], dr["ffn1_w_up"][l], dr["ffn1_w_down"][l])
            if stop_after == ("ffn1", l):
                done = True
                break
            if stop_after is not None and stop_after[0].startswith("y") and stop_after[1] == l:
                phase_mix(P, C, xT, l, dr, U, VT, debug_y=True, mixers=stop_after[0].split("_")[1:])
                done = True
                break
            if stop_after is not None and stop_after[0].startswith("m_") and stop_after[1] == l:
                phase_mix(P, C, xT, l, dr, U, VT, mixers=stop_after[0].split("_")[1:])
                done = True
                break
            phase_mix(P, C, xT, l, dr, U, VT)
            if stop_after == ("mix", l):
                done = True
                break
            g2 = TT(C.gains["ffn2_norm"].h[:, l, :], "g2")
            g2.whole = C.gains["ffn2_norm"].whole
            phase_ffn(P, C, xT, g2, dr["ffn2_w_gate"][l], dr["ffn2_w_up"][l], dr["ffn2_w_down"][l])
            if stop_after == ("ffn2", l):
                done = True
                break
        store_out(P, C, xT, gfin, out_d, do_norm=(final_norm and not done))
        P.emit()
    return nc


EXTRA = ["w_rot", "q_b_rot"]


def _host_layouts(inputs):
    w_in = inputs["w_in"]
    def rot_cols(w, c0, nheads, hd):
        half = hd // 2
        cols = []
        for h in range(nheads):
            base = c0 + h * hd
            cols += list(range(base + half, base + hd)) + list(range(base, base + half))
        return w[..., cols]
    w_rot = np.concatenate([rot_cols(w_in, O_TQ, 4, 64), rot_cols(w_in, O_TK, 4, 64), rot_cols(w_in, O_MKR, 1, 32)],
                           axis=-1)
    qb = inputs["mla_q_b"]
    cols = []
    for h in range(4):
        base = h * 96
        cols += list(range(base, base + 64)) + list(range(base + 80, base + 96)) + list(range(base + 64, base + 80))
    q_b_rot = qb[..., cols]
    return {"w_rot": np.ascontiguousarray(w_rot), "q_b_rot": np.ascontiguousarray(q_b_rot)}


def _shapes(inputs, extra):
    shapes = {"x": ((T, D), F32), "positions": ((1, T), I32)}
    for n in W_NAMES:
        shapes[n] = (inputs[n].shape, F32)
    for n in EXTRA:
        shapes[n] = (extra[n].shape, F32)
    return shapes


N_LAUNCH = 1


def kernel(_stop_after=None, **inputs):
    ncores = 8 if DEBUG_CORES is None else DEBUG_CORES
    extra = _host_layouts(inputs)
    shapes = _shapes(inputs, extra)
    if _stop_after is not None or N_LAUNCH == 1:
        plans = [((0, 1) if DEBUG_LAYERS is None else DEBUG_LAYERS, True)]
    else:
        plans = [((0,), False), ((1,), True)]
    xs = [np.ascontiguousarray(inputs["x"][c]) for c in range(ncores)]
    for layers, fin in plans:
        nc = build(shapes, stop_after=_stop_after, layers=layers, final_norm=fin)
        in_maps = []
        for c in range(ncores):
            m = {"x": xs[c],
                 "positions": np.ascontiguousarray(inputs["positions"][c:c + 1]).astype(np.int32)}
            for n in W_NAMES:
                m[n] = np.ascontiguousarray(inputs[n])
            for n in EXTRA:
                m[n] = extra[n]
            in_maps.append(m)
        if DEBUG_TRACE:
            res = run_bass_kernel_spmd(nc, in_maps, core_ids=list(range(ncores)), trace=True)
            print("EXEC_NS", res.exec_time_ns)
        else:
            res = run_bass_kernel_spmd(nc, in_maps, core_ids=list(range(ncores)))
        xs = [np.ascontiguousarray(np.asarray(r["out"])) for r in res.results]
    out = np.stack(xs + [np.zeros((T, D), np.float32)] * (8 - ncores), axis=0)
    return out.astype(np.float32)
```
